# Optimizing a Trainium2 kernel written in Bass

```python
import math
import jax, jax.numpy as jnp
from jax import lax
import numpy as np

D_MODEL = 2048
BATCH = 8
SEQ = 4096
DEPTH = 4

F32 = jnp.float32
LN_EPS = 1e-5
NORM_EPS = 1e-6

N_BRANCH = 3
BRANCH_WIDTH = D_MODEL // 2

M_HEADS = 4
M_DV = BRANCH_WIDTH // M_HEADS
M_DQK = M_DV // 2
M_CHUNK = 64

S_DINNER = BRANCH_WIDTH
S_HEADDIM = 64
S_HEADS = S_DINNER // S_HEADDIM
S_GROUPS = 2
S_HPG = S_HEADS // S_GROUPS
S_DSTATE = 128
S_CONV = 4
S_CHUNK = 128
S_CONV_CH = S_DINNER + 2 * S_GROUPS * S_DSTATE

A_HEADDIM = 64
A_QHEADS = BRANCH_WIDTH // A_HEADDIM
A_KVHEADS = 4
A_REP = A_QHEADS // A_KVHEADS
A_WINDOW = 128
A_BLOCK = 128
ROPE_THETA = 10000.0

P_HEADS = 8
P_NKEYS = 128
P_EXPERTS = P_NKEYS * P_NKEYS
P_DKEY = 256
P_TOPK = 16
P_CHUNK = 128

M_COLS = 2 * M_HEADS * M_DQK + 2 * M_HEADS * M_DV + 2 * M_HEADS
S_COLS = S_DINNER + S_CONV_CH + S_HEADS
A_COLS = A_QHEADS * A_HEADDIM + 2 * A_KVHEADS * A_HEADDIM
G_COLS = N_BRANCH * D_MODEL
IN_COLS = M_COLS + S_COLS + A_COLS + G_COLS

kernel_name = "hybrid_mlstm_ssd_swa_peer_deepnorm"


def _in_split_points():
    sizes = [M_HEADS * M_DQK, M_HEADS * M_DQK, M_HEADS * M_DV, M_HEADS * M_DV, M_HEADS, M_HEADS,
             S_DINNER, S_CONV_CH, S_HEADS,
             A_QHEADS * A_HEADDIM, A_KVHEADS * A_HEADDIM, A_KVHEADS * A_HEADDIM,
             G_COLS]
    return [int(v) for v in np.cumsum(sizes)[:-1]]


def layer_norm(x, g, b):
    xf = x.astype(F32)
    mu = jnp.mean(xf, -1, keepdims=True)
    var = jnp.mean(jnp.square(xf - mu), -1, keepdims=True)
    return ((xf - mu) * lax.rsqrt(var + LN_EPS) * g.astype(F32) + b.astype(F32)).astype(x.dtype)


def mlstm_mixer(q, k, v, o_pre, i_pre, f_pre, norm_w):
    B_, S_ = q.shape[:2]
    nc = S_ // M_CHUNK

    def to_chunks(t):
        t = t.reshape((B_, nc, M_CHUNK) + t.shape[2:])
        return jnp.moveaxis(jnp.moveaxis(t, 3, 2), 1, 0)

    qc = to_chunks(q.astype(F32))
    kc = to_chunks(k.astype(F32) * (M_DQK ** -0.5))
    vc = to_chunks(v.astype(F32))
    li = to_chunks(i_pre.astype(F32))
    lf = to_chunks(jax.nn.log_sigmoid(f_pre.astype(F32)))
    causal = jnp.tril(jnp.ones((M_CHUNK, M_CHUNK), dtype=bool))

    def step(carry, inp):
        C, n, m = carry
        qb, kb, vb, lib, lfb = inp
        b = jnp.cumsum(lfb, axis=-1)
        g = b[..., -1]
        log_d = jnp.where(causal, b[..., :, None] - b[..., None, :] + lib[..., None, :], -jnp.inf)
        log_inter = b + m[..., None]
        m_t = jnp.maximum(jnp.max(log_d, -1), log_inter)
        w_intra = jnp.exp(log_d - m_t[..., None])
        w_inter = jnp.exp(log_inter - m_t)
        s = jnp.einsum('bhtd,bhsd->bhts', qb, kb) * w_intra
        num = jnp.einsum('bhts,bhsv->bhtv', s, vb) + w_inter[..., None] * jnp.einsum('bhtd,bhdv->bhtv', qb, C)
        den = jnp.sum(s, -1) + w_inter * jnp.einsum('bhtd,bhd->bht', qb, n)
        h = num / jnp.maximum(jnp.abs(den), jnp.exp(-m_t))[..., None]
        log_ws = g[..., None] - b + lib
        m_new = jnp.maximum(g + m, jnp.max(log_ws, -1))
        a_prev = jnp.exp(g + m - m_new)
        ws = jnp.exp(log_ws - m_new[..., None])
        C_new = a_prev[..., None, None] * C + jnp.einsum('bhsd,bhsv->bhdv', kb * ws[..., None], vb)
        n_new = a_prev[..., None] * n + jnp.einsum('bhs,bhsd->bhd', ws, kb)
        return (C_new, n_new, m_new), h

    init = (jnp.zeros((B_, M_HEADS, M_DQK, M_DV), F32),
            jnp.zeros((B_, M_HEADS, M_DQK), F32),
            jnp.zeros((B_, M_HEADS), F32))
    _, hs = lax.scan(step, init, (qc, kc, vc, li, lf))
    h = jnp.swapaxes(jnp.moveaxis(hs, 0, 1), 2, 3).reshape(B_, S_, M_HEADS, M_DV)
    mu = jnp.mean(h, -1, keepdims=True)
    var = jnp.mean(jnp.square(h - mu), -1, keepdims=True)
    h = ((h - mu) * lax.rsqrt(var + NORM_EPS)).reshape(B_, S_, M_HEADS * M_DV) * norm_w.astype(F32)
    return (jax.nn.sigmoid(o_pre.astype(F32)) * h).astype(q.dtype)


def causal_depthwise_conv(u, w, b):
    out = lax.conv_general_dilated(u, w.astype(u.dtype), window_strides=(1,), padding=[(S_CONV - 1, 0)],
                                   dimension_numbers=('NWC', 'WIO', 'NWC'),
                                   feature_group_count=u.shape[-1])
    return out + b.astype(u.dtype)


def ssd_mixer(z, xbc, dt_raw, conv_w, conv_b, dt_bias, a_log, d_skip, norm_w):
    B_, S_ = z.shape[:2]
    xbc = jax.nn.silu(causal_depthwise_conv(xbc, conv_w, conv_b)).astype(F32)
    xs, bm, cm = jnp.split(xbc, [S_DINNER, S_DINNER + S_GROUPS * S_DSTATE], axis=-1)
    xs = xs.reshape(B_, S_, S_GROUPS, S_HPG, S_HEADDIM)
    dt = jax.nn.softplus(dt_raw.astype(F32) + dt_bias.astype(F32)).reshape(B_, S_, S_GROUPS, S_HPG)
    a = -jnp.exp(a_log.astype(F32)).reshape(S_GROUPS, S_HPG)
    nc = S_ // S_CHUNK
    xc = (xs * dt[..., None]).reshape(B_, nc, S_CHUNK, S_GROUPS, S_HPG, S_HEADDIM)
    bc = bm.reshape(B_, nc, S_CHUNK, S_GROUPS, S_DSTATE)
    cc = cm.reshape(B_, nc, S_CHUNK, S_GROUPS, S_DSTATE)
    acum = jnp.cumsum((dt * a).reshape(B_, nc, S_CHUNK, S_GROUPS, S_HPG), axis=2)
    causal = jnp.tril(jnp.ones((S_CHUNK, S_CHUNK), dtype=bool))[None, None, :, :, None, None]
    decay = jnp.exp(jnp.where(causal, acum[:, :, :, None] - acum[:, :, None, :], -jnp.inf))
    cb = jnp.einsum('bclgn,bcsgn->bclsg', cc, bc)
    y_diag = jnp.einsum('bclsgh,bcsghp->bclghp', cb[..., None] * decay, xc)
    decay_states = jnp.exp(acum[:, :, -1:] - acum)
    states = jnp.einsum('bclgn,bclghp->bcghpn', bc, decay_states[..., None] * xc)
    chunk_decay = jnp.exp(acum[:, :, -1])

    def step(hstate, inp):
        st, dec = inp
        return dec[..., None, None] * hstate + st, hstate

    h0 = jnp.zeros((B_, S_GROUPS, S_HPG, S_HEADDIM, S_DSTATE), F32)
    _, prev = lax.scan(step, h0, (jnp.moveaxis(states, 1, 0), jnp.moveaxis(chunk_decay, 1, 0)))
    prev = jnp.moveaxis(prev, 0, 1)
    y_off = jnp.einsum('bclgn,bcghpn->bclghp', cc, prev) * jnp.exp(acum)[..., None]
    y = (y_diag + y_off).reshape(B_, S_, S_GROUPS, S_HPG, S_HEADDIM) \
        + d_skip.astype(F32).reshape(S_GROUPS, S_HPG)[..., None] * xs
    y = y.reshape(B_, S_, S_DINNER) * jax.nn.silu(z.astype(F32))
    yg = y.reshape(B_, S_, S_GROUPS, S_DINNER // S_GROUPS)
    yg = yg * lax.rsqrt(jnp.mean(jnp.square(yg), -1, keepdims=True) + NORM_EPS)
    return (yg.reshape(B_, S_, S_DINNER) * norm_w.astype(F32)).astype(z.dtype)


def rope(x, pos):
    half = x.shape[-1] // 2
    freqs = ROPE_THETA ** (-jnp.arange(half, dtype=F32) / half)
    ang = pos.astype(F32)[:, None] * freqs[None, :]
    cos = jnp.cos(ang)[None, :, None, :]
    sin = jnp.sin(ang)[None, :, None, :]
    x1, x2 = x[..., :half], x[..., half:]
    return jnp.concatenate([x1 * cos - x2 * sin, x2 * cos + x1 * sin], axis=-1)


def swa_mixer(q, k, v, sinks):
    B_, S_ = q.shape[:2]
    pos = jnp.arange(S_)
    q = rope(q.astype(F32), pos)
    k = rope(k.astype(F32), pos)
    v = v.astype(F32)
    nb = S_ // A_BLOCK
    qb = q.reshape(B_, nb, A_BLOCK, A_KVHEADS, A_REP, A_HEADDIM)

    def with_prev(t):
        t = t.reshape(B_, nb, A_BLOCK, A_KVHEADS, A_HEADDIM)
        prev = jnp.pad(t[:, :-1], ((0, 0), (1, 0), (0, 0), (0, 0), (0, 0)))
        return jnp.concatenate([prev, t], axis=2)

    kk, vv = with_prev(k), with_prev(v)
    s = jnp.einsum('bnqgrd,bnkgd->bngrqk', qb, kk) * (A_HEADDIM ** -0.5)
    qi = jnp.arange(A_BLOCK)[:, None] + A_BLOCK
    ki = jnp.arange(2 * A_BLOCK)[None, :]
    rel = qi - ki
    band = (rel >= 0) & (rel < A_WINDOW)
    valid = band[None] & ((jnp.arange(nb)[:, None, None] > 0) | (ki >= A_BLOCK)[None])
    s = jnp.where(valid[None, :, None, None], s, -jnp.inf)
    sink = sinks.astype(F32).reshape(A_KVHEADS, A_REP)[None, None, :, :, None]
    m = jnp.maximum(jnp.max(s, -1), sink)
    p = jnp.exp(s - m[..., None])
    den = jnp.sum(p, -1) + jnp.exp(sink - m)
    o = jnp.einsum('bngrqk,bnkgd->bnqgrd', p, vv) / jnp.transpose(den, (0, 1, 4, 2, 3))[..., None]
    return o.reshape(B_, S_, A_QHEADS * A_HEADDIM)


def peer_ffn(x, wq, subkeys, u_tab, v_tab):
    B_, S_, D = x.shape
    T = B_ * S_
    xt = x.reshape(T, D)
    q = (xt @ wq).astype(F32).reshape(T, P_HEADS, 2, P_DKEY // 2)
    s = jnp.einsum('thcd,hckd->thck', q, subkeys.astype(F32))
    top_v, top_i = lax.top_k(s, P_TOPK)
    cand_v = top_v[:, :, 0, :, None] + top_v[:, :, 1, None, :]
    cand_i = top_i[:, :, 0, :, None] * P_NKEYS + top_i[:, :, 1, None, :]
    best_v, best_j = lax.top_k(cand_v.reshape(T, P_HEADS, P_TOPK * P_TOPK), P_TOPK)
    ids = jnp.take_along_axis(cand_i.reshape(T, P_HEADS, P_TOPK * P_TOPK), best_j, axis=-1)
    gates = jax.nn.softmax(best_v, axis=-1)
    nchunk = T // P_CHUNK

    def expert_block(args):
        xc, idc, gc = args
        u = jnp.take(u_tab, idc, axis=0)
        act = jax.nn.gelu(jnp.einsum('td,ted->te', xc, u).astype(F32), approximate=False)
        vsel = jnp.take(v_tab, idc, axis=0)
        return jnp.einsum('te,ted->td', (gc * act).astype(vsel.dtype), vsel)

    out = lax.map(expert_block, (xt.reshape(nchunk, P_CHUNK, D),
                                 ids.reshape(nchunk, P_CHUNK, P_HEADS * P_TOPK),
                                 gates.reshape(nchunk, P_CHUNK, P_HEADS * P_TOPK)))
    return out.reshape(B_, S_, D).astype(x.dtype)


def setup_inputs(seed: int = 0) -> dict:
    key = jax.random.key(seed)
    ks = jax.random.split(key, 26)
    L, D = DEPTH, D_MODEL
    beta = (8.0 * DEPTH) ** -0.25
    nrm = lambda k, shape, scale: jax.random.normal(k, shape, F32) * scale
    dt0 = jnp.exp(jax.random.uniform(ks[7], (L, S_HEADS), F32, math.log(1e-3), math.log(1e-1)))
    return {
        "x": nrm(ks[0], (BATCH, SEQ, D), 1.0),
        "w_in": nrm(ks[1], (L, D, IN_COLS), D ** -0.5),
        "mlstm_gate_b": jnp.stack([nrm(ks[2], (L, M_HEADS), 0.1),
                                   3.0 + 3.0 * jax.random.uniform(ks[3], (L, M_HEADS), F32)], axis=1),
        "mlstm_norm_w": 1.0 + nrm(ks[4], (L, M_HEADS * M_DV), 0.02),
        "ssm_conv_w": nrm(ks[5], (L, S_CONV, 1, S_CONV_CH), S_CONV ** -0.5),
        "ssm_conv_b": nrm(ks[6], (L, S_CONV_CH), 0.02),
        "ssm_dt_bias": dt0 + jnp.log(-jnp.expm1(-dt0)),
        "ssm_a_log": jnp.log(jax.random.uniform(ks[8], (L, S_HEADS), F32, 1.0, 16.0)),
        "ssm_d": 1.0 + nrm(ks[9], (L, S_HEADS), 0.02),
        "ssm_norm_w": 1.0 + nrm(ks[10], (L, S_DINNER), 0.02),
        "swa_sinks": nrm(ks[11], (L, A_QHEADS), 1.0),
        "merge_gate_b": nrm(ks[12], (L, N_BRANCH, D), 0.02),
        "w_branch": nrm(ks[13], (L, N_BRANCH, BRANCH_WIDTH, D), beta * BRANCH_WIDTH ** -0.5),
        "w_out": nrm(ks[14], (L, D, D), beta * D ** -0.5),
        "ln1_g": 1.0 + nrm(ks[15], (L, D), 0.02),
        "ln1_b": nrm(ks[16], (L, D), 0.02),
        "peer_wq": nrm(ks[17], (L, D, P_HEADS * P_DKEY), D ** -0.5),
        "peer_subkeys": nrm(ks[18], (L, P_HEADS, 2, P_NKEYS, P_DKEY // 2), (P_DKEY // 2) ** -0.5),
        "peer_u": nrm(ks[19], (L, P_EXPERTS, D), D ** -0.5),
        "peer_v": nrm(ks[20], (L, P_EXPERTS, D), beta * P_HEADS ** -0.5),
        "ln2_g": 1.0 + nrm(ks[21], (L, D), 0.02),
        "ln2_b": nrm(ks[22], (L, D), 0.02),
    }


def reference(x, w_in, mlstm_gate_b, mlstm_norm_w, ssm_conv_w, ssm_conv_b, ssm_dt_bias, ssm_a_log,
              ssm_d, ssm_norm_w, swa_sinks, merge_gate_b, w_branch, w_out, ln1_g, ln1_b,
              peer_wq, peer_subkeys, peer_u, peer_v, ln2_g, ln2_b):
    alpha = (2.0 * DEPTH) ** 0.25
    B_, S_, D = x.shape
    split_points = _in_split_points()
    for l in range(DEPTH):
        proj = x @ w_in[l]
        (mq, mk, mv, mo, mi, mf, sz, sxbc, sdt, aq, ak, av, gpre) = jnp.split(proj, split_points, axis=-1)
        y_m = mlstm_mixer(mq.reshape(B_, S_, M_HEADS, M_DQK), mk.reshape(B_, S_, M_HEADS, M_DQK),
                          mv.reshape(B_, S_, M_HEADS, M_DV), mo,
                          mi + mlstm_gate_b[l, 0], mf + mlstm_gate_b[l, 1], mlstm_norm_w[l])
        y_s = ssd_mixer(sz, sxbc, sdt, ssm_conv_w[l], ssm_conv_b[l], ssm_dt_bias[l], ssm_a_log[l],
                        ssm_d[l], ssm_norm_w[l])
        y_a = swa_mixer(aq.reshape(B_, S_, A_QHEADS, A_HEADDIM), ak.reshape(B_, S_, A_KVHEADS, A_HEADDIM),
                        av.reshape(B_, S_, A_KVHEADS, A_HEADDIM), swa_sinks[l]).astype(x.dtype)
        gates = jax.nn.sigmoid(gpre.reshape(B_, S_, N_BRANCH, D) + merge_gate_b[l])
        mix = (gates[:, :, 0] * (y_m @ w_branch[l, 0])
               + gates[:, :, 1] * (y_s @ w_branch[l, 1])
               + gates[:, :, 2] * (y_a @ w_branch[l, 2]))
        x = layer_norm(alpha * x + mix @ w_out[l], ln1_g[l], ln1_b[l])
        x = layer_norm(alpha * x + peer_ffn(x, peer_wq[l], peer_subkeys[l], peer_u[l], peer_v[l]),
                       ln2_g[l], ln2_b[l])
    return x
```

```python
import numpy as np
from contextlib import ExitStack
import concourse.bass as bass
import concourse.mybir as mybir
from concourse.alu_op_type import AluOpType as ALU
from concourse.bass_utils import run_bass_kernel_spmd

F32 = mybir.dt.float32
AF = mybir.ActivationFunctionType
AX = mybir.AxisListType

D = 2048
DEPTH = 4
NCORES = 8
ALPHA = (2.0 * DEPTH) ** 0.25
NEG = -30000.0
KSC = 128.0 ** -0.5

_sizes = [512, 512, 1024, 1024, 4, 4, 1024, 1536, 16, 1024, 256, 256, 6144]
_off = np.concatenate([[0], np.cumsum(_sizes)]).astype(int)
(O_MQ, O_MK, O_MV, O_MO, O_MI, O_MF, O_SZ, O_XBC, O_DT, O_AQ, O_AK, O_AV, O_G) = [int(v) for v in _off[:13]]
FCOLS = np.concatenate([np.arange(O_MQ, O_MQ + 512), np.arange(O_MK, O_MK + 512),
                        np.arange(O_XBC, O_XBC + 1536), np.arange(O_G, O_G + 6144)])
TCOLS = np.concatenate([np.arange(O_MK, O_MK + 512), np.arange(O_MV, O_MV + 1024), np.arange(O_MO, O_MO + 1024),
                        np.arange(O_SZ, O_SZ + 1024), np.arange(O_AQ, O_AQ + 1024), np.arange(O_AK, O_AK + 256),
                        np.arange(O_AV, O_AV + 256)])
SCOLS = np.concatenate([np.arange(O_MI, O_MI + 4), np.arange(O_MF, O_MF + 4), np.arange(O_DT, O_DT + 16)])

C_ID, C_TRI, C_AM, C_ONE, C_OND, C_NM4, C_MA, C_M0, NCONST = 0, 128, 256, 384, 512, 640, 1152, 1408, 1664
PC_CW, PC_CB, PC_GB, PC_L1G, PC_L1B, NPCOL = 0, 48, 60, 108, 124, 140
PR_MN, PR_SN, PR_AL, PR_DS, PR_SK, PR_L2G, PR_L2B, PR_BI, NPROW = 0, 1024, 2048, 2064, 2080, 2096, 4144, 6192, 6216


class KB:
    def __init__(self, nc):
        self.nc = nc
        self.eng = {"pe": nc.tensor, "act": nc.scalar, "dve": nc.vector, "pool": nc.gpsimd, "sp": nc.sync}
        self.sem = {}
        self.cnt = {}
        for e in self.eng:
            self.sem[e] = nc.alloc_semaphore(name="s_" + e)
            self.cnt[e] = 0
        self.seen = {e: {} for e in self.eng}
        self.W = {}
        self.R = {}
        self.dsem = {}
        self.ninstr = 0

    def _wait(self, e, deps):
        seen = self.seen[e]
        for sid, (sh, val) in deps.items():
            if seen.get(sid, 0) >= val:
                continue
            self.eng[e].wait_ge(sh, val)
            seen[sid] = val

    def _deps(self, e, reads, writes, selfsid):
        deps = {}

        def add(d):
            for sid, (sh, val) in d.items():
                if sid == selfsid and e == "pe":
                    continue
                if sid not in deps or deps[sid][1] < val:
                    deps[sid] = (sh, val)

        for k in reads:
            add(self.W.get(k, {}))
        for k in writes:
            add(self.W.get(k, {}))
            add(self.R.get(k, {}))
        return deps

    def _commit(self, reads, writes, sid, sh, val):
        for k in reads:
            self.R.setdefault(k, {})[sid] = (sh, val)
        for k in writes:
            self.W[k] = {sid: (sh, val)}
            self.R[k] = {}

    @staticmethod
    def _keys(xs):
        return [x if isinstance(x, str) else x.name for x in xs]

    def op(self, e, meth, *args, R=(), W=(), **kw):
        reads = self._keys(R)
        writes = self._keys(W)
        sid = "E" + e
        self._wait(e, self._deps(e, reads, writes, sid))
        ins = getattr(self.eng[e], meth)(*args, **kw)
        self.cnt[e] += 1
        ins.then_inc(self.sem[e], 1)
        self._commit(reads, writes, sid, self.sem[e], self.cnt[e])
        self.ninstr += 1
        return ins

    def dma(self, out, in_, slot, q="sp"):
        if slot not in self.dsem:
            self.dsem[slot] = [self.nc.alloc_semaphore(name="d_" + slot), 0]
        sh, val = self.dsem[slot]
        reads = [in_.name]
        writes = [out.name]
        sid = "D" + slot
        deps = self._deps(q, reads, writes, sid)
        if val > 0:
            deps[sid] = (sh, val)
        self._wait(q, deps)
        ins = self.eng[q].dma_start(out=out, in_=in_)
        val += 16
        ins.then_inc(sh, 16)
        self.dsem[slot][1] = val
        self._commit(reads, writes, sid, sh, val)
        self.ninstr += 1
        return ins

    def barrier(self):
        deps = {}
        for e in self.eng:
            if self.cnt[e] > 0:
                deps["E" + e] = (self.sem[e], self.cnt[e])
        for slot, (sh, val) in self.dsem.items():
            if val > 0:
                deps["D" + slot] = (sh, val)
        for e in self.eng:
            self._wait(e, deps)

    def mm(self, out, lhsT, rhs, start=True, stop=True):
        return self.op("pe", "matmul", out, lhsT, rhs, start=start, stop=stop, R=[lhsT, rhs], W=[out])

    def tr(self, out, in_, ident):
        return self.op("pe", "transpose", out, in_, ident, R=[in_, ident], W=[out])

    def act(self, out, in_, func, bias=None, scale=None, accum_out=None):
        kw = {}
        rs = [in_]
        ws = [out]
        if bias is not None:
            kw["bias"] = bias
            if not isinstance(bias, (int, float)):
                rs.append(bias)
        if scale is not None:
            kw["scale"] = scale
            if not isinstance(scale, (int, float)):
                rs.append(scale)
        if accum_out is not None:
            kw["accum_out"] = accum_out
            ws.append(accum_out)
        return self.op("act", "activation", out, in_, func, R=rs, W=ws, **kw)

    def tt(self, out, in0, in1, op, e="dve"):
        return self.op(e, "tensor_tensor", out, in0, in1, op, R=[in0, in1], W=[out])

    def ts(self, out, in0, s1, s2, op0, op1=None, e="dve"):
        rs = [in0] + [s for s in (s1, s2) if s is not None and not isinstance(s, (int, float))]
        if op1 is None:
            return self.op(e, "tensor_scalar", out, in0, s1, None, op0, R=rs, W=[out])
        return self.op(e, "tensor_scalar", out, in0, s1, s2, op0, op1, R=rs, W=[out])

    def stt(self, out, in0, scalar, in1, op0, op1):
        rs = [in0, in1] + ([] if isinstance(scalar, (int, float)) else [scalar])
        return self.op("dve", "scalar_tensor_tensor", out, in0, scalar, in1, op0, op1, R=rs, W=[out])

    def copy(self, out, in_, e="act"):
        if e == "act":
            return self.op("act", "copy", out, in_, R=[in_], W=[out])
        return self.op(e, "tensor_copy", out, in_, R=[in_], W=[out])

    def mul(self, out, in_, c):
        return self.op("act", "mul", out, in_, c, R=[in_], W=[out])

    def memset(self, ap, val, e="dve"):
        return self.op(e, "memset", ap, val, R=[], W=[ap])

    def recip(self, out, in_):
        return self.op("dve", "reciprocal", out, in_, R=[in_], W=[out])


def bc(ap, shape, axis):
    return ap.unsqueeze(axis).to_broadcast(list(shape))


class Prog:
    def __init__(self, S, nlayers, dbg=False):
        self.S = S
        self.NL = nlayers
        self.dbg = dbg
        self.BLK = min(512, S)
        self.NCH = S // 128
        nc = self.nc = bass.Bass("TRN2", target_bir_lowering=False)
        self.k = KB(nc)
        L = nlayers

        def din(name, shape):
            return nc.dram_tensor(name, list(shape), F32, kind="ExternalInput").ap()

        self.xT0 = din("xT0", [D, S])
        self.consts_d = din("consts", [128, NCONST])
        self.rope_d = din("rope", [S, 64])
        self.wf_d = din("wf", [L, 68, 128, 16, 128])
        self.wt_d = din("wt", [L, 20, 128, 16, 256])
        self.wsm_d = din("wsm", [L, 128, 16, 24])
        self.pcol_d = din("pcol", [L, 128, NPCOL])
        self.prow_d = din("prow", [L, 1, NPROW])
        self.wb_d = din("wb", [L, 3, 16, 128, 8, 128])
        self.wo_d = din("wo", [L, 16, 128, 16, 128])
        self.wq_d = din("wq", [L, 16, 128, 16, 128])
        self.skT_d = din("skT", [L, 128, 16, 128])
        self.uT_d = din("uT", [L, 128, 128, 16, 128])
        self.v_d = din("pv", [L, 128, 128, 2048])
        self.out_d = nc.dram_tensor("out", [S, D], F32, kind="ExternalOutput").ap()
        self.xTa = nc.dram_tensor("xTa", [D, S], F32, kind="Internal").ap()
        self.xTb_ = nc.dram_tensor("xTbb", [D, S], F32, kind="Internal").ap()
        self.x1T = nc.dram_tensor("x1T", [D, S], F32, kind="Internal").ap()
        self.yT = nc.dram_tensor("yT", [self.NCH, 128, 24, 128], F32, kind="Internal").ap()
        if dbg:
            self.dbg_y = nc.dram_tensor("dbg_y", [L, self.NCH, 128, 24, 128], F32, kind="ExternalOutput").ap()
            self.dbg_x1T = nc.dram_tensor("dbg_x1T", [L, D, S], F32, kind="ExternalOutput").ap()
        self.cst = nc.alloc_sbuf_tensor("cst", [128, NCONST], F32)
        self.ps = [nc.alloc_psum_tensor(f"ps{i}", [128, 512], F32) for i in range(8)]
        k = self.k
        k.dma(self.cst[:], self.consts_d[:], "cst")
        c = self.cst
        self.ident = c[:, C_ID:C_ID + 128]
        self.tri = c[:, C_TRI:C_TRI + 128]
        self.amat = c[:, C_AM:C_AM + 128]
        self.ones = c[:, C_ONE:C_ONE + 128]
        self.onesD = c[:, C_OND:C_OND + 128]
        self.nm4 = c[:, C_NM4:C_NM4 + 512]
        self.maskA = c[:, C_MA:C_MA + 256]
        self.mask0 = c[:, C_M0:C_M0 + 256]

    def build(self):
        k = self.k
        xin = self.xT0
        bufs = [self.xTa, self.xTb_]
        for l in range(self.NL):
            last = (l == self.NL - 1)
            xout = None if last else bufs[l % 2]
            self.phase_mlstm(l, xin)
            k.barrier()
            self.phase_ssd(l, xin)
            k.barrier()
            self.phase_swa(l, xin)
            k.barrier()
            if self.dbg:
                k.dma(self.dbg_y[l], self.yT, "dbgy", q="pool")
            self.phase_mix(l, xin)
            k.barrier()
            if self.dbg:
                k.dma(self.dbg_x1T[l], self.x1T, "dbgx", q="pool")
            self.phase_peer(l, xout)
            k.barrier()
            xin = xout
        deps = {}
        for key in ["out"] + (["dbg_y", "dbg_x1T"] if self.dbg else []):
            for sid, (sh, val) in k.W.get(key, {}).items():
                deps[sid] = (sh, val)
        k._wait("sp", deps)
        return self.nc

    def xT_view(self, xT, t0, n):
        return xT.rearrange("(kc p) t -> p kc t", p=128)[:, :, t0:t0 + n]

    def fmaj(self, l, c, w, bank, xTb, BLK):
        k = self.k
        k.dma(w[:], self.wf_d[l, c], "wf" + w.name[-1])
        for kc in range(16):
            k.mm(bank[:, 0:BLK], w[:, kc, :], xTb[:, kc, :], start=(kc == 0), stop=(kc == 15))

    def tmaj(self, l, ti, w, bank, xTb, cs):
        k = self.k
        k.dma(w[:], self.wt_d[l, ti], "wt" + w.name[-1])
        for kc in range(16):
            k.mm(bank[:, 0:256], xTb[:, kc, cs], w[:, kc, :], start=(kc == 0), stop=(kc == 15))

    def small_proj(self, wsm, xTb, cs, bank, small, biasb):
        k = self.k
        for kc in range(16):
            k.mm(bank[:, 0:24], xTb[:, kc, cs], wsm[:, kc, :], start=(kc == 0), stop=(kc == 15))
        k.tt(small[:], bank[:, 0:24], biasb, ALU.add)

    def emit_yT(self, ysrc, g, col0, yTc, banks):
        k = self.k
        for c in range(8):
            k.tr(banks[c // 4][:, (c % 4) * 128:(c % 4 + 1) * 128], ysrc[:, c * 128:(c + 1) * 128], self.ident)
        for hh in range(2):
            k.copy(yTc[:, hh * 4:(hh + 1) * 4, :], banks[hh][:, 0:512].rearrange("p (a b) -> p a b", a=4))
        k.dma(self.yT[g, :, col0:col0 + 8, :], yTc[:], "yst", q="pool")

    def phase_mlstm(self, l, xT):
        k, nc, ps = self.k, self.nc, self.ps
        BLK = self.BLK
        with ExitStack() as es:
            def sb(name, shape):
                return es.enter_context(nc.sbuf_tensor(f"L{l}a_{name}", list(shape), F32))
            xTb = sb("xTb", [128, 16, BLK])
            mqT = sb("mqT", [128, 4, BLK])
            mkT = sb("mkT", [128, 4, BLK])
            wf = [sb("wf0", [128, 16, 128]), sb("wf1", [128, 16, 128])]
            wt = [sb("wt0", [128, 16, 256]), sb("wt1", [128, 16, 256])]
            wsm = sb("wsm", [128, 16, 24])
            mktok = sb("mktok", [128, 512])
            mvext = sb("mvext", [128, 4, 257])
            osig = sb("osig", [128, 1024])
            small = sb("small", [128, 24])
            normw = sb("normw", [128, 1024])
            biasb = sb("biasb", [128, 24])
            Cext = [sb(f"C{h}", [128, 257]) for h in range(4)]
            e1 = sb("e1", [128, 4])
            lf = sb("lf", [128, 4])
            Bmat = sb("Bmat", [128, 4, 128])
            Dt = sb("Dt", [128, 4, 128])
            EB = sb("EB", [128, 4, 128])
            PT = sb("PT", [128, 128])
            qsT = sb("qsT", [128, 128])
            den = sb("den", [128, 1])
            hb = sb("hb", [128, 256])
            st6 = sb("st6", [128, 6])
            mv2 = sb("mv2", [128, 2])
            rstd = sb("rstd", [128, 1])
            kw = sb("kw", [128, 128])
            ym = sb("ym", [128, 1024])
            yTc = sb("yTc", [128, 8, 128])

            k.dma(wsm[:], self.wsm_d[l], "par0")
            k.dma(normw[:], self.prow_d[l, :, PR_MN:PR_MN + 1024].partition_broadcast(128), "par1")
            k.dma(biasb[:], self.prow_d[l, :, PR_BI:PR_BI + 24].partition_broadcast(128), "par2")
            for h in range(4):
                k.memset(Cext[h][:], 0.0)
            k.memset(mvext[:], 1.0)
            for b in range(self.S // BLK):
                t0 = b * BLK
                k.dma(xTb[:], self.xT_view(xT, t0, BLK), "xT")
                for c in range(8):
                    bank = ps[c % 2]
                    self.fmaj(l, c, wf[c % 2], bank, xTb, BLK)
                    if c < 4:
                        k.copy(mqT[:, c, :], bank[:, 0:BLK])
                    else:
                        k.mul(mkT[:, c - 4, :], bank[:, 0:BLK], KSC)
                for j in range(BLK // 128):
                    g = (t0 // 128) + j
                    cs = slice(j * 128, (j + 1) * 128)
                    for ti in range(10):
                        bank = ps[2 + ti % 2]
                        self.tmaj(l, ti, wt[ti % 2], bank, xTb, cs)
                        if ti < 2:
                            k.copy(mktok[:, ti * 256:(ti + 1) * 256], bank[:, 0:256])
                        elif ti < 6:
                            k.copy(mvext[:, ti - 2, 0:256], bank[:, 0:256])
                        else:
                            k.act(osig[:, (ti - 6) * 256:(ti - 5) * 256], bank[:, 0:256], AF.Sigmoid)
                    self.small_proj(wsm, xTb, cs, ps[4], small, biasb[:])
                    k.act(e1[:], small[:, 4:8], AF.Exp, scale=-1.0)
                    k.act(e1[:], e1[:], AF.Ln, bias=1.0)
                    k.mul(lf[:], e1[:], -1.0)
                    k.tt(Bmat[:], bc(lf[:], [128, 4, 128], 2), bc(self.tri, [128, 4, 128], 1), ALU.mult)
                    Bm2 = Bmat[:].rearrange("p h t -> p (h t)")
                    k.mm(ps[5][:, 0:512], self.amat, Bm2, start=True, stop=False)
                    k.mm(ps[5][:, 0:512], self.ident, self.nm4, start=False, stop=True)
                    k.mm(ps[6][:, 0:512], self.ones, Bm2)
                    for h in range(4):
                        k.act(Dt[:, h, :], ps[5][:, h * 128:(h + 1) * 128], AF.Exp, bias=small[:, h:h + 1])
                    k.act(EB[:].rearrange("p h t -> p (h t)"), ps[6][:, 0:512], AF.Exp)
                    for h in range(4):
                        k.mm(ps[7][:, 0:128], mkT[:, h, cs], mqT[:, h, cs])
                        k.tt(PT[:], ps[7][:, 0:128], Dt[:, h, :], ALU.mult)
                        k.tt(qsT[:], mqT[:, h, cs], EB[:, h, :], ALU.mult, e="pool")
                        nm = ps[h % 2]
                        k.mm(nm[:, 0:257], PT[:], mvext[:, h, :], start=True, stop=False)
                        k.mm(nm[:, 0:257], qsT[:], Cext[h][:], start=False, stop=True)
                        k.act(den[:], nm[:, 256:257], AF.Abs)
                        k.ts(den[:], den[:], 1.0, None, ALU.max)
                        k.recip(den[:], den[:])
                        k.ts(hb[:], nm[:, 0:256], den[:], None, ALU.mult)
                        k.op("dve", "bn_stats", st6[:], hb[:], R=[hb], W=[st6])
                        k.op("dve", "bn_aggr", mv2[:], st6[:], R=[st6], W=[mv2])
                        k.act(rstd[:], mv2[:, 1:2], AF.Sqrt, bias=1e-6)
                        k.recip(rstd[:], rstd[:])
                        k.ts(hb[:], hb[:], mv2[:, 0:1], rstd[:], ALU.subtract, ALU.mult)
                        k.tt(hb[:], hb[:], normw[:, h * 256:(h + 1) * 256], ALU.mult)
                        k.tt(ym[:, h * 256:(h + 1) * 256], hb[:], osig[:, h * 256:(h + 1) * 256], ALU.mult, e="pool")
                        k.ts(kw[:], mktok[:, h * 128:(h + 1) * 128], Dt[:, h, 127:128], KSC, ALU.mult, ALU.mult)
                        cu = ps[2 + h % 2]
                        k.mm(cu[:, 0:257], kw[:], mvext[:, h, :])
                        k.stt(Cext[h][:], Cext[h][:], EB[:, h, 127:128], cu[:, 0:257], ALU.mult, ALU.add)
                    self.emit_yT(ym, g, 0, yTc, [ps[4], ps[5]])

    def phase_ssd(self, l, xT):
        k, nc, ps = self.k, self.nc, self.ps
        BLK = self.BLK
        with ExitStack() as es:
            def sb(name, shape):
                return es.enter_context(nc.sbuf_tensor(f"L{l}b_{name}", list(shape), F32))
            xTb = sb("xTb", [128, 16, BLK])
            xbcT = sb("xbcT", [128, 12, BLK])
            u = sb("u", [128, BLK + 3])
            halo = sb("halo", [128, 12, 3])
            wf = [sb("wf0", [128, 16, 128]), sb("wf1", [128, 16, 128])]
            wt = [sb("wt0", [128, 16, 256]), sb("wt1", [128, 16, 256])]
            wsm = sb("wsm", [128, 16, 24])
            pcol = sb("pcol", [128, NPCOL])
            prw = sb("prw", [128, 1024 + 48])
            biasb = sb("biasb", [128, 24])
            zsil = sb("zsil", [128, 1024])
            small = sb("small", [128, 24])
            dt = sb("dt", [128, 16])
            aneg = sb("aneg", [128, 16])
            dtA = sb("dtA", [128, 16])
            ex = sb("ex", [128, 48])
            Bm = sb("Bm", [128, 16, 128])
            dec = sb("dec", [128, 16, 128])
            MT = sb("MT", [128, 16, 128])
            xstok = sb("xstok", [128, 16, 64])
            Btok = sb("Btok", [128, 2, 128])
            xdt = sb("xdt", [128, 16, 64])
            xw = sb("xw", [128, 16, 64])
            t1 = sb("t1", [128, 16, 64])
            t3 = sb("t3", [128, 16, 64])
            Dfull = sb("Dfull", [128, 16, 64])
            H = [sb("H0", [128, 8, 64]), sb("H1", [128, 8, 64])]
            ss = sb("ss", [128, 2])
            junk = sb("junk", [128, 512])
            ys = sb("ys", [128, 1024])
            yTc = sb("yTc", [128, 8, 128])

            k.dma(wsm[:], self.wsm_d[l], "par0")
            k.dma(pcol[:], self.pcol_d[l], "par1")
            k.dma(prw[:], self.prow_d[l, :, PR_SN:PR_SN + 1072].partition_broadcast(128), "par2")
            k.dma(biasb[:], self.prow_d[l, :, PR_BI:PR_BI + 24].partition_broadcast(128), "par3")
            snw = prw[:, 0:1024]
            k.act(aneg[:], prw[:, 1024:1040], AF.Exp)
            k.mul(aneg[:], aneg[:], -1.0)
            k.copy(Dfull[:], bc(prw[:, 1040:1056], [128, 16, 64], 2), e="dve")
            k.memset(halo[:], 0.0)
            k.memset(H[0][:], 0.0)
            k.memset(H[1][:], 0.0)
            for b in range(self.S // BLK):
                t0 = b * BLK
                k.dma(xTb[:], self.xT_view(xT, t0, BLK), "xT")
                for c in range(12):
                    bank = ps[c % 2]
                    self.fmaj(l, 8 + c, wf[c % 2], bank, xTb, BLK)
                    k.copy(u[:, 0:3], halo[:, c, :], e="pool")
                    k.copy(u[:, 3:3 + BLK], bank[:, 0:BLK])
                    k.copy(halo[:, c, :], u[:, BLK:BLK + 3], e="pool")
                    acc = xbcT[:, c, :]
                    k.ts(acc, u[:, 0:BLK], pcol[:, PC_CW + c * 4:PC_CW + c * 4 + 1], pcol[:, PC_CB + c:PC_CB + c + 1],
                         ALU.mult, ALU.add)
                    for kk in range(1, 4):
                        k.stt(acc, u[:, kk:kk + BLK], pcol[:, PC_CW + c * 4 + kk:PC_CW + c * 4 + kk + 1], acc,
                              ALU.mult, ALU.add)
                    k.act(acc, acc, AF.Silu)
                for j in range(BLK // 128):
                    g = (t0 // 128) + j
                    cs = slice(j * 128, (j + 1) * 128)
                    for ti in range(10, 14):
                        bank = ps[2 + ti % 2]
                        self.tmaj(l, ti, wt[ti % 2], bank, xTb, cs)
                        k.act(zsil[:, (ti - 10) * 256:(ti - 9) * 256], bank[:, 0:256], AF.Silu)
                    self.small_proj(wsm, xTb, cs, ps[4], small, biasb[:])
                    k.act(dt[:], small[:, 8:24], AF.Exp)
                    k.act(dt[:], dt[:], AF.Ln, bias=1.0)
                    k.tt(dtA[:], dt[:], aneg[:], ALU.mult)
                    X = ps[4]
                    k.mm(X[:, 0:16], self.tri, dtA[:])
                    k.mm(X[:, 16:32], self.amat, dtA[:])
                    k.mm(X[:, 32:48], self.ones, dtA[:])
                    k.act(ex[:], X[:, 0:48], AF.Exp)
                    eacum, erev, ecd = ex[:, 0:16], ex[:, 16:32], ex[:, 32:48]
                    k.tt(Bm[:], bc(dtA[:], [128, 16, 128], 2), bc(self.tri, [128, 16, 128], 1), ALU.mult)
                    for q in range(4):
                        bank = ps[5 + q % 2]
                        k.mm(bank[:, 0:512], self.amat, Bm[:, 4 * q:4 * q + 4, :].rearrange("p h t -> p (h t)"),
                             start=True, stop=False)
                        k.mm(bank[:, 0:512], self.ident, self.nm4, start=False, stop=True)
                        k.act(dec[:, 4 * q:4 * q + 4, :].rearrange("p h t -> p (h t)"), bank[:, 0:512], AF.Exp)
                    for c in range(8):
                        k.tr(ps[c // 4][:, (c % 4) * 128:(c % 4 + 1) * 128], xbcT[:, c, cs], self.ident)
                    xs2 = xstok[:].rearrange("p h d -> p (h d)")
                    k.copy(xs2[:, 0:512], ps[0][:, 0:512])
                    k.copy(xs2[:, 512:1024], ps[1][:, 0:512])
                    for gg in range(2):
                        k.tr(ps[7][:, gg * 128:(gg + 1) * 128], xbcT[:, 8 + gg, cs], self.ident)
                    k.copy(Btok[:].rearrange("p g n -> p (g n)"), ps[7][:, 0:256])
                    k.tt(xdt[:], xstok[:], bc(dt[:], [128, 16, 64], 2), ALU.mult)
                    k.tt(xw[:], xdt[:], bc(erev, [128, 16, 64], 2), ALU.mult, e="pool")
                    for gg in range(2):
                        k.mm(ps[7][:, 256 + gg * 128:256 + (gg + 1) * 128], xbcT[:, 8 + gg, cs], xbcT[:, 10 + gg, cs])
                    for gg in range(2):
                        k.tt(MT[:, gg * 8:(gg + 1) * 8, :], dec[:, gg * 8:(gg + 1) * 8, :],
                             bc(ps[7][:, 256 + gg * 128:256 + (gg + 1) * 128], [128, 8, 128], 1), ALU.mult)
                    for gg in range(2):
                        yd = ps[gg]
                        yo = ps[2 + gg]
                        for hh in range(8):
                            h = gg * 8 + hh
                            k.mm(yd[:, hh * 64:(hh + 1) * 64], MT[:, h, :], xdt[:, h, :])
                        k.mm(yo[:, 0:512], xbcT[:, 10 + gg, cs], H[gg][:].rearrange("p h d -> p (h d)"))
                        t1g = t1[:, gg * 8:(gg + 1) * 8, :]
                        k.tt(t1g, yo[:, 0:512].rearrange("p (h d) -> p h d", h=8),
                             bc(eacum[:, gg * 8:(gg + 1) * 8], [128, 8, 64], 2), ALU.mult)
                        k.tt(t1g, t1g, yd[:, 0:512].rearrange("p (h d) -> p h d", h=8), ALU.add)
                    k.tt(t3[:], xstok[:], Dfull[:], ALU.mult, e="pool")
                    k.tt(t1[:], t1[:], t3[:], ALU.add)
                    t12 = t1[:].rearrange("p h d -> p (h d)")
                    k.tt(t12, t12, zsil[:], ALU.mult)
                    for gg in range(2):
                        k.act(junk[:], t12[:, gg * 512:(gg + 1) * 512], AF.Square, accum_out=ss[:, gg:gg + 1])
                    k.act(ss[:], ss[:], AF.Sqrt, bias=1e-6, scale=1.0 / 512)
                    k.recip(ss[:], ss[:])
                    for gg in range(2):
                        k.stt(ys[:, gg * 512:(gg + 1) * 512], t12[:, gg * 512:(gg + 1) * 512], ss[:, gg:gg + 1],
                              snw[:, gg * 512:(gg + 1) * 512], ALU.mult, ALU.mult)
                    for gg in range(2):
                        sp_ = ps[5 + gg]
                        k.mm(sp_[:, 0:512], Btok[:, gg, :], xw[:, gg * 8:(gg + 1) * 8, :].rearrange("p h d -> p (h d)"))
                        k.tt(H[gg][:], H[gg][:], bc(ecd[:, gg * 8:(gg + 1) * 8], [128, 8, 64], 2), ALU.mult)
                        k.tt(H[gg][:], H[gg][:], sp_[:, 0:512].rearrange("p (h d) -> p h d", h=8), ALU.add)
                    self.emit_yT(ys, g, 8, yTc, [ps[0], ps[1]])

    def phase_swa(self, l, xT):
        k, nc, ps = self.k, self.nc, self.ps
        BLK = self.BLK
        with ExitStack() as es:
            def sb(name, shape):
                return es.enter_context(nc.sbuf_tensor(f"L{l}c_{name}", list(shape), F32))
            xTb = sb("xTb", [128, 16, BLK])
            wt = [sb("wt0", [128, 16, 256]), sb("wt1", [128, 16, 256])]
            sinks = sb("sinks", [128, 16])
            aq = sb("aq", [128, 16, 64])
            ak = sb("ak", [128, 4, 64])
            av = [sb("av0", [128, 256]), sb("av1", [128, 256])]
            cs_t = sb("cs", [128, 64])
            ta = sb("ta", [128, 16, 32])
            tb = sb("tb", [128, 16, 32])
            aqr = sb("aqr", [128, 16, 64])
            akd = sb("akd", [128, 4, 2, 64])
            qT = sb("qT", [128, 8, 128])
            kT = [sb("kT0", [128, 4, 128]), sb("kT1", [128, 4, 128])]
            Sm = sb("Sm", [128, 256])
            mx = sb("mx", [128, 1])
            negm = sb("negm", [128, 1])
            p = sb("p", [128, 256])
            rsum = sb("rsum", [128, 1])
            esk = sb("esk", [128, 1])
            den = sb("den", [128, 1])
            pT = sb("pT", [128, 256])
            ya = sb("ya", [128, 1024])
            yTc = sb("yTc", [128, 8, 128])

            k.dma(sinks[:], self.prow_d[l, :, PR_SK:PR_SK + 16].partition_broadcast(128), "par0")
            k.memset(kT[1][:], 0.0)
            k.memset(av[1][:], 0.0)
            for b in range(self.S // BLK):
                t0 = b * BLK
                k.dma(xTb[:], self.xT_view(xT, t0, BLK), "xT")
                for j in range(BLK // 128):
                    g = (t0 // 128) + j
                    cur, prv = g % 2, (g + 1) % 2
                    cs = slice(j * 128, (j + 1) * 128)
                    k.dma(cs_t[:], self.rope_d[g * 128:(g + 1) * 128, :], "rope")
                    aq2 = aq[:].rearrange("p h d -> p (h d)")
                    for ti in range(14, 20):
                        bank = ps[ti % 2]
                        self.tmaj(l, ti, wt[ti % 2], bank, xTb, cs)
                        if ti < 18:
                            k.copy(aq2[:, (ti - 14) * 256:(ti - 13) * 256], bank[:, 0:256])
                        elif ti == 18:
                            k.copy(ak[:].rearrange("p h d -> p (h d)"), bank[:, 0:256])
                        else:
                            k.copy(av[cur][:], bank[:, 0:256])
                    for (src, dst, nh) in ((aq, aqr[:], 16), (ak, akd[:, :, 0, :], 4)):
                        x1, x2 = src[:, :, 0:32], src[:, :, 32:64]
                        cb = bc(cs_t[:, 0:32], [128, nh, 32], 1)
                        sn = bc(cs_t[:, 32:64], [128, nh, 32], 1)
                        k.tt(ta[:, 0:nh, :], x1, cb, ALU.mult)
                        k.tt(tb[:, 0:nh, :], x2, sn, ALU.mult, e="pool")
                        k.tt(dst[:, :, 0:32], ta[:, 0:nh, :], tb[:, 0:nh, :], ALU.subtract)
                        k.tt(ta[:, 0:nh, :], x2, cb, ALU.mult)
                        k.tt(tb[:, 0:nh, :], x1, sn, ALU.mult, e="pool")
                        k.tt(dst[:, :, 32:64], ta[:, 0:nh, :], tb[:, 0:nh, :], ALU.add)
                    k.copy(akd[:, :, 1, :], akd[:, :, 0, :], e="pool")
                    aqr2 = aqr[:].rearrange("p h d -> p (h d)")
                    for m in range(8):
                        k.tr(ps[2 + m // 4][:, (m % 4) * 128:(m % 4 + 1) * 128], aqr2[:, m * 128:(m + 1) * 128], self.ident)
                    for hh in range(2):
                        k.copy(qT[:, hh * 4:(hh + 1) * 4, :], ps[2 + hh][:, 0:512].rearrange("p (a b) -> p a b", a=4))
                    for gg in range(4):
                        k.tr(ps[4][:, gg * 128:(gg + 1) * 128], akd[:, gg, :, :].rearrange("p a d -> p (a d)"), self.ident)
                    k.copy(kT[cur][:], ps[4][:, 0:512].rearrange("p (a b) -> p a b", a=4))
                    mask = self.mask0 if g == 0 else self.maskA
                    for h in range(16):
                        gg, base, m = h // 4, (h % 2) * 64, h // 2
                        sbk = ps[5 + h % 2]
                        k.mm(sbk[:, 0:128], qT[base:base + 64, m, :], kT[prv][base:base + 64, gg, :])
                        k.mm(sbk[:, 128:256], qT[base:base + 64, m, :], kT[cur][base:base + 64, gg, :])
                        k.stt(Sm[:], sbk[:, 0:256], 0.125, mask, ALU.mult, ALU.add)
                        k.op("dve", "reduce_max", mx[:], Sm[:], axis=AX.X, R=[Sm], W=[mx])
                        k.ts(negm[:], mx[:], sinks[:, h:h + 1], -1.0, ALU.max, ALU.mult)
                        k.act(p[:], Sm[:], AF.Exp, bias=negm[:], accum_out=rsum[:])
                        k.act(esk[:], sinks[:, h:h + 1], AF.Exp, bias=negm[:])
                        k.tt(den[:], rsum[:], esk[:], ALU.add)
                        k.recip(den[:], den[:])
                        pb = ps[h % 2]
                        k.tr(pb[:, 0:128], p[:, 0:128], self.ident)
                        k.tr(pb[:, 128:256], p[:, 128:256], self.ident)
                        k.copy(pT[:], pb[:, 0:256])
                        ob = ps[7]
                        k.mm(ob[:, 0:64], pT[:, 0:128], av[prv][:, gg * 64:(gg + 1) * 64], start=True, stop=False)
                        k.mm(ob[:, 0:64], pT[:, 128:256], av[cur][:, gg * 64:(gg + 1) * 64], start=False, stop=True)
                        k.ts(ya[:, h * 64:(h + 1) * 64], ob[:, 0:64], den[:], None, ALU.mult)
                    self.emit_yT(ya, g, 16, yTc, [ps[2], ps[3]])

    def phase_mix(self, l, xT):
        k, nc, ps = self.k, self.nc, self.ps
        BLK = self.BLK
        with ExitStack() as es:
            def sb(name, shape):
                return es.enter_context(nc.sbuf_tensor(f"L{l}d_{name}", list(shape), F32))
            xTb = sb("xTb", [128, 16, BLK])
            yTb = sb("yTb", [128, 24, BLK])
            mixT = sb("mixT", [128, 16, BLK])
            wg = [sb("wg0", [128, 16, 128]), sb("wg1", [128, 16, 128])]
            wb = [sb("wb0", [128, 8, 128]), sb("wb1", [128, 8, 128])]
            wo = [sb("wo0", [128, 16, 128]), sb("wo1", [128, 16, 128])]
            pcol = sb("pcol", [128, NPCOL])
            gsb = [sb("gsb0", [128, BLK]), sb("gsb1", [128, BLK])]
            tmp = sb("tmp", [128, BLK])
            sq = [sb("sq0", [128, BLK]), sb("sq1", [128, BLK])]
            rstd = sb("rstd", [128, BLK])
            k.dma(pcol[:], self.pcol_d[l], "par0")
            n = 0
            for b in range(self.S // BLK):
                t0 = b * BLK
                k.dma(xTb[:], self.xT_view(xT, t0, BLK), "xT")
                for j in range(BLK // 128):
                    k.dma(yTb[:, :, j * 128:(j + 1) * 128], self.yT[t0 // 128 + j], "yT")
                for c in range(16):
                    for kk in range(3):
                        n += 1
                        w = wg[n % 2]
                        k.dma(w[:], self.wf_d[l, 20 + kk * 16 + c], "wg" + w.name[-1])
                        gb = ps[n % 2]
                        for kc in range(16):
                            k.mm(gb[:, 0:BLK], w[:, kc, :], xTb[:, kc, :], start=(kc == 0), stop=(kc == 15))
                        gs = gsb[n % 2]
                        k.act(gs[:], gb[:, 0:BLK], AF.Sigmoid, bias=pcol[:, PC_GB + kk * 16 + c:PC_GB + kk * 16 + c + 1])
                        w2 = wb[n % 2]
                        k.dma(w2[:], self.wb_d[l, kk, c], "wb" + w2.name[-1])
                        bb = ps[2 + n % 2]
                        for kc in range(8):
                            k.mm(bb[:, 0:BLK], w2[:, kc, :], yTb[:, kk * 8 + kc, :], start=(kc == 0), stop=(kc == 7))
                        if kk == 0:
                            k.tt(mixT[:, c, :], gs[:], bb[:, 0:BLK], ALU.mult)
                        else:
                            k.tt(tmp[:], gs[:], bb[:, 0:BLK], ALU.mult)
                            k.tt(mixT[:, c, :], mixT[:, c, :], tmp[:], ALU.add, e="pool")
                for c in range(16):
                    w = wo[c % 2]
                    k.dma(w[:], self.wo_d[l, c], "wo" + w.name[-1])
                    ob = ps[4 + c % 2]
                    for kc in range(16):
                        k.mm(ob[:, 0:BLK], w[:, kc, :], mixT[:, kc, :], start=(kc == 0), stop=(kc == 15))
                    k.stt(xTb[:, c, :], xTb[:, c, :], ALPHA, ob[:, 0:BLK], ALU.mult, ALU.add)
                mb, vb = ps[6], ps[7]
                for c in range(16):
                    k.mm(mb[:, 0:BLK], self.onesD, xTb[:, c, :], start=(c == 0), stop=(c == 15))
                for c in range(16):
                    k.tt(xTb[:, c, :], xTb[:, c, :], mb[:, 0:BLK], ALU.subtract)
                    s_ = sq[c % 2]
                    k.act(s_[:], xTb[:, c, :], AF.Square)
                    k.mm(vb[:, 0:BLK], self.onesD, s_[:], start=(c == 0), stop=(c == 15))
                k.act(rstd[:], vb[:, 0:BLK], AF.Sqrt, bias=1e-5)
                k.recip(rstd[:], rstd[:])
                for c in range(16):
                    k.tt(xTb[:, c, :], xTb[:, c, :], rstd[:], ALU.mult)
                    k.ts(xTb[:, c, :], xTb[:, c, :], pcol[:, PC_L1G + c:PC_L1G + c + 1], pcol[:, PC_L1B + c:PC_L1B + c + 1],
                         ALU.mult, ALU.add, e="pool")
                k.dma(self.xT_view(self.x1T, t0, BLK), xTb[:], "x1st", q="pool")

    def phase_peer(self, l, xTout):
        k, nc, ps = self.k, self.nc, self.ps
        with ExitStack() as es:
            def sb(name, shape):
                return es.enter_context(nc.sbuf_tensor(f"L{l}e_{name}", list(shape), F32))
            xt = sb("xt", [128, 16, 128])
            qT = sb("qT", [128, 16, 128])
            skT = sb("skT", [128, 16, 128])
            wq = [sb("wq0", [128, 16, 128]), sb("wq1", [128, 16, 128])]
            sall = sb("sall", [128, 16, 128])
            x1tok = sb("x1tok", [128, 2048])
            top = sb("top", [128, 2, 16])
            tmp = sb("tmp", [128, 128])
            cand = sb("cand", [128, 256])
            tmp2 = sb("tmp2", [128, 256])
            best = sb("best", [128, 16])
            negmx = sb("negmx", [128, 1])
            j16 = sb("j16", [128, 16])
            Z = sb("Z", [128, 1])
            tau = sb("tau", [128, 8])
            biasE = sb("biasE", [128, 8])
            Lb = [sb("Lb0", [128, 8, 128]), sb("Lb1", [128, 8, 128])]
            Eb = [sb("Eb0", [128, 8, 128]), sb("Eb1", [128, 8, 128])]
            Mb = [sb("Mb0", [128, 8, 128]), sb("Mb1", [128, 8, 128])]
            Gacc = [sb("G0", [128, 8, 128]), sb("G1", [128, 8, 128])]
            ut = [sb(f"ut{i}", [128, 16, 128]) for i in range(3)]
            vt = [sb(f"vt{i}", [128, 2048]) for i in range(3)]
            ga = [sb("ga0", [128, 128]), sb("ga1", [128, 128])]
            hT = [sb("hT0", [128, 128]), sb("hT1", [128, 128])]
            ln2 = sb("ln2", [128, 4096])
            st = sb("st", [128, 24])
            mv2 = sb("mv2", [128, 2])
            rstd = sb("rstd", [128, 1])
            xTn = sb("xTn", [128, 16, 128])

            k.dma(skT[:], self.skT_d[l], "par0")
            k.dma(ln2[:], self.prow_d[l, :, PR_L2G:PR_L2G + 4096].partition_broadcast(128), "par1")
            ne = 0
            for g in range(self.NCH):
                k.dma(xt[:], self.xT_view(self.x1T, g * 128, 128), "xT")
                for c in range(16):
                    w = wq[c % 2]
                    k.dma(w[:], self.wq_d[l, c], "wq" + w.name[-1])
                    bank = ps[4 + c % 2]
                    for kc in range(16):
                        k.mm(bank[:, 0:128], w[:, kc, :], xt[:, kc, :], start=(kc == 0), stop=(kc == 15))
                    k.copy(qT[:, c, :], bank[:, 0:128])
                for c in range(16):
                    k.mm(ps[c // 4][:, (c % 4) * 128:(c % 4 + 1) * 128], qT[:, c, :], skT[:, c, :])
                for q in range(4):
                    k.copy(sall[:, q * 4:(q + 1) * 4, :], ps[q][:, 0:512].rearrange("p (a b) -> p a b", a=4))
                for c in range(16):
                    k.tr(ps[4 + (c // 4) % 2][:, (c % 4) * 128:(c % 4 + 1) * 128], xt[:, c, :], self.ident)
                    if c % 4 == 3:
                        q = c // 4
                        k.copy(x1tok[:, q * 512:(q + 1) * 512], ps[4 + q % 2][:, 0:512], e="dve")
                for h in range(8):
                    for half in range(2):
                        sv = sall[:, 2 * h + half, :]
                        k.op("dve", "max", top[:, half, 0:8], sv, R=[sall], W=[top])
                        k.op("dve", "match_replace", tmp[:], top[:, half, 0:8], sv, -1e30, R=[top, sall], W=[tmp])
                        k.op("dve", "max", top[:, half, 8:16], tmp[:], R=[tmp], W=[top])
                    k.tt(cand[:].rearrange("p (a b) -> p a b", a=16), bc(top[:, 0, :], [128, 16, 16], 2),
                         bc(top[:, 1, :], [128, 16, 16], 1), ALU.add)
                    k.op("dve", "max", best[:, 0:8], cand[:], R=[cand], W=[best])
                    k.op("dve", "match_replace", tmp2[:], best[:, 0:8], cand[:], -1e30, R=[best, cand], W=[tmp2])
                    k.op("dve", "max", best[:, 8:16], tmp2[:], R=[tmp2], W=[best])
                    k.ts(negmx[:], best[:, 0:1], -1.0, None, ALU.mult)
                    k.act(j16[:], best[:], AF.Exp, bias=negmx[:], accum_out=Z[:])
                    k.act(Z[:], Z[:], AF.Ln)
                    k.tt(biasE[:, h:h + 1], negmx[:], Z[:], ALU.subtract)
                    k.copy(tau[:, h:h + 1], best[:, 15:16], e="dve")
                nb = 0
                for ib in range(16):
                    G = Gacc[ib % 2]
                    for h in range(8):
                        nb += 1
                        Lh, Eh, Mh = Lb[nb % 2], Eb[nb % 2], Mb[nb % 2]
                        k.tt(Lh[:], bc(sall[:, 2 * h, ib * 8:(ib + 1) * 8], [128, 8, 128], 2),
                             bc(sall[:, 2 * h + 1, :], [128, 8, 128], 1), ALU.add, e="pool")
                        k.act(Eh[:], Lh[:], AF.Exp, bias=biasE[:, h:h + 1])
                        if h == 0:
                            k.stt(G[:], Lh[:], tau[:, h:h + 1], Eh[:], ALU.is_ge, ALU.mult)
                        else:
                            k.stt(Mh[:], Lh[:], tau[:, h:h + 1], Eh[:], ALU.is_ge, ALU.mult)
                            k.tt(G[:], G[:], Mh[:], ALU.add, e="pool")
                    for i in range(8):
                        ec = ib * 8 + i
                        ne += 1
                        u_, v_ = ut[ne % 3], vt[ne % 3]
                        k.dma(u_[:], self.uT_d[l, ec], "ut" + u_.name[-1])
                        k.dma(v_[:], self.v_d[l, ec], "vt" + v_.name[-1])
                        gbank = ps[6 + (ec // 4) % 2]
                        gsl = gbank[:, (ec % 4) * 128:(ec % 4 + 1) * 128]
                        k.tr(gsl, G[:, i, :], self.ident)
                        abank = ps[4 + ec % 2]
                        for kc in range(16):
                            k.mm(abank[:, 0:128], u_[:, kc, :], xt[:, kc, :], start=(kc == 0), stop=(kc == 15))
                        g_ = ga[ec % 2]
                        k.act(g_[:], abank[:, 0:128], AF.Gelu)
                        h_ = hT[ec % 2]
                        k.tt(h_[:], g_[:], gsl, ALU.mult)
                        for q in range(4):
                            k.mm(ps[q][:, 0:512], h_[:], v_[:, q * 512:(q + 1) * 512], start=(ec == 0), stop=(ec == 127))
                for q in range(4):
                    k.stt(x1tok[:, q * 512:(q + 1) * 512], x1tok[:, q * 512:(q + 1) * 512], ALPHA, ps[q][:, 0:512],
                          ALU.mult, ALU.add)
                    k.op("dve", "bn_stats", st[:, q * 6:(q + 1) * 6], x1tok[:, q * 512:(q + 1) * 512], R=[x1tok], W=[st])
                k.op("dve", "bn_aggr", mv2[:], st[:], R=[st], W=[mv2])
                k.act(rstd[:], mv2[:, 1:2], AF.Sqrt, bias=1e-5)
                k.recip(rstd[:], rstd[:])
                k.ts(x1tok[:], x1tok[:], mv2[:, 0:1], rstd[:], ALU.subtract, ALU.mult)
                k.tt(x1tok[:], x1tok[:], ln2[:, 0:2048], ALU.mult)
                k.tt(x1tok[:], x1tok[:], ln2[:, 2048:4096], ALU.add, e="pool")
                if xTout is None:
                    k.dma(self.out_d[g * 128:(g + 1) * 128, :], x1tok[:], "ost", q="sp")
                else:
                    for c in range(16):
                        k.tr(ps[4 + (c // 4) % 2][:, (c % 4) * 128:(c % 4 + 1) * 128], x1tok[:, c * 128:(c + 1) * 128],
                             self.ident)
                        if c % 4 == 3:
                            q = c // 4
                            k.copy(xTn[:, q * 4:(q + 1) * 4, :], ps[4 + q % 2][:, 0:512].rearrange("p (a b) -> p a b", a=4))
                    k.dma(self.xT_view(xTout, g * 128, 128), xTn[:], "ost", q="sp")


def make_consts():
    c = np.zeros((128, NCONST), np.float32)
    r = np.arange(128)
    c[:, C_ID:C_ID + 128] = np.eye(128)
    c[:, C_TRI:C_TRI + 128] = (r[:, None] <= r[None, :])
    c[:, C_AM:C_AM + 128] = (r[:, None] > r[None, :])
    c[:, C_ONE:C_ONE + 128] = 1.0
    c[:, C_OND:C_OND + 128] = 1.0 / D
    nm = np.where(r[:, None] > r[None, :], NEG, 0.0)
    c[:, C_NM4:C_NM4 + 512] = np.tile(nm, (1, 4))
    prevm = np.where(r[None, :] > r[:, None], 0.0, NEG)
    curm = np.where(r[None, :] <= r[:, None], 0.0, NEG)
    c[:, C_MA:C_MA + 256] = np.concatenate([prevm, curm], 1)
    c[:, C_M0:C_M0 + 256] = np.concatenate([np.full((128, 128), NEG), curm], 1)
    return c


def make_rope(S):
    half = 32
    freqs = (np.float32(10000.0) ** (-np.arange(half, dtype=np.float32) / np.float32(half))).astype(np.float32)
    ang = np.arange(S, dtype=np.float32)[:, None] * freqs[None, :]
    return np.concatenate([np.cos(ang), np.sin(ang)], 1).astype(np.float32)


def tile_w(w, cols, tw):
    ws = w[:, cols]
    n = ws.shape[1] // tw
    return np.ascontiguousarray(ws.reshape(16, 128, n, tw).transpose(2, 1, 0, 3))


def prep_weights(inp, NL):
    f = lambda a: np.asarray(a, dtype=np.float32)
    w_in = inp["w_in"]
    out = {}
    out["wf"] = np.stack([tile_w(f(w_in[l]), FCOLS, 128) for l in range(NL)])
    out["wt"] = np.stack([tile_w(f(w_in[l]), TCOLS, 256) for l in range(NL)])
    out["wsm"] = np.stack([tile_w(f(w_in[l]), SCOLS, 24)[0] for l in range(NL)])
    pcol = np.zeros((NL, 128, NPCOL), np.float32)
    prow = np.zeros((NL, 1, NPROW), np.float32)
    for l in range(NL):
        cw = f(inp["ssm_conv_w"][l])[:, 0, :]
        pcol[l, :, PC_CW:PC_CW + 48] = cw.reshape(4, 12, 128).transpose(2, 1, 0).reshape(128, 48)
        pcol[l, :, PC_CB:PC_CB + 12] = f(inp["ssm_conv_b"][l]).reshape(12, 128).T
        pcol[l, :, PC_GB:PC_GB + 48] = f(inp["merge_gate_b"][l]).reshape(48, 128).T
        pcol[l, :, PC_L1G:PC_L1G + 16] = f(inp["ln1_g"][l]).reshape(16, 128).T
        pcol[l, :, PC_L1B:PC_L1B + 16] = f(inp["ln1_b"][l]).reshape(16, 128).T
        prow[l, 0, PR_MN:PR_MN + 1024] = f(inp["mlstm_norm_w"][l])
        prow[l, 0, PR_SN:PR_SN + 1024] = f(inp["ssm_norm_w"][l])
        prow[l, 0, PR_AL:PR_AL + 16] = f(inp["ssm_a_log"][l])
        prow[l, 0, PR_DS:PR_DS + 16] = f(inp["ssm_d"][l])
        prow[l, 0, PR_SK:PR_SK + 16] = f(inp["swa_sinks"][l])
        prow[l, 0, PR_L2G:PR_L2G + 2048] = f(inp["ln2_g"][l])
        prow[l, 0, PR_L2B:PR_L2B + 2048] = f(inp["ln2_b"][l])
        prow[l, 0, PR_BI:PR_BI + 24] = np.concatenate([f(inp["mlstm_gate_b"][l, 0]), f(inp["mlstm_gate_b"][l, 1]),
                                                       f(inp["ssm_dt_bias"][l])])
    out["pcol"] = pcol
    out["prow"] = prow
    wb = f(inp["w_branch"][:NL])
    out["wb"] = np.ascontiguousarray(wb.reshape(NL, 3, 8, 128, 16, 128).transpose(0, 1, 4, 3, 2, 5))
    all_cols = np.arange(2048)
    out["wo"] = np.stack([tile_w(f(inp["w_out"][l]), all_cols, 128) for l in range(NL)])
    out["wq"] = np.stack([tile_w(f(inp["peer_wq"][l]), all_cols, 128) for l in range(NL)])
    sk = f(inp["peer_subkeys"][:NL])
    out["skT"] = np.ascontiguousarray(sk.reshape(NL, 16, 128, 128).transpose(0, 3, 1, 2))
    u = f(inp["peer_u"][:NL])
    out["uT"] = np.ascontiguousarray(u.reshape(NL, 128, 128, 16, 128).transpose(0, 1, 4, 3, 2))
    out["pv"] = np.ascontiguousarray(f(inp["peer_v"][:NL]).reshape(NL, 128, 128, 2048))
    return out


_CACHE = {}


def run(inp, S, NL, B, dbg=False):
    key = (S, NL, dbg)
    prog = Prog(S, NL, dbg)
    nc = prog.build()
    wts = prep_weights(inp, NL)
    wts["consts"] = make_consts()
    wts["rope"] = make_rope(S)
    x = np.asarray(inp["x"], dtype=np.float32)
    in_maps = []
    for b in range(B):
        m = dict(wts)
        m["xT0"] = np.ascontiguousarray(x[b].T)
        in_maps.append(m)
    res = run_bass_kernel_spmd(nc, in_maps, core_ids=list(range(B)))
    return res, prog


def kernel(**inputs):
    S = inputs["x"].shape[1]
    B = inputs["x"].shape[0]
    res, _ = run(inputs, S, DEPTH, B)
    return np.stack([np.asarray(r["out"]) for r in res.results], 0).astype(np.float32)
```

```python
import numpy as np
from contextlib import ExitStack
import concourse.bass as bass
import concourse.mybir as mybir
from concourse.alu_op_type import AluOpType as ALU
from concourse.bass_utils import run_bass_kernel_spmd

F32 = mybir.dt.float32
BF16 = mybir.dt.bfloat16
AF = mybir.ActivationFunctionType
AX = mybir.AxisListType

D = 2048
DEPTH = 4
NCORES = 8
ALPHA = (2.0 * DEPTH) ** 0.25
NEG = -30000.0
KSC = 128.0 ** -0.5

_sizes = [512, 512, 1024, 1024, 4, 4, 1024, 1536, 16, 1024, 256, 256, 6144]
_off = np.concatenate([[0], np.cumsum(_sizes)]).astype(int)
(O_MQ, O_MK, O_MV, O_MO, O_MI, O_MF, O_SZ, O_XBC, O_DT, O_AQ, O_AK, O_AV, O_G) = [int(v) for v in _off[:13]]
FCOLS = np.concatenate([np.arange(O_MQ, O_MQ + 512), np.arange(O_MK, O_MK + 512),
                        np.arange(O_XBC, O_XBC + 1536), np.arange(O_G, O_G + 6144)])
TCOLS = np.concatenate([np.arange(O_MK, O_MK + 512), np.arange(O_MV, O_MV + 1024), np.arange(O_MO, O_MO + 1024),
                        np.arange(O_SZ, O_SZ + 1024), np.arange(O_AQ, O_AQ + 1024), np.arange(O_AK, O_AK + 256),
                        np.arange(O_AV, O_AV + 256)])
SCOLS = np.concatenate([np.arange(O_MI, O_MI + 4), np.arange(O_MF, O_MF + 4), np.arange(O_DT, O_DT + 16)])

C_ID, C_TRI, C_AM, C_ONE, C_OND, C_NM4, C_MA, C_M0, NCONST = 0, 128, 256, 384, 512, 640, 1152, 1408, 1664
PC_CW, PC_CB, PC_GB, PC_L1G, PC_L1B, NPCOL = 0, 48, 60, 108, 124, 140
PR_MN, PR_SN, PR_AL, PR_DS, PR_SK, PR_L2G, PR_L2B, PR_BI, NPROW = 0, 1024, 2048, 2064, 2080, 2096, 4144, 6192, 6216


class KB:
    def __init__(self, nc):
        self.nc = nc
        self.eng = {"pe": nc.tensor, "act": nc.scalar, "dve": nc.vector, "pool": nc.gpsimd, "sp": nc.sync}
        self.sem = {}
        self.cnt = {}
        for e in self.eng:
            self.sem[e] = nc.alloc_semaphore(name="s_" + e)
            self.cnt[e] = 0
        self.seen = {e: {} for e in self.eng}
        self.W = {}
        self.R = {}
        self.dsem = {}
        self.ninstr = 0

    def _wait(self, e, deps):
        seen = self.seen[e]
        for sid, (sh, val) in deps.items():
            if seen.get(sid, 0) >= val:
                continue
            self.eng[e].wait_ge(sh, val)
            seen[sid] = val

    def _deps(self, e, reads, writes, selfsid):
        deps = {}

        def add(d):
            for sid, (sh, val) in d.items():
                if sid == selfsid and e == "pe":
                    continue
                if sid not in deps or deps[sid][1] < val:
                    deps[sid] = (sh, val)

        for k in reads:
            add(self.W.get(k, {}))
        for k in writes:
            add(self.W.get(k, {}))
            add(self.R.get(k, {}))
        return deps

    def _commit(self, reads, writes, sid, sh, val):
        for k in reads:
            self.R.setdefault(k, {})[sid] = (sh, val)
        for k in writes:
            self.W[k] = {sid: (sh, val)}
            self.R[k] = {}

    @staticmethod
    def _keys(xs):
        return [x if isinstance(x, str) else x.name for x in xs]

    def op(self, e, meth, *args, R=(), W=(), **kw):
        reads = self._keys(R)
        writes = self._keys(W)
        sid = "E" + e
        self._wait(e, self._deps(e, reads, writes, sid))
        ins = getattr(self.eng[e], meth)(*args, **kw)
        self.cnt[e] += 1
        ins.then_inc(self.sem[e], 1)
        self._commit(reads, writes, sid, self.sem[e], self.cnt[e])
        self.ninstr += 1
        return ins

    def dma(self, out, in_, slot, q="sp"):
        if slot not in self.dsem:
            self.dsem[slot] = [self.nc.alloc_semaphore(name="d_" + slot), 0]
        sh, val = self.dsem[slot]
        reads = [in_.name]
        writes = [out.name]
        sid = "D" + slot
        deps = self._deps(q, reads, writes, sid)
        if val > 0:
            deps[sid] = (sh, val)
        self._wait(q, deps)
        ins = self.eng[q].dma_start(out=out, in_=in_)
        val += 16
        ins.then_inc(sh, 16)
        self.dsem[slot][1] = val
        self._commit(reads, writes, sid, sh, val)
        self.ninstr += 1
        return ins

    def barrier(self):
        deps = {}
        for e in self.eng:
            if self.cnt[e] > 0:
                deps["E" + e] = (self.sem[e], self.cnt[e])
        for slot, (sh, val) in self.dsem.items():
            if val > 0:
                deps["D" + slot] = (sh, val)
        for e in self.eng:
            self._wait(e, deps)

    def mm(self, out, lhsT, rhs, start=True, stop=True):
        return self.op("pe", "matmul", out, lhsT, rhs, start=start, stop=stop, R=[lhsT, rhs], W=[out])

    def tr(self, out, in_, ident):
        return self.op("pe", "transpose", out, in_, ident, R=[in_, ident], W=[out])

    def act(self, out, in_, func, bias=None, scale=None, accum_out=None):
        kw = {}
        rs = [in_]
        ws = [out]
        if bias is not None:
            kw["bias"] = bias
            if not isinstance(bias, (int, float)):
                rs.append(bias)
        if scale is not None:
            kw["scale"] = scale
            if not isinstance(scale, (int, float)):
                rs.append(scale)
        if accum_out is not None:
            kw["accum_out"] = accum_out
            ws.append(accum_out)
        return self.op("act", "activation", out, in_, func, R=rs, W=ws, **kw)

    def tt(self, out, in0, in1, op, e="dve"):
        return self.op(e, "tensor_tensor", out, in0, in1, op, R=[in0, in1], W=[out])

    def ts(self, out, in0, s1, s2, op0, op1=None, e="dve"):
        rs = [in0] + [s for s in (s1, s2) if s is not None and not isinstance(s, (int, float))]
        if op1 is None:
            return self.op(e, "tensor_scalar", out, in0, s1, None, op0, R=rs, W=[out])
        return self.op(e, "tensor_scalar", out, in0, s1, s2, op0, op1, R=rs, W=[out])

    def stt(self, out, in0, scalar, in1, op0, op1):
        rs = [in0, in1] + ([] if isinstance(scalar, (int, float)) else [scalar])
        return self.op("dve", "scalar_tensor_tensor", out, in0, scalar, in1, op0, op1, R=rs, W=[out])

    def copy(self, out, in_, e="act"):
        if e == "act":
            return self.op("act", "copy", out, in_, R=[in_], W=[out])
        return self.op(e, "tensor_copy", out, in_, R=[in_], W=[out])

    def mul(self, out, in_, c):
        return self.op("act", "mul", out, in_, c, R=[in_], W=[out])

    def memset(self, ap, val, e="dve"):
        return self.op(e, "memset", ap, val, R=[], W=[ap])

    def recip(self, out, in_):
        return self.op("dve", "reciprocal", out, in_, R=[in_], W=[out])


def bc(ap, shape, axis):
    return ap.unsqueeze(axis).to_broadcast(list(shape))


class Prog:
    def __init__(self, S, nlayers, dbg=False):
        self.S = S
        self.NL = nlayers
        self.dbg = dbg
        self.BLK = min(512, S)
        self.NCH = S // 128
        nc = self.nc = bass.Bass("TRN2", target_bir_lowering=False)
        self.k = KB(nc)
        L = nlayers

        def din(name, shape):
            return nc.dram_tensor(name, list(shape), F32, kind="ExternalInput").ap()

        self.xT0 = din("xT0", [D, S])
        self.consts_d = din("consts", [128, NCONST])
        self.rope_d = din("rope", [S, 64])
        self.wf_d = din("wf", [L, 68, 128, 16, 128])
        self.wt_d = din("wt", [L, 20, 128, 16, 256])
        self.wsm_d = din("wsm", [L, 128, 16, 24])
        self.pcol_d = din("pcol", [L, 128, NPCOL])
        self.prow_d = din("prow", [L, 1, NPROW])
        self.wb_d = din("wb", [L, 3, 16, 128, 8, 128])
        self.wo_d = din("wo", [L, 16, 128, 16, 128])
        self.wq_d = din("wq", [L, 16, 128, 16, 128])
        self.skT_d = din("skT", [L, 128, 16, 128])
        self.uT_d = din("uT", [L, 128, 128, 16, 128])
        self.v_d = din("pv", [L, 128, 128, 2048])
        self.out_d = nc.dram_tensor("out", [S, D], F32, kind="ExternalOutput").ap()
        self.xTa = nc.dram_tensor("xTa", [D, S], F32, kind="Internal").ap()
        self.xTb_ = nc.dram_tensor("xTbb", [D, S], F32, kind="Internal").ap()
        self.x1T = nc.dram_tensor("x1T", [D, S], F32, kind="Internal").ap()
        self.yT = nc.dram_tensor("yT", [self.NCH, 128, 24, 128], F32, kind="Internal").ap()
        self.uTb = nc.dram_tensor("uTb", [128, 128, 2048], BF16, kind="Internal").ap()
        self.wfb = nc.dram_tensor("wfb", [68, 128, 2048], BF16, kind="Internal").ap()
        self.wtb = nc.dram_tensor("wtb", [20, 128, 4096], BF16, kind="Internal").ap()
        self.wbb = nc.dram_tensor("wbb", [48, 128, 1024], BF16, kind="Internal").ap()
        self.wob = nc.dram_tensor("wob", [16, 128, 2048], BF16, kind="Internal").ap()
        self.vbf = nc.dram_tensor("vbf", [128, 128, 2048], BF16, kind="Internal").ap()
        if dbg:
            self.dbg_y = nc.dram_tensor("dbg_y", [L, self.NCH, 128, 24, 128], F32, kind="ExternalOutput").ap()
            self.dbg_x1T = nc.dram_tensor("dbg_x1T", [L, D, S], F32, kind="ExternalOutput").ap()
        self.cst = nc.alloc_sbuf_tensor("cst", [128, NCONST], F32)
        self.ps = [nc.alloc_psum_tensor(f"ps{i}", [128, 512], F32) for i in range(8)]
        k = self.k
        k.dma(self.cst[:], self.consts_d[:], "cst")
        c = self.cst
        self.ident = c[:, C_ID:C_ID + 128]
        self.tri = c[:, C_TRI:C_TRI + 128]
        self.amat = c[:, C_AM:C_AM + 128]
        self.ones = c[:, C_ONE:C_ONE + 128]
        self.onesD = c[:, C_OND:C_OND + 128]
        self.nm4 = c[:, C_NM4:C_NM4 + 512]
        self.maskA = c[:, C_MA:C_MA + 256]
        self.mask0 = c[:, C_M0:C_M0 + 256]

    def build(self):
        k = self.k
        xin = self.xT0
        bufs = [self.xTa, self.xTb_]
        for l in range(self.NL):
            last = (l == self.NL - 1)
            xout = None if last else bufs[l % 2]
            self.phase_wconv(l)
            k.barrier()
            if "a" in PHASES:
                self.phase_mlstm(l, xin)
                k.barrier()
            if "b" in PHASES:
                self.phase_ssd(l, xin)
                k.barrier()
            if "c" in PHASES:
                self.phase_swa(l, xin)
                k.barrier()
            if self.dbg:
                k.dma(self.dbg_y[l], self.yT, "dbgy", q="pool")
            if "d" in PHASES:
                self.phase_mix(l, xin)
                k.barrier()
            if self.dbg:
                k.dma(self.dbg_x1T[l], self.x1T, "dbgx", q="pool")
            if "v" in PHASES:
                self.phase_peer_conv(l)
                k.barrier()
            if "e" in PHASES:
                self.phase_peer(l, xout)
                k.barrier()
            xin = xout
        deps = {}
        for key in ["out"] + (["dbg_y", "dbg_x1T"] if self.dbg else []):
            for sid, (sh, val) in k.W.get(key, {}).items():
                deps[sid] = (sh, val)
        k._wait("sp", deps)
        return self.nc

    def xT_view(self, xT, t0, n):
        return xT.rearrange("(kc p) t -> p kc t", p=128)[:, :, t0:t0 + n]

    def fmaj(self, l, c, w, bank, xTb, BLK):
        k = self.k
        k.dma(w[:], self.wfb[c].rearrange("p (k e) -> p k e", k=16), "wf" + w.name[-1])
        for kc in range(16):
            k.mm(bank[:, 0:BLK], w[:, kc, :], xTb[:, kc, :], start=(kc == 0), stop=(kc == 15))

    def tmaj(self, l, ti, w, bank, xTb, cs):
        k = self.k
        k.dma(w[:], self.wtb[ti].rearrange("p (k e) -> p k e", k=16), "wt" + w.name[-1])
        for kc in range(16):
            k.mm(bank[:, 0:256], xTb[:, kc, cs], w[:, kc, :], start=(kc == 0), stop=(kc == 15))

    def small_proj(self, wsm, xTb, cs, bank, small, biasb):
        k = self.k
        for kc in range(16):
            k.mm(bank[:, 0:24], xTb[:, kc, cs], wsm[:, kc, :], start=(kc == 0), stop=(kc == 15))
        k.tt(small[:], bank[:, 0:24], biasb, ALU.add)

    def emit_yT(self, ysrc, g, col0, yTc, banks):
        k = self.k
        for c in range(8):
            k.tr(banks[c // 4][:, (c % 4) * 128:(c % 4 + 1) * 128], ysrc[:, c * 128:(c + 1) * 128], self.ident)
        for hh in range(2):
            k.copy(yTc[:, hh * 4:(hh + 1) * 4, :], banks[hh][:, 0:512].rearrange("p (a b) -> p a b", a=4))
        k.dma(self.yT[g, :, col0:col0 + 8, :], yTc[:], "yst", q="pool")

    def phase_wconv(self, l):
        k, nc = self.k, self.nc
        items = []
        for c in range(68):
            items.append((self.wf_d[l, c].rearrange("p k e -> p (k e)"), self.wfb[c], 2048))
        for t in range(20):
            items.append((self.wt_d[l, t].rearrange("p k e -> p (k e)"), self.wtb[t], 4096))
        for kk in range(3):
            for c in range(16):
                items.append((self.wb_d[l, kk, c].rearrange("p k e -> p (k e)"), self.wbb[kk * 16 + c], 1024))
        for c in range(16):
            items.append((self.wo_d[l, c].rearrange("p k e -> p (k e)"), self.wob[c], 2048))
        with ExitStack() as es:
            s32 = [es.enter_context(nc.sbuf_tensor(f"L{l}w_s32{i}", [128, 4096], F32)) for i in range(2)]
            s16 = [es.enter_context(nc.sbuf_tensor(f"L{l}w_s16{i}", [128, 4096], BF16)) for i in range(2)]
            for n, (src, dst, w) in enumerate(items):
                i = n % 2
                k.dma(s32[i][:, 0:w], src, f"cw{i}")
                k.copy(s16[i][:, 0:w], s32[i][:, 0:w], e=("act" if i else "dve"))
                k.dma(dst, s16[i][:, 0:w], f"sw{i}", q="pool")

    def phase_mlstm(self, l, xT):
        k, nc, ps = self.k, self.nc, self.ps
        BLK = self.BLK
        with ExitStack() as es:
            def sb(name, shape):
                return es.enter_context(nc.sbuf_tensor(f"L{l}a_{name}", list(shape), F32))
            xTb = sb("xTb", [128, 16, BLK])
            mqT = sb("mqT", [128, 4, BLK])
            mkT = sb("mkT", [128, 4, BLK])
            wf = [es.enter_context(nc.sbuf_tensor(f"L{l}a_wf{i}", [128, 16, 128], BF16)) for i in range(2)]
            wt = [es.enter_context(nc.sbuf_tensor(f"L{l}a_wt{i}", [128, 16, 256], BF16)) for i in range(2)]
            xTh = es.enter_context(nc.sbuf_tensor(f"L{l}a_xTh", [128, 16, BLK], BF16))
            wsm = sb("wsm", [128, 16, 24])
            mktok = sb("mktok", [128, 512])
            mvext = sb("mvext", [128, 4, 257])
            osig = sb("osig", [128, 1024])
            small = sb("small", [128, 24])
            normw = sb("normw", [128, 1024])
            biasb = sb("biasb", [128, 24])
            Cext = [sb(f"C{h}", [128, 257]) for h in range(4)]
            e1 = sb("e1", [128, 4])
            lf = sb("lf", [128, 4])
            Bmat = sb("Bmat", [128, 4, 128])
            Dt = sb("Dt", [128, 4, 128])
            EB = sb("EB", [128, 4, 128])
            PT = sb("PT", [128, 128])
            qsT = sb("qsT", [128, 128])
            den = sb("den", [128, 1])
            hb = sb("hb", [128, 256])
            st6 = sb("st6", [128, 6])
            mv2 = sb("mv2", [128, 2])
            rstd = sb("rstd", [128, 1])
            kw = sb("kw", [128, 128])
            ym = sb("ym", [128, 1024])
            yTc = sb("yTc", [128, 8, 128])

            k.dma(wsm[:], self.wsm_d[l], "par0")
            k.dma(normw[:], self.prow_d[l, :, PR_MN:PR_MN + 1024].partition_broadcast(128), "par1")
            k.dma(biasb[:], self.prow_d[l, :, PR_BI:PR_BI + 24].partition_broadcast(128), "par2")
            for h in range(4):
                k.memset(Cext[h][:], 0.0)
            k.memset(mvext[:], 1.0)
            for b in range(self.S // BLK):
                t0 = b * BLK
                k.dma(xTb[:], self.xT_view(xT, t0, BLK), "xT")
                k.copy(xTh[:], xTb[:], e="pool")
                for c in range(8):
                    bank = ps[c % 2]
                    self.fmaj(l, c, wf[c % 2], bank, xTh, BLK)
                    if c < 4:
                        k.copy(mqT[:, c, :], bank[:, 0:BLK])
                    else:
                        k.mul(mkT[:, c - 4, :], bank[:, 0:BLK], KSC)
                for j in range(BLK // 128):
                    g = (t0 // 128) + j
                    cs = slice(j * 128, (j + 1) * 128)
                    for ti in range(10):
                        bank = ps[2 + ti % 2]
                        self.tmaj(l, ti, wt[ti % 2], bank, xTh, cs)
                        if ti < 2:
                            k.copy(mktok[:, ti * 256:(ti + 1) * 256], bank[:, 0:256])
                        elif ti < 6:
                            k.copy(mvext[:, ti - 2, 0:256], bank[:, 0:256])
                        else:
                            k.act(osig[:, (ti - 6) * 256:(ti - 5) * 256], bank[:, 0:256], AF.Sigmoid)
                    self.small_proj(wsm, xTb, cs, ps[4], small, biasb[:])
                    k.act(e1[:], small[:, 4:8], AF.Exp, scale=-1.0)
                    k.act(e1[:], e1[:], AF.Ln, bias=1.0)
                    k.mul(lf[:], e1[:], -1.0)
                    k.tt(Bmat[:], bc(lf[:], [128, 4, 128], 2), bc(self.tri, [128, 4, 128], 1), ALU.mult)
                    Bm2 = Bmat[:].rearrange("p h t -> p (h t)")
                    k.mm(ps[5][:, 0:512], self.amat, Bm2, start=True, stop=False)
                    k.mm(ps[5][:, 0:512], self.ident, self.nm4, start=False, stop=True)
                    k.mm(ps[6][:, 0:512], self.ones, Bm2)
                    for h in range(4):
                        k.act(Dt[:, h, :], ps[5][:, h * 128:(h + 1) * 128], AF.Exp, bias=small[:, h:h + 1])
                    k.act(EB[:].rearrange("p h t -> p (h t)"), ps[6][:, 0:512], AF.Exp)
                    for h in range(4):
                        k.mm(ps[7][:, 0:128], mkT[:, h, cs], mqT[:, h, cs])
                        k.tt(PT[:], ps[7][:, 0:128], Dt[:, h, :], ALU.mult)
                        k.tt(qsT[:], mqT[:, h, cs], EB[:, h, :], ALU.mult, e="pool")
                        nm = ps[h % 2]
                        k.mm(nm[:, 0:257], PT[:], mvext[:, h, :], start=True, stop=False)
                        k.mm(nm[:, 0:257], qsT[:], Cext[h][:], start=False, stop=True)
                        k.act(den[:], nm[:, 256:257], AF.Abs)
                        k.ts(den[:], den[:], 1.0, None, ALU.max)
                        k.recip(den[:], den[:])
                        k.ts(hb[:], nm[:, 0:256], den[:], None, ALU.mult)
                        k.op("dve", "bn_stats", st6[:], hb[:], R=[hb], W=[st6])
                        k.op("dve", "bn_aggr", mv2[:], st6[:], R=[st6], W=[mv2])
                        k.act(rstd[:], mv2[:, 1:2], AF.Sqrt, bias=1e-6)
                        k.recip(rstd[:], rstd[:])
                        k.ts(hb[:], hb[:], mv2[:, 0:1], rstd[:], ALU.subtract, ALU.mult)
                        k.tt(hb[:], hb[:], normw[:, h * 256:(h + 1) * 256], ALU.mult)
                        k.tt(ym[:, h * 256:(h + 1) * 256], hb[:], osig[:, h * 256:(h + 1) * 256], ALU.mult, e="pool")
                        k.ts(kw[:], mktok[:, h * 128:(h + 1) * 128], Dt[:, h, 127:128], KSC, ALU.mult, ALU.mult)
                        cu = ps[2 + h % 2]
                        k.mm(cu[:, 0:257], kw[:], mvext[:, h, :])
                        k.stt(Cext[h][:], Cext[h][:], EB[:, h, 127:128], cu[:, 0:257], ALU.mult, ALU.add)
                    self.emit_yT(ym, g, 0, yTc, [ps[4], ps[5]])

    def phase_ssd(self, l, xT):
        k, nc, ps = self.k, self.nc, self.ps
        BLK = self.BLK
        with ExitStack() as es:
            def sb(name, shape):
                return es.enter_context(nc.sbuf_tensor(f"L{l}b_{name}", list(shape), F32))
            xTb = sb("xTb", [128, 16, BLK])
            xbcT = sb("xbcT", [128, 12, BLK])
            u = sb("u", [128, BLK + 3])
            halo = sb("halo", [128, 12, 3])
            wf = [es.enter_context(nc.sbuf_tensor(f"L{l}b_wf{i}", [128, 16, 128], BF16)) for i in range(2)]
            wt = [es.enter_context(nc.sbuf_tensor(f"L{l}b_wt{i}", [128, 16, 256], BF16)) for i in range(2)]
            xTh = es.enter_context(nc.sbuf_tensor(f"L{l}b_xTh", [128, 16, BLK], BF16))
            wsm = sb("wsm", [128, 16, 24])
            pcol = sb("pcol", [128, NPCOL])
            prw = sb("prw", [128, 1024 + 48])
            biasb = sb("biasb", [128, 24])
            zsil = sb("zsil", [128, 1024])
            small = sb("small", [128, 24])
            dt = sb("dt", [128, 16])
            aneg = sb("aneg", [128, 16])
            dtA = sb("dtA", [128, 16])
            ex = sb("ex", [128, 48])
            Bm = sb("Bm", [128, 16, 128])
            dec = sb("dec", [128, 16, 128])
            MT = sb("MT", [128, 16, 128])
            xstok = sb("xstok", [128, 16, 64])
            Btok = sb("Btok", [128, 2, 128])
            xdt = sb("xdt", [128, 16, 64])
            xw = sb("xw", [128, 16, 64])
            t1 = sb("t1", [128, 16, 64])
            t3 = sb("t3", [128, 16, 64])
            Dfull = sb("Dfull", [128, 16, 64])
            H = [sb("H0", [128, 8, 64]), sb("H1", [128, 8, 64])]
            ss = sb("ss", [128, 2])
            junk = sb("junk", [128, 512])
            ys = sb("ys", [128, 1024])
            yTc = sb("yTc", [128, 8, 128])

            k.dma(wsm[:], self.wsm_d[l], "par0")
            k.dma(pcol[:], self.pcol_d[l], "par1")
            k.dma(prw[:], self.prow_d[l, :, PR_SN:PR_SN + 1072].partition_broadcast(128), "par2")
            k.dma(biasb[:], self.prow_d[l, :, PR_BI:PR_BI + 24].partition_broadcast(128), "par3")
            snw = prw[:, 0:1024]
            k.act(aneg[:], prw[:, 1024:1040], AF.Exp)
            k.mul(aneg[:], aneg[:], -1.0)
            k.copy(Dfull[:], bc(prw[:, 1040:1056], [128, 16, 64], 2), e="dve")
            k.memset(halo[:], 0.0)
            k.memset(H[0][:], 0.0)
            k.memset(H[1][:], 0.0)
            for b in range(self.S // BLK):
                t0 = b * BLK
                k.dma(xTb[:], self.xT_view(xT, t0, BLK), "xT")
                k.copy(xTh[:], xTb[:], e="pool")
                for c in range(12):
                    bank = ps[c % 2]
                    self.fmaj(l, 8 + c, wf[c % 2], bank, xTh, BLK)
                    k.copy(u[:, 0:3], halo[:, c, :], e="pool")
                    k.copy(u[:, 3:3 + BLK], bank[:, 0:BLK])
                    k.copy(halo[:, c, :], u[:, BLK:BLK + 3], e="pool")
                    acc = xbcT[:, c, :]
                    k.ts(acc, u[:, 0:BLK], pcol[:, PC_CW + c * 4:PC_CW + c * 4 + 1], pcol[:, PC_CB + c:PC_CB + c + 1],
                         ALU.mult, ALU.add)
                    for kk in range(1, 4):
                        k.stt(acc, u[:, kk:kk + BLK], pcol[:, PC_CW + c * 4 + kk:PC_CW + c * 4 + kk + 1], acc,
                              ALU.mult, ALU.add)
                    k.act(acc, acc, AF.Silu)
                for j in range(BLK // 128):
                    g = (t0 // 128) + j
                    cs = slice(j * 128, (j + 1) * 128)
                    for ti in range(10, 14):
                        bank = ps[2 + ti % 2]
                        self.tmaj(l, ti, wt[ti % 2], bank, xTh, cs)
                        k.act(zsil[:, (ti - 10) * 256:(ti - 9) * 256], bank[:, 0:256], AF.Silu)
                    self.small_proj(wsm, xTb, cs, ps[4], small, biasb[:])
                    k.act(dt[:], small[:, 8:24], AF.Exp)
                    k.act(dt[:], dt[:], AF.Ln, bias=1.0)
                    k.tt(dtA[:], dt[:], aneg[:], ALU.mult)
                    X = ps[4]
                    k.mm(X[:, 0:16], self.tri, dtA[:])
                    k.mm(X[:, 16:32], self.amat, dtA[:])
                    k.mm(X[:, 32:48], self.ones, dtA[:])
                    k.act(ex[:], X[:, 0:48], AF.Exp)
                    eacum, erev, ecd = ex[:, 0:16], ex[:, 16:32], ex[:, 32:48]
                    k.tt(Bm[:], bc(dtA[:], [128, 16, 128], 2), bc(self.tri, [128, 16, 128], 1), ALU.mult)
                    for q in range(4):
                        bank = ps[5 + q % 2]
                        k.mm(bank[:, 0:512], self.amat, Bm[:, 4 * q:4 * q + 4, :].rearrange("p h t -> p (h t)"),
                             start=True, stop=False)
                        k.mm(bank[:, 0:512], self.ident, self.nm4, start=False, stop=True)
                        k.act(dec[:, 4 * q:4 * q + 4, :].rearrange("p h t -> p (h t)"), bank[:, 0:512], AF.Exp)
                    for c in range(8):
                        k.tr(ps[c // 4][:, (c % 4) * 128:(c % 4 + 1) * 128], xbcT[:, c, cs], self.ident)
                    xs2 = xstok[:].rearrange("p h d -> p (h d)")
                    k.copy(xs2[:, 0:512], ps[0][:, 0:512])
                    k.copy(xs2[:, 512:1024], ps[1][:, 0:512])
                    for gg in range(2):
                        k.tr(ps[7][:, gg * 128:(gg + 1) * 128], xbcT[:, 8 + gg, cs], self.ident)
                    k.copy(Btok[:].rearrange("p g n -> p (g n)"), ps[7][:, 0:256])
                    k.tt(xdt[:], xstok[:], bc(dt[:], [128, 16, 64], 2), ALU.mult)
                    k.tt(xw[:], xdt[:], bc(erev, [128, 16, 64], 2), ALU.mult, e="pool")
                    for gg in range(2):
                        k.mm(ps[7][:, 256 + gg * 128:256 + (gg + 1) * 128], xbcT[:, 8 + gg, cs], xbcT[:, 10 + gg, cs])
                    for gg in range(2):
                        k.tt(MT[:, gg * 8:(gg + 1) * 8, :], dec[:, gg * 8:(gg + 1) * 8, :],
                             bc(ps[7][:, 256 + gg * 128:256 + (gg + 1) * 128], [128, 8, 128], 1), ALU.mult)
                    for gg in range(2):
                        yd = ps[gg]
                        yo = ps[2 + gg]
                        for hh in range(8):
                            h = gg * 8 + hh
                            k.mm(yd[:, hh * 64:(hh + 1) * 64], MT[:, h, :], xdt[:, h, :])
                        k.mm(yo[:, 0:512], xbcT[:, 10 + gg, cs], H[gg][:].rearrange("p h d -> p (h d)"))
                        t1g = t1[:, gg * 8:(gg + 1) * 8, :]
                        k.tt(t1g, yo[:, 0:512].rearrange("p (h d) -> p h d", h=8),
                             bc(eacum[:, gg * 8:(gg + 1) * 8], [128, 8, 64], 2), ALU.mult)
                        k.tt(t1g, t1g, yd[:, 0:512].rearrange("p (h d) -> p h d", h=8), ALU.add)
                    k.tt(t3[:], xstok[:], Dfull[:], ALU.mult, e="pool")
                    k.tt(t1[:], t1[:], t3[:], ALU.add)
                    t12 = t1[:].rearrange("p h d -> p (h d)")
                    k.tt(t12, t12, zsil[:], ALU.mult)
                    for gg in range(2):
                        k.act(junk[:], t12[:, gg * 512:(gg + 1) * 512], AF.Square, accum_out=ss[:, gg:gg + 1])
                    k.act(ss[:], ss[:], AF.Sqrt, bias=1e-6, scale=1.0 / 512)
                    k.recip(ss[:], ss[:])
                    for gg in range(2):
                        k.stt(ys[:, gg * 512:(gg + 1) * 512], t12[:, gg * 512:(gg + 1) * 512], ss[:, gg:gg + 1],
                              snw[:, gg * 512:(gg + 1) * 512], ALU.mult, ALU.mult)
                    for gg in range(2):
                        sp_ = ps[5 + gg]
                        k.mm(sp_[:, 0:512], Btok[:, gg, :], xw[:, gg * 8:(gg + 1) * 8, :].rearrange("p h d -> p (h d)"))
                        k.tt(H[gg][:], H[gg][:], bc(ecd[:, gg * 8:(gg + 1) * 8], [128, 8, 64], 2), ALU.mult)
                        k.tt(H[gg][:], H[gg][:], sp_[:, 0:512].rearrange("p (h d) -> p h d", h=8), ALU.add)
                    self.emit_yT(ys, g, 8, yTc, [ps[0], ps[1]])

    def phase_swa(self, l, xT):
        k, nc, ps = self.k, self.nc, self.ps
        BLK = self.BLK
        with ExitStack() as es:
            def sb(name, shape):
                return es.enter_context(nc.sbuf_tensor(f"L{l}c_{name}", list(shape), F32))
            xTb = sb("xTb", [128, 16, BLK])
            wt = [es.enter_context(nc.sbuf_tensor(f"L{l}c_wt{i}", [128, 16, 256], BF16)) for i in range(2)]
            xTh = es.enter_context(nc.sbuf_tensor(f"L{l}c_xTh", [128, 16, BLK], BF16))
            sinks = sb("sinks", [128, 16])
            aq = sb("aq", [128, 16, 64])
            ak = sb("ak", [128, 4, 64])
            av = [sb("av0", [128, 256]), sb("av1", [128, 256])]
            cs_t = sb("cs", [128, 64])
            ta = sb("ta", [128, 16, 32])
            tb = sb("tb", [128, 16, 32])
            aqr = sb("aqr", [128, 16, 64])
            akd = sb("akd", [128, 4, 2, 64])
            qT = sb("qT", [128, 8, 128])
            kT = [sb("kT0", [128, 4, 128]), sb("kT1", [128, 4, 128])]
            Sm = sb("Sm", [128, 256])
            mx = sb("mx", [128, 1])
            negm = sb("negm", [128, 1])
            p = sb("p", [128, 256])
            rsum = sb("rsum", [128, 1])
            esk = sb("esk", [128, 1])
            den = sb("den", [128, 1])
            pT = sb("pT", [128, 256])
            ya = sb("ya", [128, 1024])
            yTc = sb("yTc", [128, 8, 128])

            k.dma(sinks[:], self.prow_d[l, :, PR_SK:PR_SK + 16].partition_broadcast(128), "par0")
            k.memset(kT[1][:], 0.0)
            k.memset(av[1][:], 0.0)
            for b in range(self.S // BLK):
                t0 = b * BLK
                k.dma(xTb[:], self.xT_view(xT, t0, BLK), "xT")
                k.copy(xTh[:], xTb[:], e="pool")
                for j in range(BLK // 128):
                    g = (t0 // 128) + j
                    cur, prv = g % 2, (g + 1) % 2
                    cs = slice(j * 128, (j + 1) * 128)
                    k.dma(cs_t[:], self.rope_d[g * 128:(g + 1) * 128, :], "rope")
                    aq2 = aq[:].rearrange("p h d -> p (h d)")
                    for ti in range(14, 20):
                        bank = ps[ti % 2]
                        self.tmaj(l, ti, wt[ti % 2], bank, xTh, cs)
                        if ti < 18:
                            k.copy(aq2[:, (ti - 14) * 256:(ti - 13) * 256], bank[:, 0:256])
                        elif ti == 18:
                            k.copy(ak[:].rearrange("p h d -> p (h d)"), bank[:, 0:256])
                        else:
                            k.copy(av[cur][:], bank[:, 0:256])
                    for (src, dst, nh) in ((aq, aqr[:], 16), (ak, akd[:, :, 0, :], 4)):
                        x1, x2 = src[:, :, 0:32], src[:, :, 32:64]
                        cb = bc(cs_t[:, 0:32], [128, nh, 32], 1)
                        sn = bc(cs_t[:, 32:64], [128, nh, 32], 1)
                        k.tt(ta[:, 0:nh, :], x1, cb, ALU.mult)
                        k.tt(tb[:, 0:nh, :], x2, sn, ALU.mult, e="pool")
                        k.tt(dst[:, :, 0:32], ta[:, 0:nh, :], tb[:, 0:nh, :], ALU.subtract)
                        k.tt(ta[:, 0:nh, :], x2, cb, ALU.mult)
                        k.tt(tb[:, 0:nh, :], x1, sn, ALU.mult, e="pool")
                        k.tt(dst[:, :, 32:64], ta[:, 0:nh, :], tb[:, 0:nh, :], ALU.add)
                    k.copy(akd[:, :, 1, :], akd[:, :, 0, :], e="pool")
                    aqr2 = aqr[:].rearrange("p h d -> p (h d)")
                    for m in range(8):
                        k.tr(ps[2 + m // 4][:, (m % 4) * 128:(m % 4 + 1) * 128], aqr2[:, m * 128:(m + 1) * 128], self.ident)
                    for hh in range(2):
                        k.copy(qT[:, hh * 4:(hh + 1) * 4, :], ps[2 + hh][:, 0:512].rearrange("p (a b) -> p a b", a=4))
                    for gg in range(4):
                        k.tr(ps[4][:, gg * 128:(gg + 1) * 128], akd[:, gg, :, :].rearrange("p a d -> p (a d)"), self.ident)
                    k.copy(kT[cur][:], ps[4][:, 0:512].rearrange("p (a b) -> p a b", a=4))
                    mask = self.mask0 if g == 0 else self.maskA
                    for h in range(16):
                        gg, base, m = h // 4, (h % 2) * 64, h // 2
                        sbk = ps[5 + h % 2]
                        k.mm(sbk[:, 0:128], qT[base:base + 64, m, :], kT[prv][base:base + 64, gg, :])
                        k.mm(sbk[:, 128:256], qT[base:base + 64, m, :], kT[cur][base:base + 64, gg, :])
                        k.stt(Sm[:], sbk[:, 0:256], 0.125, mask, ALU.mult, ALU.add)
                        k.op("dve", "reduce_max", mx[:], Sm[:], axis=AX.X, R=[Sm], W=[mx])
                        k.ts(negm[:], mx[:], sinks[:, h:h + 1], -1.0, ALU.max, ALU.mult)
                        k.act(p[:], Sm[:], AF.Exp, bias=negm[:], accum_out=rsum[:])
                        k.act(esk[:], sinks[:, h:h + 1], AF.Exp, bias=negm[:])
                        k.tt(den[:], rsum[:], esk[:], ALU.add)
                        k.recip(den[:], den[:])
                        pb = ps[h % 2]
                        k.tr(pb[:, 0:128], p[:, 0:128], self.ident)
                        k.tr(pb[:, 128:256], p[:, 128:256], self.ident)
                        k.copy(pT[:], pb[:, 0:256])
                        ob = ps[7]
                        k.mm(ob[:, 0:64], pT[:, 0:128], av[prv][:, gg * 64:(gg + 1) * 64], start=True, stop=False)
                        k.mm(ob[:, 0:64], pT[:, 128:256], av[cur][:, gg * 64:(gg + 1) * 64], start=False, stop=True)
                        k.ts(ya[:, h * 64:(h + 1) * 64], ob[:, 0:64], den[:], None, ALU.mult)
                    self.emit_yT(ya, g, 16, yTc, [ps[2], ps[3]])

    def phase_mix(self, l, xT):
        k, nc, ps = self.k, self.nc, self.ps
        BLK = self.BLK
        with ExitStack() as es:
            def sb(name, shape, dt=F32):
                return es.enter_context(nc.sbuf_tensor(f"L{l}d_{name}", list(shape), dt))
            xTb = sb("xTb", [128, 16, BLK])
            xTh = sb("xTh", [128, 16, BLK], BF16)
            ytmp = [sb("ytmp0", [128, 24, 128]), sb("ytmp1", [128, 24, 128])]
            yTh = sb("yTh", [128, 24, BLK], BF16)
            mixT = sb("mixT", [128, 16, BLK], BF16)
            wg = [sb("wg0", [128, 16, 128], BF16), sb("wg1", [128, 16, 128], BF16)]
            wb = [sb("wb0", [128, 8, 128], BF16), sb("wb1", [128, 8, 128], BF16)]
            wo = [sb("wo0", [128, 16, 128], BF16), sb("wo1", [128, 16, 128], BF16)]
            pcol = sb("pcol", [128, NPCOL])
            gsb = [sb("gsb0", [128, BLK]), sb("gsb1", [128, BLK])]
            tmp = [sb("tmp0", [128, BLK]), sb("tmp1", [128, BLK])]
            acc = [sb("acc0", [128, BLK]), sb("acc1", [128, BLK])]
            sq = [sb("sq0", [128, BLK]), sb("sq1", [128, BLK])]
            rstd = sb("rstd", [128, BLK])
            k.dma(pcol[:], self.pcol_d[l], "par0")
            n = 0
            for b in range(self.S // BLK):
                t0 = b * BLK
                k.dma(xTb[:], self.xT_view(xT, t0, BLK), "xT")
                k.copy(xTh[:], xTb[:], e="pool")
                for j in range(BLK // 128):
                    yt = ytmp[j % 2]
                    k.dma(yt[:], self.yT[t0 // 128 + j], "yT" + yt.name[-1])
                    k.copy(yTh[:, :, j * 128:(j + 1) * 128], yt[:], e=("act" if j % 2 else "dve"))
                for c in range(16):
                    ac = acc[c % 2]
                    for kk in range(3):
                        n += 1
                        w = wg[n % 2]
                        k.dma(w[:], self.wfb[20 + kk * 16 + c].rearrange("p (k e) -> p k e", k=16), "wg" + w.name[-1])
                        gb = ps[n % 2]
                        for kc in range(16):
                            k.mm(gb[:, 0:BLK], w[:, kc, :], xTh[:, kc, :], start=(kc == 0), stop=(kc == 15))
                        gs = gsb[n % 2]
                        k.act(gs[:], gb[:, 0:BLK], AF.Sigmoid, bias=pcol[:, PC_GB + kk * 16 + c:PC_GB + kk * 16 + c + 1])
                        w2 = wb[n % 2]
                        k.dma(w2[:], self.wbb[kk * 16 + c].rearrange("p (k e) -> p k e", k=8), "wb" + w2.name[-1])
                        bb = ps[2 + n % 2]
                        for kc in range(8):
                            k.mm(bb[:, 0:BLK], w2[:, kc, :], yTh[:, kk * 8 + kc, :], start=(kc == 0), stop=(kc == 7))
                        if kk == 0:
                            k.tt(ac[:], gs[:], bb[:, 0:BLK], ALU.mult)
                        elif kk == 1:
                            tp = tmp[n % 2]
                            k.tt(tp[:], gs[:], bb[:, 0:BLK], ALU.mult)
                            k.tt(ac[:], ac[:], tp[:], ALU.add, e="pool")
                        else:
                            tp = tmp[n % 2]
                            k.tt(tp[:], gs[:], bb[:, 0:BLK], ALU.mult)
                            k.tt(mixT[:, c, :], ac[:], tp[:], ALU.add, e="pool")
                for c in range(16):
                    w = wo[c % 2]
                    k.dma(w[:], self.wob[c].rearrange("p (k e) -> p k e", k=16), "wo" + w.name[-1])
                    ob = ps[4 + c % 2]
                    for kc in range(16):
                        k.mm(ob[:, 0:BLK], w[:, kc, :], mixT[:, kc, :], start=(kc == 0), stop=(kc == 15))
                    k.stt(xTb[:, c, :], xTb[:, c, :], ALPHA, ob[:, 0:BLK], ALU.mult, ALU.add)
                mb, vb = ps[6], ps[7]
                for c in range(16):
                    k.mm(mb[:, 0:BLK], self.onesD, xTb[:, c, :], start=(c == 0), stop=(c == 15))
                for c in range(16):
                    k.tt(xTb[:, c, :], xTb[:, c, :], mb[:, 0:BLK], ALU.subtract)
                    s_ = sq[c % 2]
                    k.act(s_[:], xTb[:, c, :], AF.Square)
                    k.mm(vb[:, 0:BLK], self.onesD, s_[:], start=(c == 0), stop=(c == 15))
                k.act(rstd[:], vb[:, 0:BLK], AF.Sqrt, bias=1e-5)
                k.recip(rstd[:], rstd[:])
                for c in range(16):
                    k.tt(xTb[:, c, :], xTb[:, c, :], rstd[:], ALU.mult)
                    k.ts(xTb[:, c, :], xTb[:, c, :], pcol[:, PC_L1G + c:PC_L1G + c + 1], pcol[:, PC_L1B + c:PC_L1B + c + 1],
                         ALU.mult, ALU.add, e="pool")
                k.dma(self.xT_view(self.x1T, t0, BLK), xTb[:], "x1st", q="pool")

    def phase_peer_conv(self, l):
        k, nc = self.k, self.nc
        with ExitStack() as es:
            def sb(name, shape, dt=F32):
                return es.enter_context(nc.sbuf_tensor(f"L{l}v_{name}", list(shape), dt))
            u32 = [sb("u32a", [128, 2048]), sb("u32b", [128, 2048])]
            v32 = [sb("v32a", [128, 2048]), sb("v32b", [128, 2048])]
            u16 = [sb("u16a", [128, 2048], BF16), sb("u16b", [128, 2048], BF16)]
            v16 = [sb("v16a", [128, 2048], BF16), sb("v16b", [128, 2048], BF16)]
            for ec in range(128):
                i = ec % 2
                k.dma(u32[i][:], self.uT_d[l, ec].rearrange("p k e -> p (k e)"), f"cu{i}")
                k.copy(u16[i][:], u32[i][:], e="act")
                k.dma(self.uTb[ec], u16[i][:], f"su{i}", q="pool")
                k.dma(v32[i][:], self.v_d[l, ec], f"cv{i}")
                k.copy(v16[i][:], v32[i][:], e="dve")
                k.dma(self.vbf[ec], v16[i][:], f"sv{i}", q="pool")

    def phase_peer(self, l, xTout):
        k, nc, ps = self.k, self.nc, self.ps
        TG = min(256, self.S)
        NT = TG // 128
        with ExitStack() as es:
            def sb(name, shape, dt=F32):
                return es.enter_context(nc.sbuf_tensor(f"L{l}e_{name}", list(shape), dt))
            skT = sb("skT", [128, 16, 128])
            ln2 = sb("ln2", [128, 4096])
            sall = [sb(f"sall{t}", [128, 16, 128]) for t in range(NT)]
            x1tok = [sb(f"x1tok{t}", [128, 2048]) for t in range(NT)]
            tau = [sb(f"tau{t}", [128, 8]) for t in range(NT)]
            biasE = [sb(f"biasE{t}", [128, 8]) for t in range(NT)]
            xtb = sb("xtb", [128, 16, TG], BF16)
            HT = [sb(f"HT{e}", [128, TG], BF16) for e in range(128)]
            st = sb("st", [128, 24])
            mv2 = sb("mv2", [128, 2])
            rstd = sb("rstd", [128, 1])
            k.dma(skT[:], self.skT_d[l], "par0")
            k.dma(ln2[:], self.prow_d[l, :, PR_L2G:PR_L2G + 4096].partition_broadcast(128), "par1")
            for grp in range(self.S // TG):
                t0 = grp * TG
                with ExitStack() as e1:
                    def s1(name, shape, dt=F32):
                        return e1.enter_context(nc.sbuf_tensor(f"L{l}e{grp}p_{name}", list(shape), dt))
                    xt = s1("xt", [128, 16, TG])
                    qT = s1("qT", [128, 16, TG])
                    wq = [s1("wq0", [128, 16, 128]), s1("wq1", [128, 16, 128])]
                    top = s1("top", [128, 2, 16])
                    tmp = s1("tmp", [128, 128])
                    cand = s1("cand", [128, 256])
                    tmp2 = s1("tmp2", [128, 256])
                    best = s1("best", [128, 16])
                    negmx = s1("negmx", [128, 1])
                    j16 = s1("j16", [128, 16])
                    Z = s1("Z", [128, 1])
                    k.dma(xt[:], self.xT_view(self.x1T, t0, TG), "xT")
                    k.copy(xtb[:], xt[:], e="pool")
                    for c in range(16):
                        w = wq[c % 2]
                        k.dma(w[:], self.wq_d[l, c], "wq" + w.name[-1])
                        bank = ps[4 + c % 2]
                        for kc in range(16):
                            k.mm(bank[:, 0:TG], w[:, kc, :], xt[:, kc, :], start=(kc == 0), stop=(kc == 15))
                        k.copy(qT[:, c, :], bank[:, 0:TG])
                    for t in range(NT):
                        ts_ = slice(t * 128, (t + 1) * 128)
                        for c in range(16):
                            k.mm(ps[c // 4][:, (c % 4) * 128:(c % 4 + 1) * 128], qT[:, c, ts_], skT[:, c, :])
                        for q in range(4):
                            k.copy(sall[t][:, q * 4:(q + 1) * 4, :], ps[q][:, 0:512].rearrange("p (a b) -> p a b", a=4))
                        for c in range(16):
                            k.tr(ps[4 + (c // 4) % 2][:, (c % 4) * 128:(c % 4 + 1) * 128], xt[:, c, ts_], self.ident)
                            if c % 4 == 3:
                                q = c // 4
                                k.copy(x1tok[t][:, q * 512:(q + 1) * 512], ps[4 + q % 2][:, 0:512], e="dve")
                        for h in range(8):
                            for half in range(2):
                                sv = sall[t][:, 2 * h + half, :]
                                k.op("dve", "max", top[:, half, 0:8], sv, R=[sall[t]], W=[top])
                                k.op("dve", "match_replace", tmp[:], top[:, half, 0:8], sv, -1e30, R=[top, sall[t]], W=[tmp])
                                k.op("dve", "max", top[:, half, 8:16], tmp[:], R=[tmp], W=[top])
                            k.tt(cand[:].rearrange("p (a b) -> p a b", a=16), bc(top[:, 0, :], [128, 16, 16], 2),
                                 bc(top[:, 1, :], [128, 16, 16], 1), ALU.add)
                            k.op("dve", "max", best[:, 0:8], cand[:], R=[cand], W=[best])
                            k.op("dve", "match_replace", tmp2[:], best[:, 0:8], cand[:], -1e30, R=[best, cand], W=[tmp2])
                            k.op("dve", "max", best[:, 8:16], tmp2[:], R=[tmp2], W=[best])
                            k.ts(negmx[:], best[:, 0:1], -1.0, None, ALU.mult)
                            k.act(j16[:], best[:], AF.Exp, bias=negmx[:], accum_out=Z[:])
                            k.act(Z[:], Z[:], AF.Ln)
                            k.tt(biasE[t][:, h:h + 1], negmx[:], Z[:], ALU.subtract)
                            k.copy(tau[t][:, h:h + 1], best[:, 15:16], e="dve")
                k.barrier()
                with ExitStack() as e2:
                    def s2(name, shape, dt=F32):
                        return e2.enter_context(nc.sbuf_tensor(f"L{l}e{grp}s_{name}", list(shape), dt))
                    Lb = [s2(f"Lb{i}", [128, 8, 128]) for i in range(3)]
                    Eb = [s2(f"Eb{i}", [128, 8, 128]) for i in range(3)]
                    Mb = [s2("Mb0", [128, 8, 128]), s2("Mb1", [128, 8, 128])]
                    Gacc = [[s2(f"G{t}a", [128, 8, 128]), s2(f"G{t}b", [128, 8, 128])] for t in range(NT)]
                    ut = [s2(f"ut{i}", [128, 16, 128], BF16) for i in range(3)]
                    vh = [s2(f"vh{i}", [128, 1024], BF16) for i in range(3)]
                    ga = [s2("ga0", [128, TG]), s2("ga1", [128, TG])]
                    ne = 0
                    pend = None

                    def emit_out(ec, v_):
                        for t in range(NT):
                            for q in range(2):
                                k.mm(ps[t * 2 + q][:, 0:512], HT[ec][:, t * 128:(t + 1) * 128], v_[:, q * 512:(q + 1) * 512],
                                     start=(ec == 0), stop=(ec == 127))

                    def gunit(ib, u):
                        t, h = u // 8, u % 8
                        G = Gacc[t][ib % 2]
                        nbl[0] += 1
                        nb = nbl[0]
                        Lh, Eh, Mh = Lb[nb % 3], Eb[nb % 3], Mb[nb % 2]
                        k.tt(Lh[:], bc(sall[t][:, 2 * h, ib * 8:(ib + 1) * 8], [128, 8, 128], 2),
                             bc(sall[t][:, 2 * h + 1, :], [128, 8, 128], 1), ALU.add, e="pool")
                        k.act(Eh[:], Lh[:], AF.Exp, bias=biasE[t][:, h:h + 1])
                        if h == 0:
                            k.stt(G[:], Lh[:], tau[t][:, h:h + 1], Eh[:], ALU.is_ge, ALU.mult)
                        else:
                            k.stt(Mh[:], Lh[:], tau[t][:, h:h + 1], Eh[:], ALU.is_ge, ALU.mult)
                            k.tt(G[:], G[:], Mh[:], ALU.add, e="dve")

                    nbl = [0]
                    NU = NT * 8
                    for u in range(NU):
                        gunit(0, u)
                    for ib in range(16):
                        for i in range(8):
                            if ib + 1 < 16:
                                for u in range(i * NU // 8, (i + 1) * NU // 8):
                                    gunit(ib + 1, u)
                            ec = ib * 8 + i
                            ne += 1
                            u_, v_ = ut[ne % 3], vh[ne % 3]
                            k.dma(u_[:], self.uTb[ec].rearrange("p (k e) -> p k e", k=16), "ut" + u_.name[-1])
                            k.dma(v_[:], self.vbf[ec][:, 0:1024], "vh" + v_.name[-1])
                            gb = ps[6 + ec % 2]
                            for t in range(NT):
                                k.tr(gb[:, t * 128:(t + 1) * 128], Gacc[t][ib % 2][:, i, :], self.ident)
                            ab = ps[4 + ec % 2]
                            for kc in range(16):
                                k.mm(ab[:, 0:TG], u_[:, kc, :], xtb[:, kc, :], start=(kc == 0), stop=(kc == 15))
                            g_ = ga[ec % 2]
                            k.act(g_[:], ab[:, 0:TG], AF.Gelu)
                            k.tt(HT[ec][:], g_[:], gb[:, 0:TG], ALU.mult)
                            if pend is not None:
                                emit_out(*pend)
                            pend = (ec, v_)
                    emit_out(*pend)
                    for ec in range(128):
                        ne += 1
                        v_ = vh[ne % 3]
                        k.dma(v_[:], self.vbf[ec][:, 1024:2048], "vh" + v_.name[-1])
                        for t in range(NT):
                            for q in range(2):
                                k.mm(ps[4 + t * 2 + q][:, 0:512], HT[ec][:, t * 128:(t + 1) * 128],
                                     v_[:, q * 512:(q + 1) * 512], start=(ec == 0), stop=(ec == 127))
                k.barrier()
                with ExitStack() as e3:
                    xTn = None
                    if xTout is not None:
                        xTn = e3.enter_context(nc.sbuf_tensor(f"L{l}e{grp}x_xTn", [128, 16, TG], F32))
                    for t in range(NT):
                        xk = x1tok[t]
                        for q in range(4):
                            bank = ps[t * 2 + q] if q < 2 else ps[4 + t * 2 + (q - 2)]
                            k.stt(xk[:, q * 512:(q + 1) * 512], xk[:, q * 512:(q + 1) * 512], ALPHA, bank[:, 0:512],
                                  ALU.mult, ALU.add)
                            k.op("dve", "bn_stats", st[:, q * 6:(q + 1) * 6], xk[:, q * 512:(q + 1) * 512], R=[xk], W=[st])
                        k.op("dve", "bn_aggr", mv2[:], st[:], R=[st], W=[mv2])
                        k.act(rstd[:], mv2[:, 1:2], AF.Sqrt, bias=1e-5)
                        k.recip(rstd[:], rstd[:])
                        k.ts(xk[:], xk[:], mv2[:, 0:1], rstd[:], ALU.subtract, ALU.mult)
                        k.tt(xk[:], xk[:], ln2[:, 0:2048], ALU.mult)
                        k.tt(xk[:], xk[:], ln2[:, 2048:4096], ALU.add, e="pool")
                        if xTout is None:
                            k.dma(self.out_d[t0 + t * 128:t0 + (t + 1) * 128, :], xk[:], "ost", q="sp")
                        else:
                            for c in range(16):
                                k.tr(ps[(c // 4) % 2][:, (c % 4) * 128:(c % 4 + 1) * 128], xk[:, c * 128:(c + 1) * 128],
                                     self.ident)
                                if c % 4 == 3:
                                    q = c // 4
                                    k.copy(xTn[:, q * 4:(q + 1) * 4, t * 128:(t + 1) * 128],
                                           ps[q % 2][:, 0:512].rearrange("p (a b) -> p a b", a=4))
                    if xTout is not None:
                        k.dma(self.xT_view(xTout, t0, TG), xTn[:], "ost", q="sp")
                k.barrier()


def make_consts():
    c = np.zeros((128, NCONST), np.float32)
    r = np.arange(128)
    c[:, C_ID:C_ID + 128] = np.eye(128)
    c[:, C_TRI:C_TRI + 128] = (r[:, None] <= r[None, :])
    c[:, C_AM:C_AM + 128] = (r[:, None] > r[None, :])
    c[:, C_ONE:C_ONE + 128] = 1.0
    c[:, C_OND:C_OND + 128] = 1.0 / D
    nm = np.where(r[:, None] > r[None, :], NEG, 0.0)
    c[:, C_NM4:C_NM4 + 512] = np.tile(nm, (1, 4))
    prevm = np.where(r[None, :] > r[:, None], 0.0, NEG)
    curm = np.where(r[None, :] <= r[:, None], 0.0, NEG)
    c[:, C_MA:C_MA + 256] = np.concatenate([prevm, curm], 1)
    c[:, C_M0:C_M0 + 256] = np.concatenate([np.full((128, 128), NEG), curm], 1)
    return c


def make_rope(S):
    half = 32
    freqs = (np.float32(10000.0) ** (-np.arange(half, dtype=np.float32) / np.float32(half))).astype(np.float32)
    ang = np.arange(S, dtype=np.float32)[:, None] * freqs[None, :]
    return np.concatenate([np.cos(ang), np.sin(ang)], 1).astype(np.float32)


def tile_w(w, cols, tw):
    ws = w[:, cols]
    n = ws.shape[1] // tw
    return np.ascontiguousarray(ws.reshape(16, 128, n, tw).transpose(2, 1, 0, 3))


def prep_weights(inp, NL):
    f = lambda a: np.asarray(a, dtype=np.float32)
    w_in = inp["w_in"]
    out = {}
    out["wf"] = np.stack([tile_w(f(w_in[l]), FCOLS, 128) for l in range(NL)])
    out["wt"] = np.stack([tile_w(f(w_in[l]), TCOLS, 256) for l in range(NL)])
    out["wsm"] = np.stack([tile_w(f(w_in[l]), SCOLS, 24)[0] for l in range(NL)])
    pcol = np.zeros((NL, 128, NPCOL), np.float32)
    prow = np.zeros((NL, 1, NPROW), np.float32)
    for l in range(NL):
        cw = f(inp["ssm_conv_w"][l])[:, 0, :]
        pcol[l, :, PC_CW:PC_CW + 48] = cw.reshape(4, 12, 128).transpose(2, 1, 0).reshape(128, 48)
        pcol[l, :, PC_CB:PC_CB + 12] = f(inp["ssm_conv_b"][l]).reshape(12, 128).T
        pcol[l, :, PC_GB:PC_GB + 48] = f(inp["merge_gate_b"][l]).reshape(48, 128).T
        pcol[l, :, PC_L1G:PC_L1G + 16] = f(inp["ln1_g"][l]).reshape(16, 128).T
        pcol[l, :, PC_L1B:PC_L1B + 16] = f(inp["ln1_b"][l]).reshape(16, 128).T
        prow[l, 0, PR_MN:PR_MN + 1024] = f(inp["mlstm_norm_w"][l])
        prow[l, 0, PR_SN:PR_SN + 1024] = f(inp["ssm_norm_w"][l])
        prow[l, 0, PR_AL:PR_AL + 16] = f(inp["ssm_a_log"][l])
        prow[l, 0, PR_DS:PR_DS + 16] = f(inp["ssm_d"][l])
        prow[l, 0, PR_SK:PR_SK + 16] = f(inp["swa_sinks"][l])
        prow[l, 0, PR_L2G:PR_L2G + 2048] = f(inp["ln2_g"][l])
        prow[l, 0, PR_L2B:PR_L2B + 2048] = f(inp["ln2_b"][l])
        prow[l, 0, PR_BI:PR_BI + 24] = np.concatenate([f(inp["mlstm_gate_b"][l, 0]), f(inp["mlstm_gate_b"][l, 1]),
                                                       f(inp["ssm_dt_bias"][l])])
    out["pcol"] = pcol
    out["prow"] = prow
    wb = f(inp["w_branch"][:NL])
    out["wb"] = np.ascontiguousarray(wb.reshape(NL, 3, 8, 128, 16, 128).transpose(0, 1, 4, 3, 2, 5))
    all_cols = np.arange(2048)
    out["wo"] = np.stack([tile_w(f(inp["w_out"][l]), all_cols, 128) for l in range(NL)])
    out["wq"] = np.stack([tile_w(f(inp["peer_wq"][l]), all_cols, 128) for l in range(NL)])
    sk = f(inp["peer_subkeys"][:NL])
    out["skT"] = np.ascontiguousarray(sk.reshape(NL, 16, 128, 128).transpose(0, 3, 1, 2))
    u = f(inp["peer_u"][:NL])
    out["uT"] = np.ascontiguousarray(u.reshape(NL, 128, 128, 16, 128).transpose(0, 1, 4, 3, 2))
    out["pv"] = np.ascontiguousarray(f(inp["peer_v"][:NL]).reshape(NL, 128, 128, 2048))
    return out


_CACHE = {}
PHASES = set("abcdve")


def run(inp, S, NL, B, dbg=False, trace=False):
    key = (S, NL, dbg)
    prog = Prog(S, NL, dbg)
    nc = prog.build()
    wts = prep_weights(inp, NL)
    wts["consts"] = make_consts()
    wts["rope"] = make_rope(S)
    x = np.asarray(inp["x"], dtype=np.float32)
    in_maps = []
    for b in range(B):
        m = dict(wts)
        m["xT0"] = np.ascontiguousarray(x[b].T)
        in_maps.append(m)
    res = run_bass_kernel_spmd(nc, in_maps, core_ids=list(range(B)), trace=trace)
    return res, prog


def kernel(**inputs):
    S = inputs["x"].shape[1]
    B = inputs["x"].shape[0]
    res, _ = run(inputs, S, DEPTH, B)
    return np.stack([np.asarray(r["out"]) for r in res.results], 0).astype(np.float32)
```

```python
import numpy as np
from contextlib import ExitStack
import concourse.bass as bass
import concourse.mybir as mybir
from concourse.alu_op_type import AluOpType as ALU
from concourse.bass_utils import run_bass_kernel_spmd

F32 = mybir.dt.float32
BF16 = mybir.dt.bfloat16
AF = mybir.ActivationFunctionType
AX = mybir.AxisListType

D = 2048
DEPTH = 4
NCORES = 8
ALPHA = (2.0 * DEPTH) ** 0.25
NEG = -30000.0
KSC = 128.0 ** -0.5

_sizes = [512, 512, 1024, 1024, 4, 4, 1024, 1536, 16, 1024, 256, 256, 6144]
_off = np.concatenate([[0], np.cumsum(_sizes)]).astype(int)
(O_MQ, O_MK, O_MV, O_MO, O_MI, O_MF, O_SZ, O_XBC, O_DT, O_AQ, O_AK, O_AV, O_G) = [int(v) for v in _off[:13]]
FCOLS = np.concatenate([np.arange(O_MQ, O_MQ + 512), np.arange(O_MK, O_MK + 512),
                        np.arange(O_XBC, O_XBC + 1536), np.arange(O_G, O_G + 6144)])
TCOLS = np.concatenate([np.arange(O_MK, O_MK + 512), np.arange(O_MV, O_MV + 1024), np.arange(O_MO, O_MO + 1024),
                        np.arange(O_SZ, O_SZ + 1024), np.arange(O_AQ, O_AQ + 1024), np.arange(O_AK, O_AK + 256),
                        np.arange(O_AV, O_AV + 256)])
SCOLS = np.concatenate([np.arange(O_MI, O_MI + 4), np.arange(O_MF, O_MF + 4), np.arange(O_DT, O_DT + 16)])

C_ID, C_TRI, C_AM, C_ONE, C_OND, C_NM4, C_MA, C_M0, NCONST = 0, 128, 256, 384, 512, 640, 1152, 1408, 1664
PC_CW, PC_CB, PC_GB, PC_L1G, PC_L1B, NPCOL = 0, 48, 60, 108, 124, 140
PR_MN, PR_SN, PR_AL, PR_DS, PR_SK, PR_L2G, PR_L2B, PR_BI, NPROW = 0, 1024, 2048, 2064, 2080, 2096, 4144, 6192, 6216


class KB:
    def __init__(self, nc):
        self.nc = nc
        self.eng = {"pe": nc.tensor, "act": nc.scalar, "dve": nc.vector, "pool": nc.gpsimd, "sp": nc.sync}
        self.sem = {}
        self.cnt = {}
        for e in self.eng:
            self.sem[e] = nc.alloc_semaphore(name="s_" + e)
            self.cnt[e] = 0
        self.seen = {e: {} for e in self.eng}
        self.W = {}
        self.R = {}
        self.dsem = {}
        self.ninstr = 0

    def _wait(self, e, deps):
        seen = self.seen[e]
        for sid, (sh, val) in deps.items():
            if seen.get(sid, 0) >= val:
                continue
            self.eng[e].wait_ge(sh, val)
            seen[sid] = val

    def _deps(self, e, reads, writes, selfsid):
        deps = {}

        def add(d):
            for sid, (sh, val) in d.items():
                if sid == selfsid and e == "pe":
                    continue
                if sid not in deps or deps[sid][1] < val:
                    deps[sid] = (sh, val)

        for k in reads:
            add(self.W.get(k, {}))
        for k in writes:
            add(self.W.get(k, {}))
            add(self.R.get(k, {}))
        return deps

    def _commit(self, reads, writes, sid, sh, val):
        for k in reads:
            self.R.setdefault(k, {})[sid] = (sh, val)
        for k in writes:
            self.W[k] = {sid: (sh, val)}
            self.R[k] = {}

    @staticmethod
    def _keys(xs):
        return [x if isinstance(x, str) else x.name for x in xs]

    def op(self, e, meth, *args, R=(), W=(), **kw):
        reads = self._keys(R)
        writes = self._keys(W)
        sid = "E" + e
        self._wait(e, self._deps(e, reads, writes, sid))
        ins = getattr(self.eng[e], meth)(*args, **kw)
        self.cnt[e] += 1
        ins.then_inc(self.sem[e], 1)
        self._commit(reads, writes, sid, self.sem[e], self.cnt[e])
        self.ninstr += 1
        return ins

    def dma(self, out, in_, slot, q="sp"):
        if slot not in self.dsem:
            self.dsem[slot] = [self.nc.alloc_semaphore(name="d_" + slot), 0]
        sh, val = self.dsem[slot]
        reads = [in_.name]
        writes = [out.name]
        sid = "D" + slot
        deps = self._deps(q, reads, writes, sid)
        if val > 0:
            deps[sid] = (sh, val)
        self._wait(q, deps)
        ins = self.eng[q].dma_start(out=out, in_=in_)
        val += 16
        ins.then_inc(sh, 16)
        self.dsem[slot][1] = val
        self._commit(reads, writes, sid, sh, val)
        self.ninstr += 1
        return ins

    def barrier(self):
        deps = {}
        for e in self.eng:
            if self.cnt[e] > 0:
                deps["E" + e] = (self.sem[e], self.cnt[e])
        for slot, (sh, val) in self.dsem.items():
            if val > 0:
                deps["D" + slot] = (sh, val)
        for e in self.eng:
            self._wait(e, deps)

    def mm(self, out, lhsT, rhs, start=True, stop=True):
        return self.op("pe", "matmul", out, lhsT, rhs, start=start, stop=stop, R=[lhsT, rhs], W=[out])

    def tr(self, out, in_, ident):
        return self.op("pe", "transpose", out, in_, ident, R=[in_, ident], W=[out])

    def act(self, out, in_, func, bias=None, scale=None, accum_out=None):
        kw = {}
        rs = [in_]
        ws = [out]
        if bias is not None:
            kw["bias"] = bias
            if not isinstance(bias, (int, float)):
                rs.append(bias)
        if scale is not None:
            kw["scale"] = scale
            if not isinstance(scale, (int, float)):
                rs.append(scale)
        if accum_out is not None:
            kw["accum_out"] = accum_out
            ws.append(accum_out)
        return self.op("act", "activation", out, in_, func, R=rs, W=ws, **kw)

    def tt(self, out, in0, in1, op, e="dve"):
        return self.op(e, "tensor_tensor", out, in0, in1, op, R=[in0, in1], W=[out])

    def ts(self, out, in0, s1, s2, op0, op1=None, e="dve"):
        rs = [in0] + [s for s in (s1, s2) if s is not None and not isinstance(s, (int, float))]
        if op1 is None:
            return self.op(e, "tensor_scalar", out, in0, s1, None, op0, R=rs, W=[out])
        return self.op(e, "tensor_scalar", out, in0, s1, s2, op0, op1, R=rs, W=[out])

    def stt(self, out, in0, scalar, in1, op0, op1):
        rs = [in0, in1] + ([] if isinstance(scalar, (int, float)) else [scalar])
        return self.op("dve", "scalar_tensor_tensor", out, in0, scalar, in1, op0, op1, R=rs, W=[out])

    def copy(self, out, in_, e="act"):
        if e == "act":
            return self.op("act", "copy", out, in_, R=[in_], W=[out])
        return self.op(e, "tensor_copy", out, in_, R=[in_], W=[out])

    def mul(self, out, in_, c):
        return self.op("act", "mul", out, in_, c, R=[in_], W=[out])

    def memset(self, ap, val, e="dve"):
        return self.op(e, "memset", ap, val, R=[], W=[ap])

    def recip(self, out, in_):
        return self.op("dve", "reciprocal", out, in_, R=[in_], W=[out])


def bc(ap, shape, axis):
    return ap.unsqueeze(axis).to_broadcast(list(shape))


class Prog:
    def __init__(self, S, nlayers, dbg=False):
        self.S = S
        self.NL = nlayers
        self.dbg = dbg
        self.BLK = min(512, S)
        self.NCH = S // 128
        nc = self.nc = bass.Bass("TRN2", target_bir_lowering=False)
        self.k = KB(nc)
        L = nlayers

        def din(name, shape):
            return nc.dram_tensor(name, list(shape), F32, kind="ExternalInput").ap()

        self.xT0 = din("xT0", [D, S])
        self.consts_d = din("consts", [128, NCONST])
        self.rope_d = din("rope", [S, 64])
        self.wf_d = din("wf", [L, 68, 128, 16, 128])
        self.wt_d = din("wt", [L, 10, 128, 16, 512])
        self.wsm_d = din("wsm", [L, 128, 16, 24])
        self.pcol_d = din("pcol", [L, 128, NPCOL])
        self.prow_d = din("prow", [L, 1, NPROW])
        self.wb_d = din("wb", [L, 3, 16, 128, 8, 128])
        self.wo_d = din("wo", [L, 16, 128, 16, 128])
        self.wq_d = din("wq", [L, 16, 128, 16, 128])
        self.skT_d = din("skT", [L, 128, 16, 128])
        self.uT_d = din("uT", [L, 128, 128, 16, 128])
        self.v_d = din("pv", [L, 128, 128, 2048])
        self.out_d = nc.dram_tensor("out", [S, D], F32, kind="ExternalOutput").ap()
        self.xTa = nc.dram_tensor("xTa", [D, S], F32, kind="Internal").ap()
        self.xTb_ = nc.dram_tensor("xTbb", [D, S], F32, kind="Internal").ap()
        self.x1T = nc.dram_tensor("x1T", [D, S], F32, kind="Internal").ap()
        self.yT = nc.dram_tensor("yT", [self.NCH, 128, 24, 128], F32, kind="Internal").ap()
        self.uTb = nc.dram_tensor("uTb", [128, 128, 2048], BF16, kind="Internal").ap()
        self.wfb = nc.dram_tensor("wfb", [68, 128, 2048], BF16, kind="Internal").ap()
        self.wtb = nc.dram_tensor("wtb", [10, 128, 8192], BF16, kind="Internal").ap()
        self.wbb = nc.dram_tensor("wbb", [48, 128, 1024], BF16, kind="Internal").ap()
        self.wob = nc.dram_tensor("wob", [16, 128, 2048], BF16, kind="Internal").ap()
        self.vbf = nc.dram_tensor("vbf", [128, 128, 2048], BF16, kind="Internal").ap()
        if dbg:
            self.dbg_y = nc.dram_tensor("dbg_y", [L, self.NCH, 128, 24, 128], F32, kind="ExternalOutput").ap()
            self.dbg_x1T = nc.dram_tensor("dbg_x1T", [L, D, S], F32, kind="ExternalOutput").ap()
        self.cst = nc.alloc_sbuf_tensor("cst", [128, NCONST], F32)
        self.ps = [nc.alloc_psum_tensor(f"ps{i}", [128, 512], F32) for i in range(8)]
        k = self.k
        k.dma(self.cst[:], self.consts_d[:], "cst")
        c = self.cst
        self.ident = c[:, C_ID:C_ID + 128]
        self.tri = c[:, C_TRI:C_TRI + 128]
        self.amat = c[:, C_AM:C_AM + 128]
        self.ones = c[:, C_ONE:C_ONE + 128]
        self.onesD = c[:, C_OND:C_OND + 128]
        self.nm4 = c[:, C_NM4:C_NM4 + 512]
        self.maskA = c[:, C_MA:C_MA + 256]
        self.mask0 = c[:, C_M0:C_M0 + 256]

    def build(self):
        k = self.k
        xin = self.xT0
        bufs = [self.xTa, self.xTb_]
        for l in range(self.NL):
            last = (l == self.NL - 1)
            xout = None if last else bufs[l % 2]
            self.phase_wconv(l)
            k.barrier()
            if "a" in PHASES:
                self.phase_mlstm(l, xin)
                k.barrier()
            if "b" in PHASES:
                self.phase_ssd(l, xin)
                k.barrier()
            if "c" in PHASES:
                self.phase_swa(l, xin)
                k.barrier()
            if self.dbg:
                k.dma(self.dbg_y[l], self.yT, "dbgy", q="pool")
            if "d" in PHASES:
                self.phase_mix(l, xin)
                k.barrier()
            if self.dbg:
                k.dma(self.dbg_x1T[l], self.x1T, "dbgx", q="pool")
            if "v" in PHASES:
                self.phase_peer_conv(l)
                k.barrier()
            if "e" in PHASES:
                self.phase_peer(l, xout)
                k.barrier()
            xin = xout
        deps = {}
        for key in ["out"] + (["dbg_y", "dbg_x1T"] if self.dbg else []):
            for sid, (sh, val) in k.W.get(key, {}).items():
                deps[sid] = (sh, val)
        k._wait("sp", deps)
        return self.nc

    def xT_view(self, xT, t0, n):
        return xT.rearrange("(kc p) t -> p kc t", p=128)[:, :, t0:t0 + n]

    def fmaj(self, l, c, w, bank, xTb, BLK):
        k = self.k
        k.dma(w[:], self.wfb[c].rearrange("p (k e) -> p k e", k=16), "wf" + w.name[-1])
        for kc in range(16):
            k.mm(bank[:, 0:BLK], w[:, kc, :], xTb[:, kc, :], start=(kc == 0), stop=(kc == 15))

    def tmaj(self, l, ti, w, bank, xTb, cs):
        k = self.k
        k.dma(w[:], self.wtb[ti].rearrange("p (k e) -> p k e", k=16), "wt" + w.name[-1])
        for kc in range(16):
            k.mm(bank[:, 0:512], xTb[:, kc, cs], w[:, kc, :], start=(kc == 0), stop=(kc == 15))

    def small_proj(self, wsm, xTb, cs, bank, small, biasb):
        k = self.k
        for kc in range(16):
            k.mm(bank[:, 0:24], xTb[:, kc, cs], wsm[:, kc, :], start=(kc == 0), stop=(kc == 15))
        k.tt(small[:], bank[:, 0:24], biasb, ALU.add)

    def emit_yT(self, ysrc, g, col0, yTc, banks):
        k = self.k
        for c in range(8):
            k.tr(banks[c // 4][:, (c % 4) * 128:(c % 4 + 1) * 128], ysrc[:, c * 128:(c + 1) * 128], self.ident)
        for hh in range(2):
            k.copy(yTc[:, hh * 4:(hh + 1) * 4, :], banks[hh][:, 0:512].rearrange("p (a b) -> p a b", a=4))
        k.dma(self.yT[g, :, col0:col0 + 8, :], yTc[:], "yst", q="pool")

    def phase_wconv(self, l):
        k, nc = self.k, self.nc
        items = []
        for c in range(68):
            items.append((self.wf_d[l, c].rearrange("p k e -> p (k e)"), self.wfb[c], 2048))
        for t in range(10):
            items.append((self.wt_d[l, t].rearrange("p k e -> p (k e)"), self.wtb[t], 8192))
        for kk in range(3):
            for c in range(16):
                items.append((self.wb_d[l, kk, c].rearrange("p k e -> p (k e)"), self.wbb[kk * 16 + c], 1024))
        for c in range(16):
            items.append((self.wo_d[l, c].rearrange("p k e -> p (k e)"), self.wob[c], 2048))
        with ExitStack() as es:
            s32 = [es.enter_context(nc.sbuf_tensor(f"L{l}w_s32{i}", [128, 8192], F32)) for i in range(2)]
            s16 = [es.enter_context(nc.sbuf_tensor(f"L{l}w_s16{i}", [128, 8192], BF16)) for i in range(2)]
            for n, (src, dst, w) in enumerate(items):
                i = n % 2
                k.dma(s32[i][:, 0:w], src, f"cw{i}")
                k.copy(s16[i][:, 0:w], s32[i][:, 0:w], e=("act" if i else "dve"))
                k.dma(dst, s16[i][:, 0:w], f"sw{i}", q="pool")

    def phase_mlstm(self, l, xT):
        k, nc, ps = self.k, self.nc, self.ps
        BLK = self.BLK
        with ExitStack() as es:
            def sb(name, shape):
                return es.enter_context(nc.sbuf_tensor(f"L{l}a_{name}", list(shape), F32))
            xTb = sb("xTb", [128, 16, BLK])
            mqT = sb("mqT", [128, 4, BLK])
            mkT = sb("mkT", [128, 4, BLK])
            wf = [es.enter_context(nc.sbuf_tensor(f"L{l}a_wf{i}", [128, 16, 128], BF16)) for i in range(2)]
            wt = [es.enter_context(nc.sbuf_tensor(f"L{l}a_wt{i}", [128, 16, 512], BF16)) for i in range(2)]
            xTh = es.enter_context(nc.sbuf_tensor(f"L{l}a_xTh", [128, 16, BLK], BF16))
            wsm = sb("wsm", [128, 16, 24])
            mktok = sb("mktok", [128, 512])
            mvext = sb("mvext", [128, 4, 257])
            osig = sb("osig", [128, 1024])
            small = sb("small", [128, 24])
            normw = sb("normw", [128, 1024])
            biasb = sb("biasb", [128, 24])
            Cext = [sb(f"C{h}", [128, 257]) for h in range(4)]
            e1 = sb("e1", [128, 4])
            lf = sb("lf", [128, 4])
            Bmat = sb("Bmat", [128, 4, 128])
            Dt = sb("Dt", [128, 4, 128])
            EB = sb("EB", [128, 4, 128])
            PT = sb("PT", [128, 128])
            qsT = sb("qsT", [128, 128])
            den = sb("den", [128, 1])
            hb = sb("hb", [128, 256])
            st6 = sb("st6", [128, 6])
            mv2 = sb("mv2", [128, 2])
            rstd = sb("rstd", [128, 1])
            kw = sb("kw", [128, 128])
            ym = sb("ym", [128, 1024])
            yTc = sb("yTc", [128, 8, 128])

            k.dma(wsm[:], self.wsm_d[l], "par0")
            k.dma(normw[:], self.prow_d[l, :, PR_MN:PR_MN + 1024].partition_broadcast(128), "par1")
            k.dma(biasb[:], self.prow_d[l, :, PR_BI:PR_BI + 24].partition_broadcast(128), "par2")
            for h in range(4):
                k.memset(Cext[h][:], 0.0)
            k.memset(mvext[:], 1.0)
            for b in range(self.S // BLK):
                t0 = b * BLK
                k.dma(xTb[:], self.xT_view(xT, t0, BLK), "xT")
                k.copy(xTh[:], xTb[:], e="pool")
                for c in range(8):
                    bank = ps[c % 2]
                    self.fmaj(l, c, wf[c % 2], bank, xTh, BLK)
                    if c < 4:
                        k.copy(mqT[:, c, :], bank[:, 0:BLK])
                    else:
                        k.mul(mkT[:, c - 4, :], bank[:, 0:BLK], KSC)
                for j in range(BLK // 128):
                    g = (t0 // 128) + j
                    cs = slice(j * 128, (j + 1) * 128)
                    for ti in range(5):
                        bank = ps[2 + ti % 2]
                        self.tmaj(l, ti, wt[ti % 2], bank, xTh, cs)
                        if ti == 0:
                            k.copy(mktok[:], bank[:, 0:512])
                        elif ti < 3:
                            k.copy(mvext[:, 2 * (ti - 1):2 * ti, 0:256], bank[:, 0:512].rearrange("p (a b) -> p a b", a=2))
                        else:
                            k.act(osig[:, (ti - 3) * 512:(ti - 2) * 512], bank[:, 0:512], AF.Sigmoid)
                    self.small_proj(wsm, xTb, cs, ps[4], small, biasb[:])
                    k.act(e1[:], small[:, 4:8], AF.Exp, scale=-1.0)
                    k.act(e1[:], e1[:], AF.Ln, bias=1.0)
                    k.mul(lf[:], e1[:], -1.0)
                    k.tt(Bmat[:], bc(lf[:], [128, 4, 128], 2), bc(self.tri, [128, 4, 128], 1), ALU.mult)
                    Bm2 = Bmat[:].rearrange("p h t -> p (h t)")
                    k.mm(ps[5][:, 0:512], self.amat, Bm2, start=True, stop=False)
                    k.mm(ps[5][:, 0:512], self.ident, self.nm4, start=False, stop=True)
                    k.mm(ps[6][:, 0:512], self.ones, Bm2)
                    for h in range(4):
                        k.act(Dt[:, h, :], ps[5][:, h * 128:(h + 1) * 128], AF.Exp, bias=small[:, h:h + 1])
                    k.act(EB[:].rearrange("p h t -> p (h t)"), ps[6][:, 0:512], AF.Exp)
                    for h in range(4):
                        k.mm(ps[7][:, 0:128], mkT[:, h, cs], mqT[:, h, cs])
                        k.tt(PT[:], ps[7][:, 0:128], Dt[:, h, :], ALU.mult)
                        k.tt(qsT[:], mqT[:, h, cs], EB[:, h, :], ALU.mult, e="pool")
                        nm = ps[h % 2]
                        k.mm(nm[:, 0:257], PT[:], mvext[:, h, :], start=True, stop=False)
                        k.mm(nm[:, 0:257], qsT[:], Cext[h][:], start=False, stop=True)
                        k.act(den[:], nm[:, 256:257], AF.Abs)
                        k.ts(den[:], den[:], 1.0, None, ALU.max)
                        k.recip(den[:], den[:])
                        k.ts(hb[:], nm[:, 0:256], den[:], None, ALU.mult)
                        k.op("dve", "bn_stats", st6[:], hb[:], R=[hb], W=[st6])
                        k.op("dve", "bn_aggr", mv2[:], st6[:], R=[st6], W=[mv2])
                        k.act(rstd[:], mv2[:, 1:2], AF.Sqrt, bias=1e-6)
                        k.recip(rstd[:], rstd[:])
                        k.ts(hb[:], hb[:], mv2[:, 0:1], rstd[:], ALU.subtract, ALU.mult)
                        k.tt(hb[:], hb[:], normw[:, h * 256:(h + 1) * 256], ALU.mult)
                        k.tt(ym[:, h * 256:(h + 1) * 256], hb[:], osig[:, h * 256:(h + 1) * 256], ALU.mult, e="pool")
                        k.ts(kw[:], mktok[:, h * 128:(h + 1) * 128], Dt[:, h, 127:128], KSC, ALU.mult, ALU.mult)
                        cu = ps[2 + h % 2]
                        k.mm(cu[:, 0:257], kw[:], mvext[:, h, :])
                        k.stt(Cext[h][:], Cext[h][:], EB[:, h, 127:128], cu[:, 0:257], ALU.mult, ALU.add)
                    self.emit_yT(ym, g, 0, yTc, [ps[4], ps[5]])

    def phase_ssd(self, l, xT):
        k, nc, ps = self.k, self.nc, self.ps
        BLK = self.BLK
        with ExitStack() as es:
            def sb(name, shape):
                return es.enter_context(nc.sbuf_tensor(f"L{l}b_{name}", list(shape), F32))
            xTb = sb("xTb", [128, 16, BLK])
            xbcT = sb("xbcT", [128, 12, BLK])
            u = sb("u", [128, BLK + 3])
            halo = sb("halo", [128, 12, 3])
            wf = [es.enter_context(nc.sbuf_tensor(f"L{l}b_wf{i}", [128, 16, 128], BF16)) for i in range(2)]
            wt = [es.enter_context(nc.sbuf_tensor(f"L{l}b_wt{i}", [128, 16, 512], BF16)) for i in range(2)]
            xTh = es.enter_context(nc.sbuf_tensor(f"L{l}b_xTh", [128, 16, BLK], BF16))
            wsm = sb("wsm", [128, 16, 24])
            pcol = sb("pcol", [128, NPCOL])
            prw = sb("prw", [128, 1024 + 48])
            biasb = sb("biasb", [128, 24])
            zsil = sb("zsil", [128, 1024])
            small = sb("small", [128, 24])
            dt = sb("dt", [128, 16])
            aneg = sb("aneg", [128, 16])
            dtA = sb("dtA", [128, 16])
            ex = sb("ex", [128, 48])
            Bm = sb("Bm", [128, 16, 128])
            dec = sb("dec", [128, 16, 128])
            MT = sb("MT", [128, 16, 128])
            xstok = sb("xstok", [128, 16, 64])
            Btok = sb("Btok", [128, 2, 128])
            xdt = sb("xdt", [128, 16, 64])
            xw = sb("xw", [128, 16, 64])
            t1 = sb("t1", [128, 16, 64])
            t3 = sb("t3", [128, 16, 64])
            Dfull = sb("Dfull", [128, 16, 64])
            H = [sb("H0", [128, 8, 64]), sb("H1", [128, 8, 64])]
            ss = sb("ss", [128, 2])
            junk = sb("junk", [128, 512])
            ys = sb("ys", [128, 1024])
            yTc = sb("yTc", [128, 8, 128])

            k.dma(wsm[:], self.wsm_d[l], "par0")
            k.dma(pcol[:], self.pcol_d[l], "par1")
            k.dma(prw[:], self.prow_d[l, :, PR_SN:PR_SN + 1072].partition_broadcast(128), "par2")
            k.dma(biasb[:], self.prow_d[l, :, PR_BI:PR_BI + 24].partition_broadcast(128), "par3")
            snw = prw[:, 0:1024]
            k.act(aneg[:], prw[:, 1024:1040], AF.Exp)
            k.mul(aneg[:], aneg[:], -1.0)
            k.copy(Dfull[:], bc(prw[:, 1040:1056], [128, 16, 64], 2), e="dve")
            k.memset(halo[:], 0.0)
            k.memset(H[0][:], 0.0)
            k.memset(H[1][:], 0.0)
            for b in range(self.S // BLK):
                t0 = b * BLK
                k.dma(xTb[:], self.xT_view(xT, t0, BLK), "xT")
                k.copy(xTh[:], xTb[:], e="pool")
                for c in range(12):
                    bank = ps[c % 2]
                    self.fmaj(l, 8 + c, wf[c % 2], bank, xTh, BLK)
                    k.copy(u[:, 0:3], halo[:, c, :], e="pool")
                    k.copy(u[:, 3:3 + BLK], bank[:, 0:BLK])
                    k.copy(halo[:, c, :], u[:, BLK:BLK + 3], e="pool")
                    acc = xbcT[:, c, :]
                    k.ts(acc, u[:, 0:BLK], pcol[:, PC_CW + c * 4:PC_CW + c * 4 + 1], pcol[:, PC_CB + c:PC_CB + c + 1],
                         ALU.mult, ALU.add)
                    for kk in range(1, 4):
                        k.stt(acc, u[:, kk:kk + BLK], pcol[:, PC_CW + c * 4 + kk:PC_CW + c * 4 + kk + 1], acc,
                              ALU.mult, ALU.add)
                    k.act(acc, acc, AF.Silu)
                for j in range(BLK // 128):
                    g = (t0 // 128) + j
                    cs = slice(j * 128, (j + 1) * 128)
                    for ti in range(5, 7):
                        bank = ps[2 + ti % 2]
                        self.tmaj(l, ti, wt[ti % 2], bank, xTh, cs)
                        k.act(zsil[:, (ti - 5) * 512:(ti - 4) * 512], bank[:, 0:512], AF.Silu)
                    self.small_proj(wsm, xTb, cs, ps[4], small, biasb[:])
                    k.act(dt[:], small[:, 8:24], AF.Exp)
                    k.act(dt[:], dt[:], AF.Ln, bias=1.0)
                    k.tt(dtA[:], dt[:], aneg[:], ALU.mult)
                    X = ps[4]
                    k.mm(X[:, 0:16], self.tri, dtA[:])
                    k.mm(X[:, 16:32], self.amat, dtA[:])
                    k.mm(X[:, 32:48], self.ones, dtA[:])
                    k.act(ex[:], X[:, 0:48], AF.Exp)
                    eacum, erev, ecd = ex[:, 0:16], ex[:, 16:32], ex[:, 32:48]
                    k.tt(Bm[:], bc(dtA[:], [128, 16, 128], 2), bc(self.tri, [128, 16, 128], 1), ALU.mult)
                    for q in range(4):
                        bank = ps[5 + q % 2]
                        k.mm(bank[:, 0:512], self.amat, Bm[:, 4 * q:4 * q + 4, :].rearrange("p h t -> p (h t)"),
                             start=True, stop=False)
                        k.mm(bank[:, 0:512], self.ident, self.nm4, start=False, stop=True)
                        k.act(dec[:, 4 * q:4 * q + 4, :].rearrange("p h t -> p (h t)"), bank[:, 0:512], AF.Exp)
                    for c in range(8):
                        k.tr(ps[c // 4][:, (c % 4) * 128:(c % 4 + 1) * 128], xbcT[:, c, cs], self.ident)
                    xs2 = xstok[:].rearrange("p h d -> p (h d)")
                    k.copy(xs2[:, 0:512], ps[0][:, 0:512])
                    k.copy(xs2[:, 512:1024], ps[1][:, 0:512])
                    for gg in range(2):
                        k.tr(ps[7][:, gg * 128:(gg + 1) * 128], xbcT[:, 8 + gg, cs], self.ident)
                    k.copy(Btok[:].rearrange("p g n -> p (g n)"), ps[7][:, 0:256])
                    k.tt(xdt[:], xstok[:], bc(dt[:], [128, 16, 64], 2), ALU.mult)
                    k.tt(xw[:], xdt[:], bc(erev, [128, 16, 64], 2), ALU.mult, e="pool")
                    for gg in range(2):
                        k.mm(ps[7][:, 256 + gg * 128:256 + (gg + 1) * 128], xbcT[:, 8 + gg, cs], xbcT[:, 10 + gg, cs])
                    for gg in range(2):
                        k.tt(MT[:, gg * 8:(gg + 1) * 8, :], dec[:, gg * 8:(gg + 1) * 8, :],
                             bc(ps[7][:, 256 + gg * 128:256 + (gg + 1) * 128], [128, 8, 128], 1), ALU.mult)
                    for gg in range(2):
                        yd = ps[gg]
                        yo = ps[2 + gg]
                        for hh in range(8):
                            h = gg * 8 + hh
                            k.mm(yd[:, hh * 64:(hh + 1) * 64], MT[:, h, :], xdt[:, h, :])
                        k.mm(yo[:, 0:512], xbcT[:, 10 + gg, cs], H[gg][:].rearrange("p h d -> p (h d)"))
                        t1g = t1[:, gg * 8:(gg + 1) * 8, :]
                        k.tt(t1g, yo[:, 0:512].rearrange("p (h d) -> p h d", h=8),
                             bc(eacum[:, gg * 8:(gg + 1) * 8], [128, 8, 64], 2), ALU.mult)
                        k.tt(t1g, t1g, yd[:, 0:512].rearrange("p (h d) -> p h d", h=8), ALU.add)
                    k.tt(t3[:], xstok[:], Dfull[:], ALU.mult, e="pool")
                    k.tt(t1[:], t1[:], t3[:], ALU.add)
                    t12 = t1[:].rearrange("p h d -> p (h d)")
                    k.tt(t12, t12, zsil[:], ALU.mult)
                    for gg in range(2):
                        k.act(junk[:], t12[:, gg * 512:(gg + 1) * 512], AF.Square, accum_out=ss[:, gg:gg + 1])
                    k.act(ss[:], ss[:], AF.Sqrt, bias=1e-6, scale=1.0 / 512)
                    k.recip(ss[:], ss[:])
                    for gg in range(2):
                        k.stt(ys[:, gg * 512:(gg + 1) * 512], t12[:, gg * 512:(gg + 1) * 512], ss[:, gg:gg + 1],
                              snw[:, gg * 512:(gg + 1) * 512], ALU.mult, ALU.mult)
                    for gg in range(2):
                        sp_ = ps[5 + gg]
                        k.mm(sp_[:, 0:512], Btok[:, gg, :], xw[:, gg * 8:(gg + 1) * 8, :].rearrange("p h d -> p (h d)"))
                        k.tt(H[gg][:], H[gg][:], bc(ecd[:, gg * 8:(gg + 1) * 8], [128, 8, 64], 2), ALU.mult)
                        k.tt(H[gg][:], H[gg][:], sp_[:, 0:512].rearrange("p (h d) -> p h d", h=8), ALU.add)
                    self.emit_yT(ys, g, 8, yTc, [ps[0], ps[1]])

    def phase_swa(self, l, xT):
        k, nc, ps = self.k, self.nc, self.ps
        BLK = self.BLK
        with ExitStack() as es:
            def sb(name, shape):
                return es.enter_context(nc.sbuf_tensor(f"L{l}c_{name}", list(shape), F32))
            xTb = sb("xTb", [128, 16, BLK])
            wt = [es.enter_context(nc.sbuf_tensor(f"L{l}c_wt{i}", [128, 16, 512], BF16)) for i in range(2)]
            xTh = es.enter_context(nc.sbuf_tensor(f"L{l}c_xTh", [128, 16, BLK], BF16))
            sinks = sb("sinks", [128, 16])
            aq = sb("aq", [128, 16, 64])
            ak = sb("ak", [128, 4, 64])
            av = [sb("av0", [128, 256]), sb("av1", [128, 256])]
            cs_t = sb("cs", [128, 64])
            ta = sb("ta", [128, 16, 32])
            tb = sb("tb", [128, 16, 32])
            aqr = sb("aqr", [128, 16, 64])
            akd = sb("akd", [128, 4, 2, 64])
            qT = sb("qT", [128, 8, 128])
            kT = [sb("kT0", [128, 4, 128]), sb("kT1", [128, 4, 128])]
            Sm = sb("Sm", [128, 256])
            mx = sb("mx", [128, 1])
            negm = sb("negm", [128, 1])
            p = sb("p", [128, 256])
            rsum = sb("rsum", [128, 1])
            esk = sb("esk", [128, 1])
            den = sb("den", [128, 1])
            pT = sb("pT", [128, 256])
            ya = sb("ya", [128, 1024])
            yTc = sb("yTc", [128, 8, 128])

            k.dma(sinks[:], self.prow_d[l, :, PR_SK:PR_SK + 16].partition_broadcast(128), "par0")
            k.memset(kT[1][:], 0.0)
            k.memset(av[1][:], 0.0)
            for b in range(self.S // BLK):
                t0 = b * BLK
                k.dma(xTb[:], self.xT_view(xT, t0, BLK), "xT")
                k.copy(xTh[:], xTb[:], e="pool")
                for j in range(BLK // 128):
                    g = (t0 // 128) + j
                    cur, prv = g % 2, (g + 1) % 2
                    cs = slice(j * 128, (j + 1) * 128)
                    k.dma(cs_t[:], self.rope_d[g * 128:(g + 1) * 128, :], "rope")
                    aq2 = aq[:].rearrange("p h d -> p (h d)")
                    for ti in range(7, 10):
                        bank = ps[ti % 2]
                        self.tmaj(l, ti, wt[ti % 2], bank, xTh, cs)
                        if ti < 9:
                            k.copy(aq2[:, (ti - 7) * 512:(ti - 6) * 512], bank[:, 0:512])
                        else:
                            k.copy(ak[:].rearrange("p h d -> p (h d)"), bank[:, 0:256])
                            k.copy(av[cur][:], bank[:, 256:512])
                    for (src, dst, nh) in ((aq, aqr[:], 16), (ak, akd[:, :, 0, :], 4)):
                        x1, x2 = src[:, :, 0:32], src[:, :, 32:64]
                        cb = bc(cs_t[:, 0:32], [128, nh, 32], 1)
                        sn = bc(cs_t[:, 32:64], [128, nh, 32], 1)
                        k.tt(ta[:, 0:nh, :], x1, cb, ALU.mult)
                        k.tt(tb[:, 0:nh, :], x2, sn, ALU.mult, e="pool")
                        k.tt(dst[:, :, 0:32], ta[:, 0:nh, :], tb[:, 0:nh, :], ALU.subtract)
                        k.tt(ta[:, 0:nh, :], x2, cb, ALU.mult)
                        k.tt(tb[:, 0:nh, :], x1, sn, ALU.mult, e="pool")
                        k.tt(dst[:, :, 32:64], ta[:, 0:nh, :], tb[:, 0:nh, :], ALU.add)
                    k.copy(akd[:, :, 1, :], akd[:, :, 0, :], e="pool")
                    aqr2 = aqr[:].rearrange("p h d -> p (h d)")
                    for m in range(8):
                        k.tr(ps[2 + m // 4][:, (m % 4) * 128:(m % 4 + 1) * 128], aqr2[:, m * 128:(m + 1) * 128], self.ident)
                    for hh in range(2):
                        k.copy(qT[:, hh * 4:(hh + 1) * 4, :], ps[2 + hh][:, 0:512].rearrange("p (a b) -> p a b", a=4))
                    for gg in range(4):
                        k.tr(ps[4][:, gg * 128:(gg + 1) * 128], akd[:, gg, :, :].rearrange("p a d -> p (a d)"), self.ident)
                    k.copy(kT[cur][:], ps[4][:, 0:512].rearrange("p (a b) -> p a b", a=4))
                    mask = self.mask0 if g == 0 else self.maskA
                    for h in range(16):
                        gg, base, m = h // 4, (h % 2) * 64, h // 2
                        sbk = ps[5 + h % 2]
                        k.mm(sbk[:, 0:128], qT[base:base + 64, m, :], kT[prv][base:base + 64, gg, :])
                        k.mm(sbk[:, 128:256], qT[base:base + 64, m, :], kT[cur][base:base + 64, gg, :])
                        k.stt(Sm[:], sbk[:, 0:256], 0.125, mask, ALU.mult, ALU.add)
                        k.op("dve", "reduce_max", mx[:], Sm[:], axis=AX.X, R=[Sm], W=[mx])
                        k.ts(negm[:], mx[:], sinks[:, h:h + 1], -1.0, ALU.max, ALU.mult)
                        k.act(p[:], Sm[:], AF.Exp, bias=negm[:], accum_out=rsum[:])
                        k.act(esk[:], sinks[:, h:h + 1], AF.Exp, bias=negm[:])
                        k.tt(den[:], rsum[:], esk[:], ALU.add)
                        k.recip(den[:], den[:])
                        pb = ps[h % 2]
                        k.tr(pb[:, 0:128], p[:, 0:128], self.ident)
                        k.tr(pb[:, 128:256], p[:, 128:256], self.ident)
                        k.copy(pT[:], pb[:, 0:256])
                        ob = ps[7]
                        k.mm(ob[:, 0:64], pT[:, 0:128], av[prv][:, gg * 64:(gg + 1) * 64], start=True, stop=False)
                        k.mm(ob[:, 0:64], pT[:, 128:256], av[cur][:, gg * 64:(gg + 1) * 64], start=False, stop=True)
                        k.ts(ya[:, h * 64:(h + 1) * 64], ob[:, 0:64], den[:], None, ALU.mult)
                    self.emit_yT(ya, g, 16, yTc, [ps[2], ps[3]])

    def phase_mix(self, l, xT):
        k, nc, ps = self.k, self.nc, self.ps
        BLK = self.BLK
        with ExitStack() as es:
            def sb(name, shape, dt=F32):
                return es.enter_context(nc.sbuf_tensor(f"L{l}d_{name}", list(shape), dt))
            xTb = sb("xTb", [128, 16, BLK])
            xTh = sb("xTh", [128, 16, BLK], BF16)
            ytmp = [sb("ytmp0", [128, 24, 128]), sb("ytmp1", [128, 24, 128])]
            yTh = sb("yTh", [128, 24, BLK], BF16)
            mixT = sb("mixT", [128, 16, BLK], BF16)
            wg = [sb("wg0", [128, 16, 128], BF16), sb("wg1", [128, 16, 128], BF16)]
            wb = [sb("wb0", [128, 8, 128], BF16), sb("wb1", [128, 8, 128], BF16)]
            wo = [sb("wo0", [128, 16, 128], BF16), sb("wo1", [128, 16, 128], BF16)]
            pcol = sb("pcol", [128, NPCOL])
            gsb = [sb("gsb0", [128, BLK]), sb("gsb1", [128, BLK])]
            tmp = [sb("tmp0", [128, BLK]), sb("tmp1", [128, BLK])]
            acc = [sb("acc0", [128, BLK]), sb("acc1", [128, BLK])]
            sq = [sb("sq0", [128, BLK]), sb("sq1", [128, BLK])]
            rstd = sb("rstd", [128, BLK])
            k.dma(pcol[:], self.pcol_d[l], "par0")
            n = 0
            for b in range(self.S // BLK):
                t0 = b * BLK
                k.dma(xTb[:], self.xT_view(xT, t0, BLK), "xT")
                k.copy(xTh[:], xTb[:], e="pool")
                for j in range(BLK // 128):
                    yt = ytmp[j % 2]
                    k.dma(yt[:], self.yT[t0 // 128 + j], "yT" + yt.name[-1])
                    k.copy(yTh[:, :, j * 128:(j + 1) * 128], yt[:], e=("act" if j % 2 else "dve"))
                for c in range(16):
                    ac = acc[c % 2]
                    for kk in range(3):
                        n += 1
                        w = wg[n % 2]
                        k.dma(w[:], self.wfb[20 + kk * 16 + c].rearrange("p (k e) -> p k e", k=16), "wg" + w.name[-1])
                        gb = ps[n % 2]
                        for kc in range(16):
                            k.mm(gb[:, 0:BLK], w[:, kc, :], xTh[:, kc, :], start=(kc == 0), stop=(kc == 15))
                        gs = gsb[n % 2]
                        k.act(gs[:], gb[:, 0:BLK], AF.Sigmoid, bias=pcol[:, PC_GB + kk * 16 + c:PC_GB + kk * 16 + c + 1])
                        w2 = wb[n % 2]
                        k.dma(w2[:], self.wbb[kk * 16 + c].rearrange("p (k e) -> p k e", k=8), "wb" + w2.name[-1])
                        bb = ps[2 + n % 2]
                        for kc in range(8):
                            k.mm(bb[:, 0:BLK], w2[:, kc, :], yTh[:, kk * 8 + kc, :], start=(kc == 0), stop=(kc == 7))
                        if kk == 0:
                            k.tt(ac[:], gs[:], bb[:, 0:BLK], ALU.mult)
                        elif kk == 1:
                            tp = tmp[n % 2]
                            k.tt(tp[:], gs[:], bb[:, 0:BLK], ALU.mult)
                            k.tt(ac[:], ac[:], tp[:], ALU.add, e="pool")
                        else:
                            tp = tmp[n % 2]
                            k.tt(tp[:], gs[:], bb[:, 0:BLK], ALU.mult)
                            k.tt(mixT[:, c, :], ac[:], tp[:], ALU.add, e="pool")
                for c in range(16):
                    w = wo[c % 2]
                    k.dma(w[:], self.wob[c].rearrange("p (k e) -> p k e", k=16), "wo" + w.name[-1])
                    ob = ps[4 + c % 2]
                    for kc in range(16):
                        k.mm(ob[:, 0:BLK], w[:, kc, :], mixT[:, kc, :], start=(kc == 0), stop=(kc == 15))
                    k.stt(xTb[:, c, :], xTb[:, c, :], ALPHA, ob[:, 0:BLK], ALU.mult, ALU.add)
                mb, vb = ps[6], ps[7]
                for c in range(16):
                    k.mm(mb[:, 0:BLK], self.onesD, xTb[:, c, :], start=(c == 0), stop=(c == 15))
                for c in range(16):
                    k.tt(xTb[:, c, :], xTb[:, c, :], mb[:, 0:BLK], ALU.subtract)
                    s_ = sq[c % 2]
                    k.act(s_[:], xTb[:, c, :], AF.Square)
                    k.mm(vb[:, 0:BLK], self.onesD, s_[:], start=(c == 0), stop=(c == 15))
                k.act(rstd[:], vb[:, 0:BLK], AF.Sqrt, bias=1e-5)
                k.recip(rstd[:], rstd[:])
                for c in range(16):
                    k.tt(xTb[:, c, :], xTb[:, c, :], rstd[:], ALU.mult)
                    k.ts(xTb[:, c, :], xTb[:, c, :], pcol[:, PC_L1G + c:PC_L1G + c + 1], pcol[:, PC_L1B + c:PC_L1B + c + 1],
                         ALU.mult, ALU.add, e="pool")
                k.dma(self.xT_view(self.x1T, t0, BLK), xTb[:], "x1st", q="pool")

    def phase_peer_conv(self, l):
        k, nc = self.k, self.nc
        with ExitStack() as es:
            def sb(name, shape, dt=F32):
                return es.enter_context(nc.sbuf_tensor(f"L{l}v_{name}", list(shape), dt))
            u32 = [sb("u32a", [128, 2048]), sb("u32b", [128, 2048])]
            v32 = [sb("v32a", [128, 2048]), sb("v32b", [128, 2048])]
            u16 = [sb("u16a", [128, 2048], BF16), sb("u16b", [128, 2048], BF16)]
            v16 = [sb("v16a", [128, 2048], BF16), sb("v16b", [128, 2048], BF16)]
            for ec in range(128):
                i = ec % 2
                k.dma(u32[i][:], self.uT_d[l, ec].rearrange("p k e -> p (k e)"), f"cu{i}")
                k.copy(u16[i][:], u32[i][:], e="act")
                k.dma(self.uTb[ec], u16[i][:], f"su{i}", q="pool")
                k.dma(v32[i][:], self.v_d[l, ec], f"cv{i}")
                k.copy(v16[i][:], v32[i][:], e="dve")
                k.dma(self.vbf[ec], v16[i][:], f"sv{i}", q="pool")

    def phase_peer(self, l, xTout):
        k, nc, ps = self.k, self.nc, self.ps
        TG = min(256, self.S)
        NT = TG // 128
        with ExitStack() as es:
            def sb(name, shape, dt=F32):
                return es.enter_context(nc.sbuf_tensor(f"L{l}e_{name}", list(shape), dt))
            skT = sb("skT", [128, 16, 128])
            ln2 = sb("ln2", [128, 4096])
            sall = [sb(f"sall{t}", [128, 16, 128]) for t in range(NT)]
            x1tok = [sb(f"x1tok{t}", [128, 2048]) for t in range(NT)]
            tau = [sb(f"tau{t}", [128, 8]) for t in range(NT)]
            biasE = [sb(f"biasE{t}", [128, 8]) for t in range(NT)]
            xtb = sb("xtb", [128, 16, TG], BF16)
            HT = [sb(f"HT{e}", [128, TG], BF16) for e in range(128)]
            st = sb("st", [128, 24])
            mv2 = sb("mv2", [128, 2])
            rstd = sb("rstd", [128, 1])
            k.dma(skT[:], self.skT_d[l], "par0")
            k.dma(ln2[:], self.prow_d[l, :, PR_L2G:PR_L2G + 4096].partition_broadcast(128), "par1")
            for grp in range(self.S // TG):
                t0 = grp * TG
                with ExitStack() as e1:
                    def s1(name, shape, dt=F32):
                        return e1.enter_context(nc.sbuf_tensor(f"L{l}e{grp}p_{name}", list(shape), dt))
                    xt = s1("xt", [128, 16, TG])
                    qT = s1("qT", [128, 16, TG])
                    wq = [s1("wq0", [128, 16, 128]), s1("wq1", [128, 16, 128])]
                    top = s1("top", [128, 2, 16])
                    tmp = s1("tmp", [128, 128])
                    cand = s1("cand", [128, 256])
                    tmp2 = s1("tmp2", [128, 256])
                    best = s1("best", [128, 16])
                    negmx = s1("negmx", [128, 1])
                    j16 = s1("j16", [128, 16])
                    Z = s1("Z", [128, 1])
                    k.dma(xt[:], self.xT_view(self.x1T, t0, TG), "xT")
                    k.copy(xtb[:], xt[:], e="pool")
                    for c in range(16):
                        w = wq[c % 2]
                        k.dma(w[:], self.wq_d[l, c], "wq" + w.name[-1])
                        bank = ps[4 + c % 2]
                        for kc in range(16):
                            k.mm(bank[:, 0:TG], w[:, kc, :], xt[:, kc, :], start=(kc == 0), stop=(kc == 15))
                        k.copy(qT[:, c, :], bank[:, 0:TG])
                    for t in range(NT):
                        ts_ = slice(t * 128, (t + 1) * 128)
                        for c in range(16):
                            k.mm(ps[c // 4][:, (c % 4) * 128:(c % 4 + 1) * 128], qT[:, c, ts_], skT[:, c, :])
                        for q in range(4):
                            k.copy(sall[t][:, q * 4:(q + 1) * 4, :], ps[q][:, 0:512].rearrange("p (a b) -> p a b", a=4))
                        for c in range(16):
                            k.tr(ps[4 + (c // 4) % 2][:, (c % 4) * 128:(c % 4 + 1) * 128], xt[:, c, ts_], self.ident)
                            if c % 4 == 3:
                                q = c // 4
                                k.copy(x1tok[t][:, q * 512:(q + 1) * 512], ps[4 + q % 2][:, 0:512], e="dve")
                        for h in range(8):
                            for half in range(2):
                                sv = sall[t][:, 2 * h + half, :]
                                k.op("dve", "max", top[:, half, 0:8], sv, R=[sall[t]], W=[top])
                                k.op("dve", "match_replace", tmp[:], top[:, half, 0:8], sv, -1e30, R=[top, sall[t]], W=[tmp])
                                k.op("dve", "max", top[:, half, 8:16], tmp[:], R=[tmp], W=[top])
                            k.tt(cand[:].rearrange("p (a b) -> p a b", a=16), bc(top[:, 0, :], [128, 16, 16], 2),
                                 bc(top[:, 1, :], [128, 16, 16], 1), ALU.add)
                            k.op("dve", "max", best[:, 0:8], cand[:], R=[cand], W=[best])
                            k.op("dve", "match_replace", tmp2[:], best[:, 0:8], cand[:], -1e30, R=[best, cand], W=[tmp2])
                            k.op("dve", "max", best[:, 8:16], tmp2[:], R=[tmp2], W=[best])
                            k.ts(negmx[:], best[:, 0:1], -1.0, None, ALU.mult)
                            k.act(j16[:], best[:], AF.Exp, bias=negmx[:], accum_out=Z[:])
                            k.act(Z[:], Z[:], AF.Ln)
                            k.tt(biasE[t][:, h:h + 1], negmx[:], Z[:], ALU.subtract)
                            k.copy(tau[t][:, h:h + 1], best[:, 15:16], e="dve")
                k.barrier()
                with ExitStack() as e2:
                    def s2(name, shape, dt=F32):
                        return e2.enter_context(nc.sbuf_tensor(f"L{l}e{grp}s_{name}", list(shape), dt))
                    Lb = [s2(f"Lb{i}", [128, 8, 128]) for i in range(3)]
                    Eb = [s2(f"Eb{i}", [128, 8, 128]) for i in range(3)]
                    Mb = [s2("Mb0", [128, 8, 128]), s2("Mb1", [128, 8, 128])]
                    Gacc = [[s2(f"G{t}a", [128, 8, 128]), s2(f"G{t}b", [128, 8, 128])] for t in range(NT)]
                    ut = [s2(f"ut{i}", [128, 16, 128], BF16) for i in range(3)]
                    vh = [s2(f"vh{i}", [128, 1536], BF16) for i in range(3)]
                    ga = [s2("ga0", [128, TG]), s2("ga1", [128, TG])]
                    ne = 0
                    pend = None

                    def obank(t, q):
                        if q < 2:
                            return ps[t * 2 + q]
                        return ps[6 + t] if q == 2 else ps[4 + t]

                    def emit_out(ec, v_):
                        for t in range(NT):
                            for q in range(3):
                                k.mm(obank(t, q)[:, 0:512], HT[ec][:, t * 128:(t + 1) * 128], v_[:, q * 512:(q + 1) * 512],
                                     start=(ec == 0), stop=(ec == 127))

                    def gunit(ib, u):
                        t, h = u // 8, u % 8
                        G = Gacc[t][ib % 2]
                        nbl[0] += 1
                        nb = nbl[0]
                        Lh, Eh, Mh = Lb[nb % 3], Eb[nb % 3], Mb[nb % 2]
                        k.tt(Lh[:], bc(sall[t][:, 2 * h, ib * 8:(ib + 1) * 8], [128, 8, 128], 2),
                             bc(sall[t][:, 2 * h + 1, :], [128, 8, 128], 1), ALU.add, e="pool")
                        k.act(Eh[:], Lh[:], AF.Exp, bias=biasE[t][:, h:h + 1])
                        if h == 0:
                            k.stt(G[:], Lh[:], tau[t][:, h:h + 1], Eh[:], ALU.is_ge, ALU.mult)
                        else:
                            k.stt(Mh[:], Lh[:], tau[t][:, h:h + 1], Eh[:], ALU.is_ge, ALU.mult)
                            k.tt(G[:], G[:], Mh[:], ALU.add, e="dve")

                    nbl = [0]
                    NU = NT * 8
                    for u in range(NU):
                        gunit(0, u)
                    for ib in range(16):
                        for i in range(8):
                            if ib + 1 < 16 and i % 2 == 0:
                                for u in range(i * NU // 8, (i + 2) * NU // 8):
                                    gunit(ib + 1, u)
                            ec = ib * 8 + i
                            ne += 1
                            u_, v_ = ut[ne % 3], vh[ne % 3]
                            k.dma(u_[:], self.uTb[ec].rearrange("p (k e) -> p k e", k=16), "ut" + u_.name[-1])
                            k.dma(v_[:], self.vbf[ec][:, 0:1536], "vh" + v_.name[-1])
                            ab = ps[4 + ec % 2]
                            for t in range(NT):
                                k.tr(ab[:, 256 + t * 128:256 + (t + 1) * 128], Gacc[t][ib % 2][:, i, :], self.ident)
                            for kc in range(16):
                                k.mm(ab[:, 0:TG], u_[:, kc, :], xtb[:, kc, :], start=(kc == 0), stop=(kc == 15))
                            g_ = ga[ec % 2]
                            k.act(g_[:], ab[:, 0:TG], AF.Gelu)
                            k.tt(HT[ec][:], g_[:], ab[:, 256:256 + TG], ALU.mult)
                            if pend is not None:
                                emit_out(*pend)
                            pend = (ec, v_)
                    emit_out(*pend)
                    for ec in range(128):
                        ne += 1
                        v_ = vh[ne % 3]
                        k.dma(v_[:, 0:512], self.vbf[ec][:, 1536:2048], "vh" + v_.name[-1])
                        for t in range(NT):
                            k.mm(obank(t, 3)[:, 0:512], HT[ec][:, t * 128:(t + 1) * 128], v_[:, 0:512],
                                 start=(ec == 0), stop=(ec == 127))
                k.barrier()
                with ExitStack() as e3:
                    xTn = None
                    if xTout is not None:
                        xTn = e3.enter_context(nc.sbuf_tensor(f"L{l}e{grp}x_xTn", [128, 16, TG], F32))
                    for t in range(NT):
                        xk = x1tok[t]
                        for q in range(4):
                            bank = obank(t, q)
                            k.stt(xk[:, q * 512:(q + 1) * 512], xk[:, q * 512:(q + 1) * 512], ALPHA, bank[:, 0:512],
                                  ALU.mult, ALU.add)
                            k.op("dve", "bn_stats", st[:, q * 6:(q + 1) * 6], xk[:, q * 512:(q + 1) * 512], R=[xk], W=[st])
                        k.op("dve", "bn_aggr", mv2[:], st[:], R=[st], W=[mv2])
                        k.act(rstd[:], mv2[:, 1:2], AF.Sqrt, bias=1e-5)
                        k.recip(rstd[:], rstd[:])
                        k.ts(xk[:], xk[:], mv2[:, 0:1], rstd[:], ALU.subtract, ALU.mult)
                        k.tt(xk[:], xk[:], ln2[:, 0:2048], ALU.mult)
                        k.tt(xk[:], xk[:], ln2[:, 2048:4096], ALU.add, e="pool")
                        if xTout is None:
                            k.dma(self.out_d[t0 + t * 128:t0 + (t + 1) * 128, :], xk[:], "ost", q="sp")
                        else:
                            for c in range(16):
                                k.tr(ps[(c // 4) % 2][:, (c % 4) * 128:(c % 4 + 1) * 128], xk[:, c * 128:(c + 1) * 128],
                                     self.ident)
                                if c % 4 == 3:
                                    q = c // 4
                                    k.copy(xTn[:, q * 4:(q + 1) * 4, t * 128:(t + 1) * 128],
                                           ps[q % 2][:, 0:512].rearrange("p (a b) -> p a b", a=4))
                    if xTout is not None:
                        k.dma(self.xT_view(xTout, t0, TG), xTn[:], "ost", q="sp")
                k.barrier()


def make_consts():
    c = np.zeros((128, NCONST), np.float32)
    r = np.arange(128)
    c[:, C_ID:C_ID + 128] = np.eye(128)
    c[:, C_TRI:C_TRI + 128] = (r[:, None] <= r[None, :])
    c[:, C_AM:C_AM + 128] = (r[:, None] > r[None, :])
    c[:, C_ONE:C_ONE + 128] = 1.0
    c[:, C_OND:C_OND + 128] = 1.0 / D
    nm = np.where(r[:, None] > r[None, :], NEG, 0.0)
    c[:, C_NM4:C_NM4 + 512] = np.tile(nm, (1, 4))
    prevm = np.where(r[None, :] > r[:, None], 0.0, NEG)
    curm = np.where(r[None, :] <= r[:, None], 0.0, NEG)
    c[:, C_MA:C_MA + 256] = np.concatenate([prevm, curm], 1)
    c[:, C_M0:C_M0 + 256] = np.concatenate([np.full((128, 128), NEG), curm], 1)
    return c


def make_rope(S):
    half = 32
    freqs = (np.float32(10000.0) ** (-np.arange(half, dtype=np.float32) / np.float32(half))).astype(np.float32)
    ang = np.arange(S, dtype=np.float32)[:, None] * freqs[None, :]
    return np.concatenate([np.cos(ang), np.sin(ang)], 1).astype(np.float32)


def tile_w(w, cols, tw):
    ws = w[:, cols]
    n = ws.shape[1] // tw
    return np.ascontiguousarray(ws.reshape(16, 128, n, tw).transpose(2, 1, 0, 3))


def prep_weights(inp, NL):
    f = lambda a: np.asarray(a, dtype=np.float32)
    w_in = inp["w_in"]
    out = {}
    out["wf"] = np.stack([tile_w(f(w_in[l]), FCOLS, 128) for l in range(NL)])
    out["wt"] = np.stack([tile_w(f(w_in[l]), TCOLS, 512) for l in range(NL)])
    out["wsm"] = np.stack([tile_w(f(w_in[l]), SCOLS, 24)[0] for l in range(NL)])
    pcol = np.zeros((NL, 128, NPCOL), np.float32)
    prow = np.zeros((NL, 1, NPROW), np.float32)
    for l in range(NL):
        cw = f(inp["ssm_conv_w"][l])[:, 0, :]
        pcol[l, :, PC_CW:PC_CW + 48] = cw.reshape(4, 12, 128).transpose(2, 1, 0).reshape(128, 48)
        pcol[l, :, PC_CB:PC_CB + 12] = f(inp["ssm_conv_b"][l]).reshape(12, 128).T
        pcol[l, :, PC_GB:PC_GB + 48] = f(inp["merge_gate_b"][l]).reshape(48, 128).T
        pcol[l, :, PC_L1G:PC_L1G + 16] = f(inp["ln1_g"][l]).reshape(16, 128).T
        pcol[l, :, PC_L1B:PC_L1B + 16] = f(inp["ln1_b"][l]).reshape(16, 128).T
        prow[l, 0, PR_MN:PR_MN + 1024] = f(inp["mlstm_norm_w"][l])
        prow[l, 0, PR_SN:PR_SN + 1024] = f(inp["ssm_norm_w"][l])
        prow[l, 0, PR_AL:PR_AL + 16] = f(inp["ssm_a_log"][l])
        prow[l, 0, PR_DS:PR_DS + 16] = f(inp["ssm_d"][l])
        prow[l, 0, PR_SK:PR_SK + 16] = f(inp["swa_sinks"][l])
        prow[l, 0, PR_L2G:PR_L2G + 2048] = f(inp["ln2_g"][l])
        prow[l, 0, PR_L2B:PR_L2B + 2048] = f(inp["ln2_b"][l])
        prow[l, 0, PR_BI:PR_BI + 24] = np.concatenate([f(inp["mlstm_gate_b"][l, 0]), f(inp["mlstm_gate_b"][l, 1]),
                                                       f(inp["ssm_dt_bias"][l])])
    out["pcol"] = pcol
    out["prow"] = prow
    wb = f(inp["w_branch"][:NL])
    out["wb"] = np.ascontiguousarray(wb.reshape(NL, 3, 8, 128, 16, 128).transpose(0, 1, 4, 3, 2, 5))
    all_cols = np.arange(2048)
    out["wo"] = np.stack([tile_w(f(inp["w_out"][l]), all_cols, 128) for l in range(NL)])
    out["wq"] = np.stack([tile_w(f(inp["peer_wq"][l]), all_cols, 128) for l in range(NL)])
    sk = f(inp["peer_subkeys"][:NL])
    out["skT"] = np.ascontiguousarray(sk.reshape(NL, 16, 128, 128).transpose(0, 3, 1, 2))
    u = f(inp["peer_u"][:NL])
    out["uT"] = np.ascontiguousarray(u.reshape(NL, 128, 128, 16, 128).transpose(0, 1, 4, 3, 2))
    out["pv"] = np.ascontiguousarray(f(inp["peer_v"][:NL]).reshape(NL, 128, 128, 2048))
    return out


_CACHE = {}
PHASES = set("abcdve")


def run(inp, S, NL, B, dbg=False, trace=False):
    key = (S, NL, dbg)
    prog = Prog(S, NL, dbg)
    nc = prog.build()
    wts = prep_weights(inp, NL)
    wts["consts"] = make_consts()
    wts["rope"] = make_rope(S)
    x = np.asarray(inp["x"], dtype=np.float32)
    in_maps = []
    for b in range(B):
        m = dict(wts)
        m["xT0"] = np.ascontiguousarray(x[b].T)
        in_maps.append(m)
    res = run_bass_kernel_spmd(nc, in_maps, core_ids=list(range(B)), trace=trace)
    return res, prog


def kernel(**inputs):
    S = inputs["x"].shape[1]
    B = inputs["x"].shape[0]
    res, _ = run(inputs, S, DEPTH, B)
    return np.stack([np.asarray(r["out"]) for r in res.results], 0).astype(np.float32)
```

```python
import numpy as np
from contextlib import ExitStack
import concourse.bass as bass
import concourse.mybir as mybir
from concourse.alu_op_type import AluOpType as ALU
from concourse.bass_utils import run_bass_kernel_spmd

F32 = mybir.dt.float32
BF16 = mybir.dt.bfloat16
AF = mybir.ActivationFunctionType
AX = mybir.AxisListType

D = 2048
DEPTH = 4
NCORES = 8
ALPHA = (2.0 * DEPTH) ** 0.25
NEG = -30000.0
KSC = 128.0 ** -0.5

_sizes = [512, 512, 1024, 1024, 4, 4, 1024, 1536, 16, 1024, 256, 256, 6144]
_off = np.concatenate([[0], np.cumsum(_sizes)]).astype(int)
(O_MQ, O_MK, O_MV, O_MO, O_MI, O_MF, O_SZ, O_XBC, O_DT, O_AQ, O_AK, O_AV, O_G) = [int(v) for v in _off[:13]]
FCOLS = np.concatenate([np.arange(O_MQ, O_MQ + 512), np.arange(O_MK, O_MK + 512),
                        np.arange(O_XBC, O_XBC + 1536), np.arange(O_G, O_G + 6144)])
TCOLS = np.concatenate([np.arange(O_MK, O_MK + 512), np.arange(O_MV, O_MV + 1024), np.arange(O_MO, O_MO + 1024),
                        np.arange(O_SZ, O_SZ + 1024), np.arange(O_AQ, O_AQ + 1024), np.arange(O_AK, O_AK + 256),
                        np.arange(O_AV, O_AV + 256)])
SCOLS = np.concatenate([np.arange(O_MI, O_MI + 4), np.arange(O_MF, O_MF + 4), np.arange(O_DT, O_DT + 16)])

C_ID, C_TRI, C_AM, C_ONE, C_OND, C_NM4, C_MA, C_M0, NCONST = 0, 128, 256, 384, 512, 640, 1152, 1408, 1664
PC_CW, PC_CB, PC_GB, PC_L1G, PC_L1B, NPCOL = 0, 48, 60, 108, 124, 140
PR_MN, PR_SN, PR_AL, PR_DS, PR_SK, PR_L2G, PR_L2B, PR_BI, NPROW = 0, 1024, 2048, 2064, 2080, 2096, 4144, 6192, 6216


class KB:
    def __init__(self, nc):
        self.nc = nc
        self.eng = {"pe": nc.tensor, "act": nc.scalar, "dve": nc.vector, "pool": nc.gpsimd, "sp": nc.sync}
        self.sem = {}
        self.cnt = {}
        for e in self.eng:
            self.sem[e] = nc.alloc_semaphore(name="s_" + e)
            self.cnt[e] = 0
        self.seen = {e: {} for e in self.eng}
        self.W = {}
        self.R = {}
        self.dsem = {}
        self.ninstr = 0

    def _wait(self, e, deps):
        seen = self.seen[e]
        for sid, (sh, val) in deps.items():
            if seen.get(sid, 0) >= val:
                continue
            self.eng[e].wait_ge(sh, val)
            seen[sid] = val

    def _deps(self, e, reads, writes, selfsid):
        deps = {}

        def add(d):
            for sid, (sh, val) in d.items():
                if sid == selfsid and e == "pe":
                    continue
                if sid not in deps or deps[sid][1] < val:
                    deps[sid] = (sh, val)

        for k in reads:
            add(self.W.get(k, {}))
        for k in writes:
            add(self.W.get(k, {}))
            add(self.R.get(k, {}))
        return deps

    def _commit(self, reads, writes, sid, sh, val):
        for k in reads:
            self.R.setdefault(k, {})[sid] = (sh, val)
        for k in writes:
            self.W[k] = {sid: (sh, val)}
            self.R[k] = {}

    @staticmethod
    def _keys(xs):
        return [x if isinstance(x, str) else x.name for x in xs]

    def op(self, e, meth, *args, R=(), W=(), **kw):
        reads = self._keys(R)
        writes = self._keys(W)
        sid = "E" + e
        self._wait(e, self._deps(e, reads, writes, sid))
        ins = getattr(self.eng[e], meth)(*args, **kw)
        self.cnt[e] += 1
        ins.then_inc(self.sem[e], 1)
        self._commit(reads, writes, sid, self.sem[e], self.cnt[e])
        self.ninstr += 1
        return ins

    def dma(self, out, in_, slot, q="sp"):
        if slot not in self.dsem:
            self.dsem[slot] = [self.nc.alloc_semaphore(name="d_" + slot), 0]
        sh, val = self.dsem[slot]
        reads = [in_.name]
        writes = [out.name]
        sid = "D" + slot
        deps = self._deps(q, reads, writes, sid)
        if val > 0:
            deps[sid] = (sh, val)
        self._wait(q, deps)
        ins = self.eng[q].dma_start(out=out, in_=in_)
        val += 16
        ins.then_inc(sh, 16)
        self.dsem[slot][1] = val
        self._commit(reads, writes, sid, sh, val)
        self.ninstr += 1
        return ins

    def barrier(self):
        deps = {}
        for e in self.eng:
            if self.cnt[e] > 0:
                deps["E" + e] = (self.sem[e], self.cnt[e])
        for slot, (sh, val) in self.dsem.items():
            if val > 0:
                deps["D" + slot] = (sh, val)
        for e in self.eng:
            self._wait(e, deps)

    def mm(self, out, lhsT, rhs, start=True, stop=True):
        return self.op("pe", "matmul", out, lhsT, rhs, start=start, stop=stop, R=[lhsT, rhs], W=[out])

    def tr(self, out, in_, ident):
        return self.op("pe", "transpose", out, in_, ident, R=[in_, ident], W=[out])

    def act(self, out, in_, func, bias=None, scale=None, accum_out=None):
        kw = {}
        rs = [in_]
        ws = [out]
        if bias is not None:
            kw["bias"] = bias
            if not isinstance(bias, (int, float)):
                rs.append(bias)
        if scale is not None:
            kw["scale"] = scale
            if not isinstance(scale, (int, float)):
                rs.append(scale)
        if accum_out is not None:
            kw["accum_out"] = accum_out
            ws.append(accum_out)
        return self.op("act", "activation", out, in_, func, R=rs, W=ws, **kw)

    def tt(self, out, in0, in1, op, e="dve"):
        return self.op(e, "tensor_tensor", out, in0, in1, op, R=[in0, in1], W=[out])

    def ts(self, out, in0, s1, s2, op0, op1=None, e="dve"):
        rs = [in0] + [s for s in (s1, s2) if s is not None and not isinstance(s, (int, float))]
        if op1 is None:
            return self.op(e, "tensor_scalar", out, in0, s1, None, op0, R=rs, W=[out])
        return self.op(e, "tensor_scalar", out, in0, s1, s2, op0, op1, R=rs, W=[out])

    def stt(self, out, in0, scalar, in1, op0, op1):
        rs = [in0, in1] + ([] if isinstance(scalar, (int, float)) else [scalar])
        return self.op("dve", "scalar_tensor_tensor", out, in0, scalar, in1, op0, op1, R=rs, W=[out])

    def copy(self, out, in_, e="act"):
        if e == "act":
            return self.op("act", "copy", out, in_, R=[in_], W=[out])
        return self.op(e, "tensor_copy", out, in_, R=[in_], W=[out])

    def mul(self, out, in_, c):
        return self.op("act", "mul", out, in_, c, R=[in_], W=[out])

    def memset(self, ap, val, e="dve"):
        return self.op(e, "memset", ap, val, R=[], W=[ap])

    def recip(self, out, in_):
        return self.op("dve", "reciprocal", out, in_, R=[in_], W=[out])


def bc(ap, shape, axis):
    return ap.unsqueeze(axis).to_broadcast(list(shape))


class Prog:
    def __init__(self, S, nlayers, dbg=False):
        self.S = S
        self.NL = nlayers
        self.dbg = dbg
        self.BLK = min(512, S)
        self.NCH = S // 128
        nc = self.nc = bass.Bass("TRN2", target_bir_lowering=False)
        self.k = KB(nc)
        L = nlayers

        def din(name, shape):
            return nc.dram_tensor(name, list(shape), F32, kind="ExternalInput").ap()

        self.xT0 = din("xT0", [D, S])
        self.consts_d = din("consts", [128, NCONST])
        self.rope_d = din("rope", [S, 64])
        self.wf_d = din("wf", [L, 68, 128, 16, 128])
        self.wt_d = din("wt", [L, 10, 128, 16, 512])
        self.wsm_d = din("wsm", [L, 128, 16, 24])
        self.pcol_d = din("pcol", [L, 128, NPCOL])
        self.prow_d = din("prow", [L, 1, NPROW])
        self.wb_d = din("wb", [L, 3, 16, 128, 8, 128])
        self.wo_d = din("wo", [L, 16, 128, 16, 128])
        self.wq_d = din("wq", [L, 16, 128, 16, 128])
        self.skT_d = din("skT", [L, 128, 16, 128])
        self.uT_d = din("uT", [L, 128, 128, 16, 128])
        self.v_d = din("pv", [L, 128, 128, 2048])
        self.out_d = nc.dram_tensor("out", [S, D], F32, kind="ExternalOutput").ap()
        self.xTa = nc.dram_tensor("xTa", [D, S], F32, kind="Internal").ap()
        self.xTb_ = nc.dram_tensor("xTbb", [D, S], F32, kind="Internal").ap()
        self.x1T = nc.dram_tensor("x1T", [D, S], F32, kind="Internal").ap()
        self.yT = nc.dram_tensor("yT", [self.NCH, 128, 24, 128], F32, kind="Internal").ap()
        self.uTb = nc.dram_tensor("uTb", [128, 128, 2048], BF16, kind="Internal").ap()
        self.wfb = nc.dram_tensor("wfb", [68, 128, 2048], BF16, kind="Internal").ap()
        self.wtb = nc.dram_tensor("wtb", [10, 128, 8192], BF16, kind="Internal").ap()
        self.wbb = nc.dram_tensor("wbb", [48, 128, 1024], BF16, kind="Internal").ap()
        self.wob = nc.dram_tensor("wob", [16, 128, 2048], BF16, kind="Internal").ap()
        self.vbf = nc.dram_tensor("vbf", [128, 128, 2048], BF16, kind="Internal").ap()
        if dbg:
            self.dbg_y = nc.dram_tensor("dbg_y", [L, self.NCH, 128, 24, 128], F32, kind="ExternalOutput").ap()
            self.dbg_x1T = nc.dram_tensor("dbg_x1T", [L, D, S], F32, kind="ExternalOutput").ap()
        self.cst = nc.alloc_sbuf_tensor("cst", [128, NCONST], F32)
        self.ps = [nc.alloc_psum_tensor(f"ps{i}", [128, 512], F32) for i in range(8)]
        k = self.k
        k.dma(self.cst[:], self.consts_d[:], "cst")
        c = self.cst
        self.ident = c[:, C_ID:C_ID + 128]
        self.tri = c[:, C_TRI:C_TRI + 128]
        self.amat = c[:, C_AM:C_AM + 128]
        self.ones = c[:, C_ONE:C_ONE + 128]
        self.onesD = c[:, C_OND:C_OND + 128]
        self.nm4 = c[:, C_NM4:C_NM4 + 512]
        self.maskA = c[:, C_MA:C_MA + 256]
        self.mask0 = c[:, C_M0:C_M0 + 256]

    def build(self):
        k = self.k
        xin = self.xT0
        bufs = [self.xTa, self.xTb_]
        for l in range(self.NL):
            last = (l == self.NL - 1)
            xout = None if last else bufs[l % 2]
            self.phase_wconv(l)
            k.barrier()
            if "a" in PHASES:
                self.phase_mlstm(l, xin)
                k.barrier()
            if "b" in PHASES:
                self.phase_ssd(l, xin)
                k.barrier()
            if "c" in PHASES:
                self.phase_swa(l, xin)
                k.barrier()
            if self.dbg:
                k.dma(self.dbg_y[l], self.yT, "dbgy", q="pool")
            if "d" in PHASES:
                self.phase_mix(l, xin)
                k.barrier()
            if self.dbg:
                k.dma(self.dbg_x1T[l], self.x1T, "dbgx", q="pool")
            if "v" in PHASES:
                self.phase_peer_conv(l)
                k.barrier()
            if "e" in PHASES:
                self.phase_peer(l, xout)
                k.barrier()
            xin = xout
        deps = {}
        for key in ["out"] + (["dbg_y", "dbg_x1T"] if self.dbg else []):
            for sid, (sh, val) in k.W.get(key, {}).items():
                deps[sid] = (sh, val)
        k._wait("sp", deps)
        return self.nc

    def xT_view(self, xT, t0, n):
        return xT.rearrange("(kc p) t -> p kc t", p=128)[:, :, t0:t0 + n]

    def fmaj(self, l, c, w, bank, xTb, BLK):
        k = self.k
        k.dma(w[:], self.wfb[c].rearrange("p (k e) -> p k e", k=16), "wf" + w.name[-1])
        for kc in range(16):
            k.mm(bank[:, 0:BLK], w[:, kc, :], xTb[:, kc, :], start=(kc == 0), stop=(kc == 15))

    def tmaj(self, l, ti, w, bank, xTb, cs):
        k = self.k
        k.dma(w[:], self.wtb[ti].rearrange("p (k e) -> p k e", k=16), "wt" + w.name[-1])
        for kc in range(16):
            k.mm(bank[:, 0:512], xTb[:, kc, cs], w[:, kc, :], start=(kc == 0), stop=(kc == 15))

    def small_proj(self, wsm, xTb, cs, bank, small, biasb):
        k = self.k
        for kc in range(16):
            k.mm(bank[:, 0:24], xTb[:, kc, cs], wsm[:, kc, :], start=(kc == 0), stop=(kc == 15))
        k.tt(small[:], bank[:, 0:24], biasb, ALU.add)

    def emit_yT(self, ysrc, g, col0, yTc, banks):
        k = self.k
        for c in range(8):
            k.tr(banks[c // 4][:, (c % 4) * 128:(c % 4 + 1) * 128], ysrc[:, c * 128:(c + 1) * 128], self.ident)
        for hh in range(2):
            k.copy(yTc[:, hh * 4:(hh + 1) * 4, :], banks[hh][:, 0:512].rearrange("p (a b) -> p a b", a=4))
        k.dma(self.yT[g, :, col0:col0 + 8, :], yTc[:], "yst", q="pool")

    def phase_wconv(self, l):
        k, nc = self.k, self.nc
        items = []
        for c in range(68):
            items.append((self.wf_d[l, c].rearrange("p k e -> p (k e)"), self.wfb[c], 2048))
        for t in range(10):
            items.append((self.wt_d[l, t].rearrange("p k e -> p (k e)"), self.wtb[t], 8192))
        for kk in range(3):
            for c in range(16):
                items.append((self.wb_d[l, kk, c].rearrange("p k e -> p (k e)"), self.wbb[kk * 16 + c], 1024))
        for c in range(16):
            items.append((self.wo_d[l, c].rearrange("p k e -> p (k e)"), self.wob[c], 2048))
        with ExitStack() as es:
            s32 = [es.enter_context(nc.sbuf_tensor(f"L{l}w_s32{i}", [128, 8192], F32)) for i in range(2)]
            s16 = [es.enter_context(nc.sbuf_tensor(f"L{l}w_s16{i}", [128, 8192], BF16)) for i in range(2)]
            for n, (src, dst, w) in enumerate(items):
                i = n % 2
                k.dma(s32[i][:, 0:w], src, f"cw{i}")
                k.copy(s16[i][:, 0:w], s32[i][:, 0:w], e=("act" if i else "dve"))
                k.dma(dst, s16[i][:, 0:w], f"sw{i}", q="pool")

    def phase_mlstm(self, l, xT):
        k, nc, ps = self.k, self.nc, self.ps
        BLK = self.BLK
        with ExitStack() as es:
            def sb(name, shape):
                return es.enter_context(nc.sbuf_tensor(f"L{l}a_{name}", list(shape), F32))
            xTb = sb("xTb", [128, 16, BLK])
            mqT = sb("mqT", [128, 4, BLK])
            mkT = sb("mkT", [128, 4, BLK])
            wf = [es.enter_context(nc.sbuf_tensor(f"L{l}a_wf{i}", [128, 16, 128], BF16)) for i in range(2)]
            wt = [es.enter_context(nc.sbuf_tensor(f"L{l}a_wt{i}", [128, 16, 512], BF16)) for i in range(2)]
            xTh = es.enter_context(nc.sbuf_tensor(f"L{l}a_xTh", [128, 16, BLK], BF16))
            wsm = sb("wsm", [128, 16, 24])
            mktok = sb("mktok", [128, 512])
            mvext = sb("mvext", [128, 4, 257])
            osig = sb("osig", [128, 1024])
            small = sb("small", [128, 24])
            normw = sb("normw", [128, 1024])
            biasb = sb("biasb", [128, 24])
            Cext = [sb(f"C{h}", [128, 257]) for h in range(4)]
            e1 = sb("e1", [128, 4])
            lf = sb("lf", [128, 4])
            Bmat = sb("Bmat", [128, 4, 128])
            Dt = sb("Dt", [128, 4, 128])
            EB = sb("EB", [128, 4, 128])
            PT = sb("PT", [128, 128])
            qsT = sb("qsT", [128, 128])
            den = sb("den", [128, 1])
            hb = sb("hb", [128, 256])
            st6 = sb("st6", [128, 6])
            mv2 = sb("mv2", [128, 2])
            rstd = sb("rstd", [128, 1])
            kw = sb("kw", [128, 128])
            ym = sb("ym", [128, 1024])
            yTc = sb("yTc", [128, 8, 128])

            k.dma(wsm[:], self.wsm_d[l], "par0")
            k.dma(normw[:], self.prow_d[l, :, PR_MN:PR_MN + 1024].partition_broadcast(128), "par1")
            k.dma(biasb[:], self.prow_d[l, :, PR_BI:PR_BI + 24].partition_broadcast(128), "par2")
            for h in range(4):
                k.memset(Cext[h][:], 0.0)
            k.memset(mvext[:], 1.0)
            for b in range(self.S // BLK):
                t0 = b * BLK
                k.dma(xTb[:], self.xT_view(xT, t0, BLK), "xT")
                k.copy(xTh[:], xTb[:], e="pool")
                for c in range(8):
                    bank = ps[c % 2]
                    self.fmaj(l, c, wf[c % 2], bank, xTh, BLK)
                    if c < 4:
                        k.copy(mqT[:, c, :], bank[:, 0:BLK])
                    else:
                        k.mul(mkT[:, c - 4, :], bank[:, 0:BLK], KSC)
                for j in range(BLK // 128):
                    g = (t0 // 128) + j
                    cs = slice(j * 128, (j + 1) * 128)
                    for ti in range(5):
                        bank = ps[2 + ti % 2]
                        self.tmaj(l, ti, wt[ti % 2], bank, xTh, cs)
                        if ti == 0:
                            k.copy(mktok[:], bank[:, 0:512])
                        elif ti < 3:
                            k.copy(mvext[:, 2 * (ti - 1):2 * ti, 0:256], bank[:, 0:512].rearrange("p (a b) -> p a b", a=2))
                        else:
                            k.act(osig[:, (ti - 3) * 512:(ti - 2) * 512], bank[:, 0:512], AF.Sigmoid)
                    self.small_proj(wsm, xTb, cs, ps[4], small, biasb[:])
                    k.act(e1[:], small[:, 4:8], AF.Exp, scale=-1.0)
                    k.act(e1[:], e1[:], AF.Ln, bias=1.0)
                    k.mul(lf[:], e1[:], -1.0)
                    k.tt(Bmat[:], bc(lf[:], [128, 4, 128], 2), bc(self.tri, [128, 4, 128], 1), ALU.mult)
                    Bm2 = Bmat[:].rearrange("p h t -> p (h t)")
                    k.mm(ps[5][:, 0:512], self.amat, Bm2, start=True, stop=False)
                    k.mm(ps[5][:, 0:512], self.ident, self.nm4, start=False, stop=True)
                    k.mm(ps[6][:, 0:512], self.ones, Bm2)
                    for h in range(4):
                        k.act(Dt[:, h, :], ps[5][:, h * 128:(h + 1) * 128], AF.Exp, bias=small[:, h:h + 1])
                    k.act(EB[:].rearrange("p h t -> p (h t)"), ps[6][:, 0:512], AF.Exp)
                    for h in range(4):
                        k.mm(ps[7][:, 0:128], mkT[:, h, cs], mqT[:, h, cs])
                        k.tt(PT[:], ps[7][:, 0:128], Dt[:, h, :], ALU.mult)
                        k.tt(qsT[:], mqT[:, h, cs], EB[:, h, :], ALU.mult, e="pool")
                        nm = ps[h % 2]
                        k.mm(nm[:, 0:257], PT[:], mvext[:, h, :], start=True, stop=False)
                        k.mm(nm[:, 0:257], qsT[:], Cext[h][:], start=False, stop=True)
                        k.act(den[:], nm[:, 256:257], AF.Abs)
                        k.ts(den[:], den[:], 1.0, None, ALU.max)
                        k.recip(den[:], den[:])
                        k.ts(hb[:], nm[:, 0:256], den[:], None, ALU.mult)
                        k.op("dve", "bn_stats", st6[:], hb[:], R=[hb], W=[st6])
                        k.op("dve", "bn_aggr", mv2[:], st6[:], R=[st6], W=[mv2])
                        k.act(rstd[:], mv2[:, 1:2], AF.Sqrt, bias=1e-6)
                        k.recip(rstd[:], rstd[:])
                        k.ts(hb[:], hb[:], mv2[:, 0:1], rstd[:], ALU.subtract, ALU.mult)
                        k.tt(hb[:], hb[:], normw[:, h * 256:(h + 1) * 256], ALU.mult)
                        k.tt(ym[:, h * 256:(h + 1) * 256], hb[:], osig[:, h * 256:(h + 1) * 256], ALU.mult, e="pool")
                        k.ts(kw[:], mktok[:, h * 128:(h + 1) * 128], Dt[:, h, 127:128], KSC, ALU.mult, ALU.mult)
                        cu = ps[2 + h % 2]
                        k.mm(cu[:, 0:257], kw[:], mvext[:, h, :])
                        k.stt(Cext[h][:], Cext[h][:], EB[:, h, 127:128], cu[:, 0:257], ALU.mult, ALU.add)
                    self.emit_yT(ym, g, 0, yTc, [ps[4], ps[5]])

    def phase_ssd(self, l, xT):
        k, nc, ps = self.k, self.nc, self.ps
        BLK = self.BLK
        with ExitStack() as es:
            def sb(name, shape):
                return es.enter_context(nc.sbuf_tensor(f"L{l}b_{name}", list(shape), F32))
            xTb = sb("xTb", [128, 16, BLK])
            xbcT = sb("xbcT", [128, 12, BLK])
            u = sb("u", [128, BLK + 3])
            halo = sb("halo", [128, 12, 3])
            wf = [es.enter_context(nc.sbuf_tensor(f"L{l}b_wf{i}", [128, 16, 128], BF16)) for i in range(2)]
            wt = [es.enter_context(nc.sbuf_tensor(f"L{l}b_wt{i}", [128, 16, 512], BF16)) for i in range(2)]
            xTh = es.enter_context(nc.sbuf_tensor(f"L{l}b_xTh", [128, 16, BLK], BF16))
            wsm = sb("wsm", [128, 16, 24])
            pcol = sb("pcol", [128, NPCOL])
            prw = sb("prw", [128, 1024 + 48])
            biasb = sb("biasb", [128, 24])
            zsil = sb("zsil", [128, 1024])
            small = sb("small", [128, 24])
            dt = sb("dt", [128, 16])
            aneg = sb("aneg", [128, 16])
            dtA = sb("dtA", [128, 16])
            ex = sb("ex", [128, 48])
            Bm = sb("Bm", [128, 16, 128])
            dec = sb("dec", [128, 16, 128])
            MT = sb("MT", [128, 16, 128])
            xstok = sb("xstok", [128, 16, 64])
            Btok = sb("Btok", [128, 2, 128])
            xdt = sb("xdt", [128, 16, 64])
            xw = sb("xw", [128, 16, 64])
            t1 = sb("t1", [128, 16, 64])
            t3 = sb("t3", [128, 16, 64])
            Dfull = sb("Dfull", [128, 16, 64])
            H = [sb("H0", [128, 8, 64]), sb("H1", [128, 8, 64])]
            ss = sb("ss", [128, 2])
            junk = sb("junk", [128, 512])
            ys = sb("ys", [128, 1024])
            yTc = sb("yTc", [128, 8, 128])

            k.dma(wsm[:], self.wsm_d[l], "par0")
            k.dma(pcol[:], self.pcol_d[l], "par1")
            k.dma(prw[:], self.prow_d[l, :, PR_SN:PR_SN + 1072].partition_broadcast(128), "par2")
            k.dma(biasb[:], self.prow_d[l, :, PR_BI:PR_BI + 24].partition_broadcast(128), "par3")
            snw = prw[:, 0:1024]
            k.act(aneg[:], prw[:, 1024:1040], AF.Exp)
            k.mul(aneg[:], aneg[:], -1.0)
            k.copy(Dfull[:], bc(prw[:, 1040:1056], [128, 16, 64], 2), e="dve")
            k.memset(halo[:], 0.0)
            k.memset(H[0][:], 0.0)
            k.memset(H[1][:], 0.0)
            for b in range(self.S // BLK):
                t0 = b * BLK
                k.dma(xTb[:], self.xT_view(xT, t0, BLK), "xT")
                k.copy(xTh[:], xTb[:], e="pool")
                for c in range(12):
                    bank = ps[c % 2]
                    self.fmaj(l, 8 + c, wf[c % 2], bank, xTh, BLK)
                    k.copy(u[:, 0:3], halo[:, c, :], e="pool")
                    k.copy(u[:, 3:3 + BLK], bank[:, 0:BLK])
                    k.copy(halo[:, c, :], u[:, BLK:BLK + 3], e="pool")
                    acc = xbcT[:, c, :]
                    k.ts(acc, u[:, 0:BLK], pcol[:, PC_CW + c * 4:PC_CW + c * 4 + 1], pcol[:, PC_CB + c:PC_CB + c + 1],
                         ALU.mult, ALU.add)
                    for kk in range(1, 4):
                        k.stt(acc, u[:, kk:kk + BLK], pcol[:, PC_CW + c * 4 + kk:PC_CW + c * 4 + kk + 1], acc,
                              ALU.mult, ALU.add)
                    k.act(acc, acc, AF.Silu)
                for j in range(BLK // 128):
                    g = (t0 // 128) + j
                    cs = slice(j * 128, (j + 1) * 128)
                    for ti in range(5, 7):
                        bank = ps[2 + ti % 2]
                        self.tmaj(l, ti, wt[ti % 2], bank, xTh, cs)
                        k.act(zsil[:, (ti - 5) * 512:(ti - 4) * 512], bank[:, 0:512], AF.Silu)
                    self.small_proj(wsm, xTb, cs, ps[4], small, biasb[:])
                    k.act(dt[:], small[:, 8:24], AF.Exp)
                    k.act(dt[:], dt[:], AF.Ln, bias=1.0)
                    k.tt(dtA[:], dt[:], aneg[:], ALU.mult)
                    X = ps[4]
                    k.mm(X[:, 0:16], self.tri, dtA[:])
                    k.mm(X[:, 16:32], self.amat, dtA[:])
                    k.mm(X[:, 32:48], self.ones, dtA[:])
                    k.act(ex[:], X[:, 0:48], AF.Exp)
                    eacum, erev, ecd = ex[:, 0:16], ex[:, 16:32], ex[:, 32:48]
                    k.tt(Bm[:], bc(dtA[:], [128, 16, 128], 2), bc(self.tri, [128, 16, 128], 1), ALU.mult)
                    for q in range(4):
                        bank = ps[5 + q % 2]
                        k.mm(bank[:, 0:512], self.amat, Bm[:, 4 * q:4 * q + 4, :].rearrange("p h t -> p (h t)"),
                             start=True, stop=False)
                        k.mm(bank[:, 0:512], self.ident, self.nm4, start=False, stop=True)
                        k.act(dec[:, 4 * q:4 * q + 4, :].rearrange("p h t -> p (h t)"), bank[:, 0:512], AF.Exp)
                    for c in range(8):
                        k.tr(ps[c // 4][:, (c % 4) * 128:(c % 4 + 1) * 128], xbcT[:, c, cs], self.ident)
                    xs2 = xstok[:].rearrange("p h d -> p (h d)")
                    k.copy(xs2[:, 0:512], ps[0][:, 0:512])
                    k.copy(xs2[:, 512:1024], ps[1][:, 0:512])
                    for gg in range(2):
                        k.tr(ps[7][:, gg * 128:(gg + 1) * 128], xbcT[:, 8 + gg, cs], self.ident)
                    k.copy(Btok[:].rearrange("p g n -> p (g n)"), ps[7][:, 0:256])
                    k.tt(xdt[:], xstok[:], bc(dt[:], [128, 16, 64], 2), ALU.mult)
                    k.tt(xw[:], xdt[:], bc(erev, [128, 16, 64], 2), ALU.mult, e="pool")
                    for gg in range(2):
                        k.mm(ps[7][:, 256 + gg * 128:256 + (gg + 1) * 128], xbcT[:, 8 + gg, cs], xbcT[:, 10 + gg, cs])
                    for gg in range(2):
                        k.tt(MT[:, gg * 8:(gg + 1) * 8, :], dec[:, gg * 8:(gg + 1) * 8, :],
                             bc(ps[7][:, 256 + gg * 128:256 + (gg + 1) * 128], [128, 8, 128], 1), ALU.mult)
                    for gg in range(2):
                        yd = ps[gg]
                        yo = ps[2 + gg]
                        for hh in range(8):
                            h = gg * 8 + hh
                            k.mm(yd[:, hh * 64:(hh + 1) * 64], MT[:, h, :], xdt[:, h, :])
                        k.mm(yo[:, 0:512], xbcT[:, 10 + gg, cs], H[gg][:].rearrange("p h d -> p (h d)"))
                        t1g = t1[:, gg * 8:(gg + 1) * 8, :]
                        k.tt(t1g, yo[:, 0:512].rearrange("p (h d) -> p h d", h=8),
                             bc(eacum[:, gg * 8:(gg + 1) * 8], [128, 8, 64], 2), ALU.mult)
                        k.tt(t1g, t1g, yd[:, 0:512].rearrange("p (h d) -> p h d", h=8), ALU.add)
                    k.tt(t3[:], xstok[:], Dfull[:], ALU.mult, e="pool")
                    k.tt(t1[:], t1[:], t3[:], ALU.add)
                    t12 = t1[:].rearrange("p h d -> p (h d)")
                    k.tt(t12, t12, zsil[:], ALU.mult)
                    for gg in range(2):
                        k.act(junk[:], t12[:, gg * 512:(gg + 1) * 512], AF.Square, accum_out=ss[:, gg:gg + 1])
                    k.act(ss[:], ss[:], AF.Sqrt, bias=1e-6, scale=1.0 / 512)
                    k.recip(ss[:], ss[:])
                    for gg in range(2):
                        k.stt(ys[:, gg * 512:(gg + 1) * 512], t12[:, gg * 512:(gg + 1) * 512], ss[:, gg:gg + 1],
                              snw[:, gg * 512:(gg + 1) * 512], ALU.mult, ALU.mult)
                    for gg in range(2):
                        sp_ = ps[5 + gg]
                        k.mm(sp_[:, 0:512], Btok[:, gg, :], xw[:, gg * 8:(gg + 1) * 8, :].rearrange("p h d -> p (h d)"))
                        k.tt(H[gg][:], H[gg][:], bc(ecd[:, gg * 8:(gg + 1) * 8], [128, 8, 64], 2), ALU.mult)
                        k.tt(H[gg][:], H[gg][:], sp_[:, 0:512].rearrange("p (h d) -> p h d", h=8), ALU.add)
                    self.emit_yT(ys, g, 8, yTc, [ps[0], ps[1]])

    def phase_swa(self, l, xT):
        k, nc, ps = self.k, self.nc, self.ps
        BLK = self.BLK
        with ExitStack() as es:
            def sb(name, shape):
                return es.enter_context(nc.sbuf_tensor(f"L{l}c_{name}", list(shape), F32))
            xTb = sb("xTb", [128, 16, BLK])
            wt = [es.enter_context(nc.sbuf_tensor(f"L{l}c_wt{i}", [128, 16, 512], BF16)) for i in range(2)]
            xTh = es.enter_context(nc.sbuf_tensor(f"L{l}c_xTh", [128, 16, BLK], BF16))
            sinks = sb("sinks", [128, 16])
            aq = sb("aq", [128, 16, 64])
            ak = sb("ak", [128, 4, 64])
            av = [sb("av0", [128, 256]), sb("av1", [128, 256])]
            cs_t = sb("cs", [128, 64])
            ta = sb("ta", [128, 16, 32])
            tb = sb("tb", [128, 16, 32])
            aqr = sb("aqr", [128, 16, 64])
            akd = sb("akd", [128, 4, 2, 64])
            qT = sb("qT", [128, 8, 128])
            kT = [sb("kT0", [128, 4, 128]), sb("kT1", [128, 4, 128])]
            Sm = sb("Sm", [128, 256])
            mx = sb("mx", [128, 1])
            negm = sb("negm", [128, 1])
            p = sb("p", [128, 256])
            rsum = sb("rsum", [128, 1])
            esk = sb("esk", [128, 1])
            den = sb("den", [128, 1])
            pT = sb("pT", [128, 256])
            ya = sb("ya", [128, 1024])
            yTc = sb("yTc", [128, 8, 128])

            k.dma(sinks[:], self.prow_d[l, :, PR_SK:PR_SK + 16].partition_broadcast(128), "par0")
            k.memset(kT[1][:], 0.0)
            k.memset(av[1][:], 0.0)
            for b in range(self.S // BLK):
                t0 = b * BLK
                k.dma(xTb[:], self.xT_view(xT, t0, BLK), "xT")
                k.copy(xTh[:], xTb[:], e="pool")
                for j in range(BLK // 128):
                    g = (t0 // 128) + j
                    cur, prv = g % 2, (g + 1) % 2
                    cs = slice(j * 128, (j + 1) * 128)
                    k.dma(cs_t[:], self.rope_d[g * 128:(g + 1) * 128, :], "rope")
                    aq2 = aq[:].rearrange("p h d -> p (h d)")
                    for ti in range(7, 10):
                        bank = ps[ti % 2]
                        self.tmaj(l, ti, wt[ti % 2], bank, xTh, cs)
                        if ti < 9:
                            k.copy(aq2[:, (ti - 7) * 512:(ti - 6) * 512], bank[:, 0:512])
                        else:
                            k.copy(ak[:].rearrange("p h d -> p (h d)"), bank[:, 0:256])
                            k.copy(av[cur][:], bank[:, 256:512])
                    for (src, dst, nh) in ((aq, aqr[:], 16), (ak, akd[:, :, 0, :], 4)):
                        x1, x2 = src[:, :, 0:32], src[:, :, 32:64]
                        cb = bc(cs_t[:, 0:32], [128, nh, 32], 1)
                        sn = bc(cs_t[:, 32:64], [128, nh, 32], 1)
                        k.tt(ta[:, 0:nh, :], x1, cb, ALU.mult)
                        k.tt(tb[:, 0:nh, :], x2, sn, ALU.mult, e="pool")
                        k.tt(dst[:, :, 0:32], ta[:, 0:nh, :], tb[:, 0:nh, :], ALU.subtract)
                        k.tt(ta[:, 0:nh, :], x2, cb, ALU.mult)
                        k.tt(tb[:, 0:nh, :], x1, sn, ALU.mult, e="pool")
                        k.tt(dst[:, :, 32:64], ta[:, 0:nh, :], tb[:, 0:nh, :], ALU.add)
                    k.copy(akd[:, :, 1, :], akd[:, :, 0, :], e="pool")
                    aqr2 = aqr[:].rearrange("p h d -> p (h d)")
                    for m in range(8):
                        k.tr(ps[2 + m // 4][:, (m % 4) * 128:(m % 4 + 1) * 128], aqr2[:, m * 128:(m + 1) * 128], self.ident)
                    for hh in range(2):
                        k.copy(qT[:, hh * 4:(hh + 1) * 4, :], ps[2 + hh][:, 0:512].rearrange("p (a b) -> p a b", a=4))
                    for gg in range(4):
                        k.tr(ps[4][:, gg * 128:(gg + 1) * 128], akd[:, gg, :, :].rearrange("p a d -> p (a d)"), self.ident)
                    k.copy(kT[cur][:], ps[4][:, 0:512].rearrange("p (a b) -> p a b", a=4))
                    mask = self.mask0 if g == 0 else self.maskA
                    for h in range(16):
                        gg, base, m = h // 4, (h % 2) * 64, h // 2
                        sbk = ps[5 + h % 2]
                        k.mm(sbk[:, 0:128], qT[base:base + 64, m, :], kT[prv][base:base + 64, gg, :])
                        k.mm(sbk[:, 128:256], qT[base:base + 64, m, :], kT[cur][base:base + 64, gg, :])
                        k.stt(Sm[:], sbk[:, 0:256], 0.125, mask, ALU.mult, ALU.add)
                        k.op("dve", "reduce_max", mx[:], Sm[:], axis=AX.X, R=[Sm], W=[mx])
                        k.ts(negm[:], mx[:], sinks[:, h:h + 1], -1.0, ALU.max, ALU.mult)
                        k.act(p[:], Sm[:], AF.Exp, bias=negm[:], accum_out=rsum[:])
                        k.act(esk[:], sinks[:, h:h + 1], AF.Exp, bias=negm[:])
                        k.tt(den[:], rsum[:], esk[:], ALU.add)
                        k.recip(den[:], den[:])
                        pb = ps[h % 2]
                        k.tr(pb[:, 0:128], p[:, 0:128], self.ident)
                        k.tr(pb[:, 128:256], p[:, 128:256], self.ident)
                        k.copy(pT[:], pb[:, 0:256])
                        ob = ps[7]
                        k.mm(ob[:, 0:64], pT[:, 0:128], av[prv][:, gg * 64:(gg + 1) * 64], start=True, stop=False)
                        k.mm(ob[:, 0:64], pT[:, 128:256], av[cur][:, gg * 64:(gg + 1) * 64], start=False, stop=True)
                        k.ts(ya[:, h * 64:(h + 1) * 64], ob[:, 0:64], den[:], None, ALU.mult)
                    self.emit_yT(ya, g, 16, yTc, [ps[2], ps[3]])

    def phase_mix(self, l, xT):
        k, nc, ps = self.k, self.nc, self.ps
        BLK = self.BLK
        with ExitStack() as es:
            def sb(name, shape, dt=F32):
                return es.enter_context(nc.sbuf_tensor(f"L{l}d_{name}", list(shape), dt))
            xTb = sb("xTb", [128, 16, BLK])
            xTh = sb("xTh", [128, 16, BLK], BF16)
            ytmp = [sb("ytmp0", [128, 24, 128]), sb("ytmp1", [128, 24, 128])]
            yTh = sb("yTh", [128, 24, BLK], BF16)
            mixT = sb("mixT", [128, 16, BLK], BF16)
            wg = [sb("wg0", [128, 16, 128], BF16), sb("wg1", [128, 16, 128], BF16)]
            wb = [sb("wb0", [128, 8, 128], BF16), sb("wb1", [128, 8, 128], BF16)]
            wo = [sb("wo0", [128, 16, 128], BF16), sb("wo1", [128, 16, 128], BF16)]
            pcol = sb("pcol", [128, NPCOL])
            gsb = [sb("gsb0", [128, BLK]), sb("gsb1", [128, BLK])]
            tmp = [sb("tmp0", [128, BLK]), sb("tmp1", [128, BLK])]
            acc = [sb("acc0", [128, BLK]), sb("acc1", [128, BLK])]
            sq = [sb("sq0", [128, BLK]), sb("sq1", [128, BLK])]
            rstd = sb("rstd", [128, BLK])
            k.dma(pcol[:], self.pcol_d[l], "par0")
            n = 0
            for b in range(self.S // BLK):
                t0 = b * BLK
                k.dma(xTb[:], self.xT_view(xT, t0, BLK), "xT")
                k.copy(xTh[:], xTb[:], e="pool")
                for j in range(BLK // 128):
                    yt = ytmp[j % 2]
                    k.dma(yt[:], self.yT[t0 // 128 + j], "yT" + yt.name[-1])
                    k.copy(yTh[:, :, j * 128:(j + 1) * 128], yt[:], e=("act" if j % 2 else "dve"))
                for c in range(16):
                    ac = acc[c % 2]
                    for kk in range(3):
                        n += 1
                        w = wg[n % 2]
                        k.dma(w[:], self.wfb[20 + kk * 16 + c].rearrange("p (k e) -> p k e", k=16), "wg" + w.name[-1])
                        gb = ps[n % 2]
                        for kc in range(16):
                            k.mm(gb[:, 0:BLK], w[:, kc, :], xTh[:, kc, :], start=(kc == 0), stop=(kc == 15))
                        gs = gsb[n % 2]
                        k.act(gs[:], gb[:, 0:BLK], AF.Sigmoid, bias=pcol[:, PC_GB + kk * 16 + c:PC_GB + kk * 16 + c + 1])
                        w2 = wb[n % 2]
                        k.dma(w2[:], self.wbb[kk * 16 + c].rearrange("p (k e) -> p k e", k=8), "wb" + w2.name[-1])
                        bb = ps[2 + n % 2]
                        for kc in range(8):
                            k.mm(bb[:, 0:BLK], w2[:, kc, :], yTh[:, kk * 8 + kc, :], start=(kc == 0), stop=(kc == 7))
                        if kk == 0:
                            k.tt(ac[:], gs[:], bb[:, 0:BLK], ALU.mult)
                        elif kk == 1:
                            tp = tmp[n % 2]
                            k.tt(tp[:], gs[:], bb[:, 0:BLK], ALU.mult)
                            k.tt(ac[:], ac[:], tp[:], ALU.add, e="pool")
                        else:
                            tp = tmp[n % 2]
                            k.tt(tp[:], gs[:], bb[:, 0:BLK], ALU.mult)
                            k.tt(mixT[:, c, :], ac[:], tp[:], ALU.add, e="pool")
                for c in range(16):
                    w = wo[c % 2]
                    k.dma(w[:], self.wob[c].rearrange("p (k e) -> p k e", k=16), "wo" + w.name[-1])
                    ob = ps[4 + c % 2]
                    for kc in range(16):
                        k.mm(ob[:, 0:BLK], w[:, kc, :], mixT[:, kc, :], start=(kc == 0), stop=(kc == 15))
                    k.stt(xTb[:, c, :], xTb[:, c, :], ALPHA, ob[:, 0:BLK], ALU.mult, ALU.add)
                mb, vb = ps[6], ps[7]
                for c in range(16):
                    k.mm(mb[:, 0:BLK], self.onesD, xTb[:, c, :], start=(c == 0), stop=(c == 15))
                for c in range(16):
                    k.tt(xTb[:, c, :], xTb[:, c, :], mb[:, 0:BLK], ALU.subtract)
                    s_ = sq[c % 2]
                    k.act(s_[:], xTb[:, c, :], AF.Square)
                    k.mm(vb[:, 0:BLK], self.onesD, s_[:], start=(c == 0), stop=(c == 15))
                k.act(rstd[:], vb[:, 0:BLK], AF.Sqrt, bias=1e-5)
                k.recip(rstd[:], rstd[:])
                for c in range(16):
                    k.tt(xTb[:, c, :], xTb[:, c, :], rstd[:], ALU.mult)
                    k.ts(xTb[:, c, :], xTb[:, c, :], pcol[:, PC_L1G + c:PC_L1G + c + 1], pcol[:, PC_L1B + c:PC_L1B + c + 1],
                         ALU.mult, ALU.add, e="pool")
                k.dma(self.xT_view(self.x1T, t0, BLK), xTb[:], "x1st", q="pool")

    def phase_peer_conv(self, l):
        k, nc = self.k, self.nc
        with ExitStack() as es:
            def sb(name, shape, dt=F32):
                return es.enter_context(nc.sbuf_tensor(f"L{l}v_{name}", list(shape), dt))
            u32 = [sb("u32a", [128, 2048]), sb("u32b", [128, 2048])]
            v32 = [sb("v32a", [128, 2048]), sb("v32b", [128, 2048])]
            u16 = [sb("u16a", [128, 2048], BF16), sb("u16b", [128, 2048], BF16)]
            v16 = [sb("v16a", [128, 2048], BF16), sb("v16b", [128, 2048], BF16)]
            for ec in range(128):
                i = ec % 2
                k.dma(u32[i][:], self.uT_d[l, ec].rearrange("p k e -> p (k e)"), f"cu{i}")
                k.copy(u16[i][:], u32[i][:], e="act")
                k.dma(self.uTb[ec], u16[i][:], f"su{i}", q="pool")
                k.dma(v32[i][:], self.v_d[l, ec], f"cv{i}")
                k.copy(v16[i][:], v32[i][:], e="dve")
                k.dma(self.vbf[ec], v16[i][:], f"sv{i}", q="pool")

    def phase_peer(self, l, xTout):
        k, nc, ps = self.k, self.nc, self.ps
        TG = min(256, self.S)
        NT = TG // 128
        with ExitStack() as es:
            def sb(name, shape, dt=F32):
                return es.enter_context(nc.sbuf_tensor(f"L{l}e_{name}", list(shape), dt))
            skT = sb("skT", [128, 16, 128])
            ln2 = sb("ln2", [128, 4096])
            sall = [sb(f"sall{t}", [128, 16, 128]) for t in range(NT)]
            x1tok = [sb(f"x1tok{t}", [128, 2048]) for t in range(NT)]
            tau = [sb(f"tau{t}", [128, 8]) for t in range(NT)]
            biasE = [sb(f"biasE{t}", [128, 8]) for t in range(NT)]
            xtb = sb("xtb", [128, 16, TG], BF16)
            HT = [sb(f"HT{e}", [128, TG], BF16) for e in range(128)]
            st = sb("st", [128, 24])
            mv2 = sb("mv2", [128, 2])
            rstd = sb("rstd", [128, 1])
            k.dma(skT[:], self.skT_d[l], "par0")
            k.dma(ln2[:], self.prow_d[l, :, PR_L2G:PR_L2G + 4096].partition_broadcast(128), "par1")
            for grp in range(self.S // TG):
                t0 = grp * TG
                with ExitStack() as e1:
                    def s1(name, shape, dt=F32):
                        return e1.enter_context(nc.sbuf_tensor(f"L{l}e{grp}p_{name}", list(shape), dt))
                    xt = s1("xt", [128, 16, TG])
                    qT = s1("qT", [128, 16, TG])
                    wq = [s1("wq0", [128, 16, 128]), s1("wq1", [128, 16, 128])]
                    top = s1("top", [128, 2, 16])
                    tmp = s1("tmp", [128, 128])
                    cand = s1("cand", [128, 256])
                    tmp2 = s1("tmp2", [128, 256])
                    best = s1("best", [128, 16])
                    negmx = s1("negmx", [128, 1])
                    j16 = s1("j16", [128, 16])
                    Z = s1("Z", [128, 1])
                    k.dma(xt[:], self.xT_view(self.x1T, t0, TG), "xT")
                    k.copy(xtb[:], xt[:], e="pool")
                    for c in range(16):
                        w = wq[c % 2]
                        k.dma(w[:], self.wq_d[l, c], "wq" + w.name[-1])
                        bank = ps[4 + c % 2]
                        for kc in range(16):
                            k.mm(bank[:, 0:TG], w[:, kc, :], xt[:, kc, :], start=(kc == 0), stop=(kc == 15))
                        k.copy(qT[:, c, :], bank[:, 0:TG])
                    for t in range(NT):
                        ts_ = slice(t * 128, (t + 1) * 128)
                        for c in range(16):
                            k.mm(ps[c // 4][:, (c % 4) * 128:(c % 4 + 1) * 128], qT[:, c, ts_], skT[:, c, :])
                        for q in range(4):
                            k.copy(sall[t][:, q * 4:(q + 1) * 4, :], ps[q][:, 0:512].rearrange("p (a b) -> p a b", a=4))
                        for c in range(16):
                            k.tr(ps[4 + (c // 4) % 2][:, (c % 4) * 128:(c % 4 + 1) * 128], xt[:, c, ts_], self.ident)
                            if c % 4 == 3:
                                q = c // 4
                                k.copy(x1tok[t][:, q * 512:(q + 1) * 512], ps[4 + q % 2][:, 0:512], e="dve")
                        for h in range(8):
                            for half in range(2):
                                sv = sall[t][:, 2 * h + half, :]
                                k.op("dve", "max", top[:, half, 0:8], sv, R=[sall[t]], W=[top])
                                k.op("dve", "match_replace", tmp[:], top[:, half, 0:8], sv, -1e30, R=[top, sall[t]], W=[tmp])
                                k.op("dve", "max", top[:, half, 8:16], tmp[:], R=[tmp], W=[top])
                            k.tt(cand[:].rearrange("p (a b) -> p a b", a=16), bc(top[:, 0, :], [128, 16, 16], 2),
                                 bc(top[:, 1, :], [128, 16, 16], 1), ALU.add)
                            k.op("dve", "max", best[:, 0:8], cand[:], R=[cand], W=[best])
                            k.op("dve", "match_replace", tmp2[:], best[:, 0:8], cand[:], -1e30, R=[best, cand], W=[tmp2])
                            k.op("dve", "max", best[:, 8:16], tmp2[:], R=[tmp2], W=[best])
                            k.ts(negmx[:], best[:, 0:1], -1.0, None, ALU.mult)
                            k.act(j16[:], best[:], AF.Exp, bias=negmx[:], accum_out=Z[:])
                            k.act(Z[:], Z[:], AF.Ln)
                            k.tt(biasE[t][:, h:h + 1], negmx[:], Z[:], ALU.subtract)
                            k.copy(tau[t][:, h:h + 1], best[:, 15:16], e="dve")
                k.barrier()
                with ExitStack() as e2:
                    def s2(name, shape, dt=F32):
                        return e2.enter_context(nc.sbuf_tensor(f"L{l}e{grp}s_{name}", list(shape), dt))
                    Lb = [s2(f"Lb{i}", [128, 8, 128]) for i in range(3)]
                    Eb = [s2(f"Eb{i}", [128, 8, 128]) for i in range(2)]
                    Mb = [s2("Mb0", [128, 8, 128]), s2("Mb1", [128, 8, 128])]
                    Gacc = [[s2(f"G{t}a", [128, 8, 128]), s2(f"G{t}b", [128, 8, 128])] for t in range(NT)]
                    ut = [s2(f"ut{i}", [128, 16, 128], BF16) for i in range(3)]
                    vh = [s2(f"vh{i}", [128, 1536], BF16) for i in range(4)]
                    ga = [s2("ga0", [128, TG]), s2("ga1", [128, TG])]
                    ne = 0
                    pend = None

                    def obank(t, q):
                        if q < 2:
                            return ps[t * 2 + q]
                        return ps[6 + t] if q == 2 else ps[4 + t]

                    def emit_out(ec, v_):
                        for t in range(NT):
                            for q in range(3):
                                k.mm(obank(t, q)[:, 0:512], HT[ec][:, t * 128:(t + 1) * 128], v_[:, q * 512:(q + 1) * 512],
                                     start=(ec == 0), stop=(ec == 127))

                    def gunit(ib, u):
                        t, h = u // 8, u % 8
                        G = Gacc[t][ib % 2]
                        nbl[0] += 1
                        nb = nbl[0]
                        Lh, Eh, Mh = Lb[nb % 3], Eb[nb % 2], Mb[nb % 2]
                        k.tt(Lh[:], bc(sall[t][:, 2 * h, ib * 8:(ib + 1) * 8], [128, 8, 128], 2),
                             bc(sall[t][:, 2 * h + 1, :], [128, 8, 128], 1), ALU.add, e="pool")
                        k.act(Eh[:], Lh[:], AF.Exp, bias=biasE[t][:, h:h + 1])
                        if h == 0:
                            k.stt(G[:], Lh[:], tau[t][:, h:h + 1], Eh[:], ALU.is_ge, ALU.mult)
                        else:
                            k.stt(Mh[:], Lh[:], tau[t][:, h:h + 1], Eh[:], ALU.is_ge, ALU.mult)
                            k.tt(G[:], G[:], Mh[:], ALU.add, e="dve")

                    nbl = [0]
                    NU = NT * 8
                    for u in range(NU):
                        gunit(0, u)
                    pendH = None
                    pendO = None
                    for ib in range(16):
                        for i in range(8):
                            ec = ib * 8 + i
                            ne += 1
                            u_, v_ = ut[ne % 3], vh[ne % 4]
                            k.dma(u_[:], self.uTb[ec].rearrange("p (k e) -> p k e", k=16), "ut" + u_.name[-1])
                            k.dma(v_[:], self.vbf[ec][:, 0:1536], "vh" + v_.name[-1])
                            ab = ps[4 + ec % 2]
                            for t in range(NT):
                                k.tr(ab[:, 256 + t * 128:256 + (t + 1) * 128], Gacc[t][ib % 2][:, i, :], self.ident)
                            for kc in range(16):
                                k.mm(ab[:, 0:TG], u_[:, kc, :], xtb[:, kc, :], start=(kc == 0), stop=(kc == 15))
                            if pendH is not None:
                                pe_, pg_, pab_, pv_ = pendH
                                k.tt(HT[pe_][:], pg_[:], pab_[:, 256:256 + TG], ALU.mult)
                            if ib + 1 < 16 and i % 2 == 0:
                                for u in range(i * NU // 8, (i + 2) * NU // 8):
                                    gunit(ib + 1, u)
                            g_ = ga[ec % 2]
                            k.act(g_[:], ab[:, 0:TG], AF.Gelu)
                            if pendO is not None:
                                emit_out(*pendO)
                            pendO = (pendH[0], pendH[3]) if pendH is not None else None
                            pendH = (ec, g_, ab, v_)
                    pe_, pg_, pab_, pv_ = pendH
                    k.tt(HT[pe_][:], pg_[:], pab_[:, 256:256 + TG], ALU.mult)
                    if pendO is not None:
                        emit_out(*pendO)
                    emit_out(pe_, pv_)
                    for ec in range(128):
                        ne += 1
                        v_ = vh[ne % 4]
                        k.dma(v_[:, 0:512], self.vbf[ec][:, 1536:2048], "vh" + v_.name[-1])
                        for t in range(NT):
                            k.mm(obank(t, 3)[:, 0:512], HT[ec][:, t * 128:(t + 1) * 128], v_[:, 0:512],
                                 start=(ec == 0), stop=(ec == 127))
                k.barrier()
                with ExitStack() as e3:
                    xTn = None
                    if xTout is not None:
                        xTn = e3.enter_context(nc.sbuf_tensor(f"L{l}e{grp}x_xTn", [128, 16, TG], F32))
                    for t in range(NT):
                        xk = x1tok[t]
                        for q in range(4):
                            bank = obank(t, q)
                            k.stt(xk[:, q * 512:(q + 1) * 512], xk[:, q * 512:(q + 1) * 512], ALPHA, bank[:, 0:512],
                                  ALU.mult, ALU.add)
                            k.op("dve", "bn_stats", st[:, q * 6:(q + 1) * 6], xk[:, q * 512:(q + 1) * 512], R=[xk], W=[st])
                        k.op("dve", "bn_aggr", mv2[:], st[:], R=[st], W=[mv2])
                        k.act(rstd[:], mv2[:, 1:2], AF.Sqrt, bias=1e-5)
                        k.recip(rstd[:], rstd[:])
                        k.ts(xk[:], xk[:], mv2[:, 0:1], rstd[:], ALU.subtract, ALU.mult)
                        k.tt(xk[:], xk[:], ln2[:, 0:2048], ALU.mult)
                        k.tt(xk[:], xk[:], ln2[:, 2048:4096], ALU.add, e="pool")
                        if xTout is None:
                            k.dma(self.out_d[t0 + t * 128:t0 + (t + 1) * 128, :], xk[:], "ost", q="sp")
                        else:
                            for c in range(16):
                                k.tr(ps[(c // 4) % 2][:, (c % 4) * 128:(c % 4 + 1) * 128], xk[:, c * 128:(c + 1) * 128],
                                     self.ident)
                                if c % 4 == 3:
                                    q = c // 4
                                    k.copy(xTn[:, q * 4:(q + 1) * 4, t * 128:(t + 1) * 128],
                                           ps[q % 2][:, 0:512].rearrange("p (a b) -> p a b", a=4))
                    if xTout is not None:
                        k.dma(self.xT_view(xTout, t0, TG), xTn[:], "ost", q="sp")
                k.barrier()


def make_consts():
    c = np.zeros((128, NCONST), np.float32)
    r = np.arange(128)
    c[:, C_ID:C_ID + 128] = np.eye(128)
    c[:, C_TRI:C_TRI + 128] = (r[:, None] <= r[None, :])
    c[:, C_AM:C_AM + 128] = (r[:, None] > r[None, :])
    c[:, C_ONE:C_ONE + 128] = 1.0
    c[:, C_OND:C_OND + 128] = 1.0 / D
    nm = np.where(r[:, None] > r[None, :], NEG, 0.0)
    c[:, C_NM4:C_NM4 + 512] = np.tile(nm, (1, 4))
    prevm = np.where(r[None, :] > r[:, None], 0.0, NEG)
    curm = np.where(r[None, :] <= r[:, None], 0.0, NEG)
    c[:, C_MA:C_MA + 256] = np.concatenate([prevm, curm], 1)
    c[:, C_M0:C_M0 + 256] = np.concatenate([np.full((128, 128), NEG), curm], 1)
    return c


def make_rope(S):
    half = 32
    freqs = (np.float32(10000.0) ** (-np.arange(half, dtype=np.float32) / np.float32(half))).astype(np.float32)
    ang = np.arange(S, dtype=np.float32)[:, None] * freqs[None, :]
    return np.concatenate([np.cos(ang), np.sin(ang)], 1).astype(np.float32)


def tile_w(w, cols, tw):
    ws = w[:, cols]
    n = ws.shape[1] // tw
    return np.ascontiguousarray(ws.reshape(16, 128, n, tw).transpose(2, 1, 0, 3))


def prep_weights(inp, NL):
    f = lambda a: np.asarray(a, dtype=np.float32)
    w_in = inp["w_in"]
    out = {}
    out["wf"] = np.stack([tile_w(f(w_in[l]), FCOLS, 128) for l in range(NL)])
    out["wt"] = np.stack([tile_w(f(w_in[l]), TCOLS, 512) for l in range(NL)])
    out["wsm"] = np.stack([tile_w(f(w_in[l]), SCOLS, 24)[0] for l in range(NL)])
    pcol = np.zeros((NL, 128, NPCOL), np.float32)
    prow = np.zeros((NL, 1, NPROW), np.float32)
    for l in range(NL):
        cw = f(inp["ssm_conv_w"][l])[:, 0, :]
        pcol[l, :, PC_CW:PC_CW + 48] = cw.reshape(4, 12, 128).transpose(2, 1, 0).reshape(128, 48)
        pcol[l, :, PC_CB:PC_CB + 12] = f(inp["ssm_conv_b"][l]).reshape(12, 128).T
        pcol[l, :, PC_GB:PC_GB + 48] = f(inp["merge_gate_b"][l]).reshape(48, 128).T
        pcol[l, :, PC_L1G:PC_L1G + 16] = f(inp["ln1_g"][l]).reshape(16, 128).T
        pcol[l, :, PC_L1B:PC_L1B + 16] = f(inp["ln1_b"][l]).reshape(16, 128).T
        prow[l, 0, PR_MN:PR_MN + 1024] = f(inp["mlstm_norm_w"][l])
        prow[l, 0, PR_SN:PR_SN + 1024] = f(inp["ssm_norm_w"][l])
        prow[l, 0, PR_AL:PR_AL + 16] = f(inp["ssm_a_log"][l])
        prow[l, 0, PR_DS:PR_DS + 16] = f(inp["ssm_d"][l])
        prow[l, 0, PR_SK:PR_SK + 16] = f(inp["swa_sinks"][l])
        prow[l, 0, PR_L2G:PR_L2G + 2048] = f(inp["ln2_g"][l])
        prow[l, 0, PR_L2B:PR_L2B + 2048] = f(inp["ln2_b"][l])
        prow[l, 0, PR_BI:PR_BI + 24] = np.concatenate([f(inp["mlstm_gate_b"][l, 0]), f(inp["mlstm_gate_b"][l, 1]),
                                                       f(inp["ssm_dt_bias"][l])])
    out["pcol"] = pcol
    out["prow"] = prow
    wb = f(inp["w_branch"][:NL])
    out["wb"] = np.ascontiguousarray(wb.reshape(NL, 3, 8, 128, 16, 128).transpose(0, 1, 4, 3, 2, 5))
    all_cols = np.arange(2048)
    out["wo"] = np.stack([tile_w(f(inp["w_out"][l]), all_cols, 128) for l in range(NL)])
    out["wq"] = np.stack([tile_w(f(inp["peer_wq"][l]), all_cols, 128) for l in range(NL)])
    sk = f(inp["peer_subkeys"][:NL])
    out["skT"] = np.ascontiguousarray(sk.reshape(NL, 16, 128, 128).transpose(0, 3, 1, 2))
    u = f(inp["peer_u"][:NL])
    out["uT"] = np.ascontiguousarray(u.reshape(NL, 128, 128, 16, 128).transpose(0, 1, 4, 3, 2))
    out["pv"] = np.ascontiguousarray(f(inp["peer_v"][:NL]).reshape(NL, 128, 128, 2048))
    return out


_CACHE = {}
PHASES = set("abcdve")


def run(inp, S, NL, B, dbg=False, trace=False):
    key = (S, NL, dbg)
    prog = Prog(S, NL, dbg)
    nc = prog.build()
    wts = prep_weights(inp, NL)
    wts["consts"] = make_consts()
    wts["rope"] = make_rope(S)
    x = np.asarray(inp["x"], dtype=np.float32)
    in_maps = []
    for b in range(B):
        m = dict(wts)
        m["xT0"] = np.ascontiguousarray(x[b].T)
        in_maps.append(m)
    res = run_bass_kernel_spmd(nc, in_maps, core_ids=list(range(B)), trace=trace)
    return res, prog


def kernel(**inputs):
    S = inputs["x"].shape[1]
    B = inputs["x"].shape[0]
    res, _ = run(inputs, S, DEPTH, B)
    return np.stack([np.asarray(r["out"]) for r in res.results], 0).astype(np.float32)
```
